# Optimizing a Trainium2 kernel written in Bass

```python
import jax
import jax.numpy as jnp
from jax import lax
import numpy as np

D_MODEL = 1024
BATCH = 2
SEQ = 8192
DEPTH = 2
DEC_BATCH = 128
DEC_SEQ = 8
PAST_LEN = 16384
PAGE_SIZE = 128

HEAD_DIM = 64
MIX_WIDTH = D_MODEL
GROUP_WIDTH = MIX_WIDTH // 4
RET_HEADS = GROUP_WIDTH // HEAD_DIM
RET_CHUNK = 128
ATTN_HEADS = GROUP_WIDTH // HEAD_DIM
ATTN_KV_HEADS = ATTN_HEADS // 2
ATTN_GROUP = ATTN_HEADS // ATTN_KV_HEADS
WINDOW = 128
ATTN_BLOCK = WINDOW
CONF_CH = GROUP_WIDTH
CONF_WIDTH = 31
SC_CH = GROUP_WIDTH
SC_WIDTH = 3
D_FF = 4 * D_MODEL
IN_WIDTH = 4 * GROUP_WIDTH + (ATTN_HEADS + 2 * ATTN_KV_HEADS) * HEAD_DIM + 2 * CONF_CH + 3 * SC_CH
EPS = 1e-6

kernel_name = 'hybrid_parallel_headgroup_decoder_step'


def _split_offsets():
    sizes = [GROUP_WIDTH] * 4 + [ATTN_HEADS * HEAD_DIM, ATTN_KV_HEADS * HEAD_DIM, ATTN_KV_HEADS * HEAD_DIM] + [CONF_CH] * 2 + [SC_CH] * 3
    offs, acc = [], 0
    for s in sizes[:-1]:
        acc += s
        offs.append(acc)
    return offs


def rms_norm(x, g):
    xf = x.astype(jnp.float32)
    y = xf * lax.rsqrt(jnp.mean(xf * xf, -1, keepdims=True) + EPS)
    return (y * g.astype(jnp.float32)).astype(x.dtype)


def layer_norm(x, g, b):
    xf = x.astype(jnp.float32)
    xc = xf - jnp.mean(xf, -1, keepdims=True)
    y = xc * lax.rsqrt(jnp.mean(xc * xc, -1, keepdims=True) + EPS)
    return (y * g.astype(jnp.float32) + b.astype(jnp.float32)).astype(x.dtype)


def alibi_slopes(n):
    return jnp.exp2(-8.0 * (jnp.arange(n, dtype=jnp.float32) + 1.0) / n)


def retention_log_decay(n):
    return jnp.log1p(-jnp.exp2(-5.0 - jnp.arange(n, dtype=jnp.float32)))


def retention(q, k, v, s0, chunk):
    b, l, h, dk = q.shape
    n_chunks = l // chunk
    log_g = retention_log_decay(h)
    i = jnp.arange(chunk, dtype=jnp.float32)
    diff = i[:, None] - i[None, :]
    intra = jnp.where(diff[None] >= 0, jnp.exp(jnp.maximum(diff, 0.0)[None] * log_g[:, None, None]), 0.0)
    q_dec = jnp.exp((i[:, None] + 1.0) * log_g[None, :])
    k_dec = jnp.exp((chunk - 1.0 - i)[:, None] * log_g[None, :])
    c_dec = jnp.exp(chunk * log_g)

    def to_chunks(t):
        return t.astype(jnp.float32).reshape(b, n_chunks, chunk, h, t.shape[-1]).transpose(1, 0, 2, 3, 4)

    qc, kc, vc = to_chunks(q), to_chunks(k * dk ** -0.5), to_chunks(v)

    def step(s, inp):
        qi, ki, vi = inp
        att = jnp.einsum('bihd,bjhd->bhij', qi, ki) * intra
        o = jnp.einsum('bhij,bjhe->bihe', att, vi) + jnp.einsum('bihd,bhde->bihe', qi * q_dec[None, :, :, None], s)
        s = s * c_dec[None, :, None, None] + jnp.einsum('bjhd,bjhe->bhde', ki * k_dec[None, :, :, None], vi)
        return s, o

    s, o = lax.scan(step, s0.astype(jnp.float32), (qc, kc, vc))
    return o.transpose(1, 0, 2, 3, 4).reshape(b, l, h, -1), s


def causal_dwconv(u, buf, w):
    width, ch = w.shape
    xp = jnp.concatenate([buf.astype(u.dtype), u], axis=1)
    y = lax.conv_general_dilated(xp, w.astype(u.dtype)[:, None, :], window_strides=(1,), padding='VALID',
                                 dimension_numbers=('NWC', 'WIO', 'NWC'), feature_group_count=ch)
    return y, xp[:, xp.shape[1] - (width - 1):]


def sink_attend(q, k, v, dist, valid, slopes, sinks):
    s = jnp.einsum('bnqhgd,bnkhd->bnhgqk', q, k).astype(jnp.float32) * HEAD_DIM ** -0.5
    s = s - slopes[:, :, None, None] * dist
    s = jnp.where(valid, s, -jnp.inf)
    sk = sinks.astype(jnp.float32)[:, :, None, None]
    m = jnp.maximum(jnp.max(s, -1, keepdims=True), sk)
    p = jnp.exp(s - m)
    p = p / (jnp.sum(p, -1, keepdims=True) + jnp.exp(sk - m))
    return jnp.einsum('bnhgqk,bnkhd->bnqhgd', p.astype(v.dtype), v)


def swa_prompt(q, k, v, slopes, sinks):
    b, n = q.shape[:2]
    nb = n // ATTN_BLOCK
    qb = q.reshape(b, nb, ATTN_BLOCK, ATTN_KV_HEADS, ATTN_GROUP, HEAD_DIM)

    def with_prev(t):
        tb = t.reshape(b, nb, ATTN_BLOCK, ATTN_KV_HEADS, HEAD_DIM)
        prev = jnp.concatenate([jnp.zeros_like(tb[:, :1]), tb[:, :-1]], axis=1)
        return jnp.concatenate([prev, tb], axis=2)

    iq = jnp.arange(ATTN_BLOCK)[:, None] + ATTN_BLOCK
    jk = jnp.arange(2 * ATTN_BLOCK)[None, :]
    dist = iq - jk
    band = (dist >= 0) & (dist <= WINDOW)
    has_prev = (jnp.arange(nb) > 0)[:, None, None] | (jk >= ATTN_BLOCK)[None]
    valid = (band[None] & has_prev)[:, None, None]
    o = sink_attend(qb, with_prev(k), with_prev(v), dist.astype(jnp.float32)[None, None, None], valid, slopes, sinks)
    return o.reshape(b, n, ATTN_HEADS * HEAD_DIM)


def swa_sample(q, k, v, k_buf, v_buf, slopes, sinks):
    b, n = q.shape[:2]
    wb = k_buf.shape[1]
    kk = jnp.concatenate([k_buf.astype(k.dtype), k], axis=1)
    vv = jnp.concatenate([v_buf.astype(v.dtype), v], axis=1)
    dist = (jnp.arange(n)[:, None] + wb) - jnp.arange(wb + n)[None, :]
    valid = (dist >= 0) & (dist <= WINDOW)
    o = sink_attend(q[:, None], kk[:, None], vv[:, None], dist.astype(jnp.float32)[None, None, None],
                    valid[None, None, None], slopes, sinks)
    return o.reshape(b, n, ATTN_HEADS * HEAD_DIM), kk[:, kk.shape[1] - wb:], vv[:, vv.shape[1] - wb:]


def decoder_layer(x, c, l, state, win_buf, prompt, wts):
    (ada_w, ada_b, norm1_g, norm2_g, w_in, q_norm_g, k_norm_g, attn_sinks,
     conf_dw, conf_ln_g, conf_ln_b, sconv_dw, w_out, w_ff1, w_ff2) = wts
    b, n, _ = x.shape
    mod = jax.nn.silu(c) @ ada_w[l] + ada_b[l]
    shift1, scale1, gate1, shift2, scale2, gate2 = jnp.split(mod, 6, axis=-1)

    h = rms_norm(x, norm1_g[l]) * (1.0 + scale1[:, None]) + shift1[:, None]
    z = h @ w_in[l]
    (r_q, r_k, r_v, r_g, a_q, a_k, a_v, c_a, c_b, s_b, s_c, s_h) = jnp.split(z, _split_offsets(), axis=-1)
    if prompt:
        ret_s0 = jnp.zeros((b, RET_HEADS, HEAD_DIM, HEAD_DIM), jnp.float32)
        conf_buf = jnp.zeros((b, CONF_WIDTH - 1, CONF_CH), x.dtype)
        sc_buf = jnp.zeros((b, SC_WIDTH - 1, SC_CH), x.dtype)
    else:
        ret_s0, k_buf, v_buf, conf_buf, sc_buf = state

    def to_heads(t):
        return t.reshape(b, n, -1, HEAD_DIM)
    ro, ret_s = retention(to_heads(r_q), to_heads(r_k), to_heads(r_v), ret_s0, RET_CHUNK if prompt else n)
    ro = ro - jnp.mean(ro, -1, keepdims=True)
    ro = ro * lax.rsqrt(jnp.mean(ro * ro, -1, keepdims=True) + EPS)
    out_a = ro.reshape(b, n, GROUP_WIDTH).astype(x.dtype) * jax.nn.silu(r_g)

    q = rms_norm(a_q.reshape(b, n, ATTN_KV_HEADS, ATTN_GROUP, HEAD_DIM), q_norm_g[l])
    k = rms_norm(a_k.reshape(b, n, ATTN_KV_HEADS, HEAD_DIM), k_norm_g[l])
    v = a_v.reshape(b, n, ATTN_KV_HEADS, HEAD_DIM)
    slopes = alibi_slopes(ATTN_HEADS).reshape(ATTN_KV_HEADS, ATTN_GROUP)
    sinks = attn_sinks[l].reshape(ATTN_KV_HEADS, ATTN_GROUP)
    if prompt:
        out_b = swa_prompt(q, k, v, slopes, sinks)
        k_new, v_new = k[:, n - win_buf:], v[:, n - win_buf:]
    else:
        out_b, k_new, v_new = swa_sample(q, k, v, k_buf, v_buf, slopes, sinks)

    c_y, conf_new = causal_dwconv(c_a * jax.nn.sigmoid(c_b), conf_buf, conf_dw[l])
    out_c = jax.nn.silu(layer_norm(c_y, conf_ln_g[l], conf_ln_b[l]))

    s_y, sc_new = causal_dwconv(s_c * s_h, sc_buf, sconv_dw[l])
    out_d = s_b * s_y

    mixed = jnp.concatenate([out_a, out_b, out_c, out_d], axis=-1) @ w_out[l]
    x = x + gate1[:, None] * mixed

    h2 = rms_norm(x, norm2_g[l]) * (1.0 + scale2[:, None]) + shift2[:, None]
    ff = jnp.square(jax.nn.relu(h2 @ w_ff1[l])) @ w_ff2[l]
    x = x + gate2[:, None] * ff
    return x, (ret_s.astype(x.dtype), k_new, v_new, conf_new, sc_new)


def setup_inputs(seed: int = 0) -> dict:
    key = jax.random.key(seed)
    ks = jax.random.split(key, 24)
    wb = min(WINDOW, PAST_LEN)

    def nrm(k, shape, s):
        return jax.random.normal(k, shape, jnp.float32) * s

    return {
        'x_prompt': nrm(ks[0], (BATCH, SEQ, D_MODEL), 1.0),
        'x_sample': nrm(ks[1], (DEC_BATCH, DEC_SEQ, D_MODEL), 1.0),
        'c_prompt': nrm(ks[2], (BATCH, D_MODEL), 1.0),
        'c_sample': nrm(ks[3], (DEC_BATCH, D_MODEL), 1.0),
        'state_ret': nrm(ks[4], (DEPTH, DEC_BATCH, RET_HEADS, HEAD_DIM, HEAD_DIM), 0.5),
        'cache_swa_k': nrm(ks[5], (DEPTH, DEC_BATCH, wb, ATTN_KV_HEADS, HEAD_DIM), 1.0),
        'cache_swa_v': nrm(ks[6], (DEPTH, DEC_BATCH, wb, ATTN_KV_HEADS, HEAD_DIM), 1.0),
        'state_conf': nrm(ks[7], (DEPTH, DEC_BATCH, CONF_WIDTH - 1, CONF_CH), 0.5),
        'state_sconv': nrm(ks[8], (DEPTH, DEC_BATCH, SC_WIDTH - 1, SC_CH), 1.0),
        'ada_w': nrm(ks[9], (DEPTH, D_MODEL, 6 * D_MODEL), 0.5 * D_MODEL ** -0.5),
        'ada_b': nrm(ks[10], (DEPTH, 6 * D_MODEL), 0.02),
        'norm1_g': 1.0 + nrm(ks[11], (DEPTH, D_MODEL), 0.02),
        'norm2_g': 1.0 + nrm(ks[12], (DEPTH, D_MODEL), 0.02),
        'w_in': nrm(ks[13], (DEPTH, D_MODEL, IN_WIDTH), D_MODEL ** -0.5),
        'q_norm_g': 1.0 + nrm(ks[14], (DEPTH, HEAD_DIM), 0.02),
        'k_norm_g': 1.0 + nrm(ks[15], (DEPTH, HEAD_DIM), 0.02),
        'attn_sinks': nrm(ks[16], (DEPTH, ATTN_HEADS), 0.5),
        'conf_dw': nrm(ks[17], (DEPTH, CONF_WIDTH, CONF_CH), CONF_WIDTH ** -0.5),
        'conf_ln_g': 1.0 + nrm(ks[18], (DEPTH, CONF_CH), 0.02),
        'conf_ln_b': nrm(ks[19], (DEPTH, CONF_CH), 0.02),
        'sconv_dw': nrm(ks[20], (DEPTH, SC_WIDTH, SC_CH), SC_WIDTH ** -0.5),
        'w_out': nrm(ks[21], (DEPTH, MIX_WIDTH, D_MODEL), MIX_WIDTH ** -0.5),
        'w_ff1': nrm(ks[22], (DEPTH, D_MODEL, D_FF), D_MODEL ** -0.5),
        'w_ff2': nrm(ks[23], (DEPTH, D_FF, D_MODEL), D_FF ** -0.5),
    }


def reference(x_prompt, x_sample, c_prompt, c_sample, state_ret, cache_swa_k, cache_swa_v, state_conf, state_sconv,
              ada_w, ada_b, norm1_g, norm2_g, w_in, q_norm_g, k_norm_g, attn_sinks,
              conf_dw, conf_ln_g, conf_ln_b, sconv_dw, w_out, w_ff1, w_ff2):
    wts = (ada_w, ada_b, norm1_g, norm2_g, w_in, q_norm_g, k_norm_g, attn_sinks,
           conf_dw, conf_ln_g, conf_ln_b, sconv_dw, w_out, w_ff1, w_ff2)
    win_buf = cache_swa_k.shape[2]

    y_prompt, prompt_states = x_prompt, []
    for l in range(DEPTH):
        y_prompt, st = decoder_layer(y_prompt, c_prompt, l, None, win_buf, True, wts)
        prompt_states.append(st)

    y_sample, sample_states = x_sample, []
    for l in range(DEPTH):
        st_in = (state_ret[l], cache_swa_k[l], cache_swa_v[l], state_conf[l], state_sconv[l])
        y_sample, st = decoder_layer(y_sample, c_sample, l, st_in, win_buf, False, wts)
        sample_states.append(st)

    def stack(sts, i):
        return jnp.stack([s[i] for s in sts])

    ret_p, k_p, v_p = stack(prompt_states, 0), stack(prompt_states, 1), stack(prompt_states, 2)
    conf_p, sconv_p = stack(prompt_states, 3), stack(prompt_states, 4)
    ret_s, k_s, v_s = stack(sample_states, 0), stack(sample_states, 1), stack(sample_states, 2)
    conf_s, sconv_s = stack(sample_states, 3), stack(sample_states, 4)
    return (y_prompt, y_sample, ret_p, k_p, v_p, conf_p, sconv_p, ret_s, k_s, v_s, conf_s, sconv_s)
```

```python
import contextlib
import numpy as np
import concourse.bass as bass
import concourse.mybir as mybir
from concourse.bass_utils import run_bass_kernel_spmd

F32 = mybir.dt.float32
BF = mybir.dt.bfloat16
AF = mybir.ActivationFunctionType
ALU = mybir.AluOpType
AX = mybir.AxisListType

ENGS = ("pe", "act", "dve", "pool", "sp")
EPOCH = 6000
EPS = 1e-6
NT = 17
DEPTH = 2


class Sched:
    def __init__(self, nc):
        self.nc = nc
        self.streams = {e: [] for e in ENGS}
        self.res = {}
        self.seen = {e: {} for e in ENGS}
        self.dma_count = {}
        self.flag = {e: set() for e in ENGS}

    def _need(self, eng, tok, waits):
        if tok is None:
            return
        if tok[0] == 'E':
            _, src, idx = tok
            if src == eng:
                if eng in ("pe", "sp"):
                    return
                if idx < len(self.streams[eng]) - 2:
                    return
            key = ('E', src)
        else:
            _, sem, idx = tok
            key = ('D', sem)
        if self.seen[eng].get(key, -1) >= idx:
            return
        waits[key] = max(waits.get(key, -1), idx)

    def op(self, eng, fn, reads=(), writes=(), dma_sem=None, inc=16, after=()):
        waits = {}
        for r in reads:
            st = self.res.get(r)
            if st is not None:
                self._need(eng, st['w'], waits)
        for w in list(writes) + list(after):
            st = self.res.get(w)
            if st is not None:
                self._need(eng, st['w'], waits)
                for t in st['r']:
                    self._need(eng, t, waits)
        wl = []
        for key, v in waits.items():
            self.seen[eng][key] = v
            if key[0] == 'E':
                self.flag[key[1]].add(v)
            wl.append((key, v))
        idx = len(self.streams[eng])
        if dma_sem is not None:
            c = self.dma_count.get(dma_sem, 0) + inc
            self.dma_count[dma_sem] = c
            tok = ('D', dma_sem, c)
        else:
            tok = ('E', eng, idx)
        self.streams[eng].append({'fn': fn, 'waits': wl, 'dma_sem': dma_sem, 'inc': inc})
        for r in reads:
            st = self.res.setdefault(r, {'w': None, 'r': []})
            st['r'].append(tok)
            if len(st['r']) > 64:
                st['r'] = _compact(st['r'])
        for w in writes:
            self.res[w] = {'w': tok, 'r': []}
        return tok

    def wait_all(self, eng):
        waits = {}
        for r, st in self.res.items():
            self._need(eng, st['w'], waits)
            for t in st['r']:
                self._need(eng, t, waits)
        wl = []
        for key, v in waits.items():
            self.seen[eng][key] = v
            if key[0] == 'E':
                self.flag[key[1]].add(v)
            wl.append((key, v))
        self.streams[eng].append({'fn': None, 'waits': wl, 'dma_sem': None, 'inc': 0})

    def emit(self, es):
        nc = self.nc
        val = {}
        nep = {}
        for e in ENGS:
            c = 0
            for i in range(len(self.streams[e])):
                if i in self.flag[e]:
                    val[(e, i)] = (c // EPOCH, c % EPOCH + 1)
                    c += 1
            nep[e] = (c + EPOCH - 1) // EPOCH
        esem = {}
        for e in ENGS:
            for k in range(nep[e]):
                esem[(e, k)] = es.enter_context(nc.semaphore(f"s_{e}_{k}"))
        dsem = {}
        for s in self.dma_count:
            dsem[s] = es.enter_context(nc.semaphore(f"d_{s}"))
        block = es.enter_context(nc.Block())
        engmap = {"pe": block.tensor, "act": block.scalar, "dve": block.vector,
                  "pool": block.gpsimd, "sp": block.sync}

        def mk(e):
            def body(eng):
                for i, o in enumerate(self.streams[e]):
                    for key, v in o['waits']:
                        if key[0] == 'E':
                            ep, vv = val[(key[1], v)]
                            eng.wait_ge(esem[(key[1], ep)], vv)
                        else:
                            eng.wait_ge(dsem[key[1]], v)
                    if o['fn'] is None:
                        continue
                    ins = o['fn'](eng)
                    if o['dma_sem'] is not None:
                        ins.then_inc(dsem[o['dma_sem']], o['inc'])
                    elif (e, i) in val:
                        ep, vv = val[(e, i)]
                        ins.then_inc(esem[(e, ep)], 1)
            return body
        for e in ENGS:
            if self.streams[e]:
                engmap[e](mk(e))


def _compact(toks):
    best = {}
    for t in toks:
        k = (t[0], t[1])
        if k not in best or best[k][2] < t[2]:
            best[k] = t
    return list(best.values())


C_OFF = {}


def _const_tables(core):
    j = core % 4
    gam = 1.0 - np.exp2(-5.0 - np.arange(4, dtype=np.float64))
    lg = np.log(gam)
    slopes = np.exp2(-8.0 * (np.arange(4, dtype=np.float64) + 1.0) / 4)
    parts = []

    def add(name, arr):
        arr = np.asarray(arr, np.float64).reshape(128, -1)
        C_OFF[name] = (sum(p.shape[1] for p in parts), arr.shape[1])
        parts.append(arr)

    i = np.arange(128)
    m = np.zeros((128, 4, 128))
    for h in range(4):
        d = i[None, :] - i[:, None]
        m[:, h, :] = np.where(d >= 0, np.exp(np.maximum(d, 0) * lg[h]), 0.0)
    add("maskP", m)
    q = np.zeros((128, 2, 128))
    for h in range(4):
        q[(h % 2) * 64:(h % 2) * 64 + 64, h // 2, :] = np.exp((i + 1.0) * lg[h])[None, :]
    add("qdecP", q)
    k = np.zeros((128, 4, 64))
    for h in range(4):
        k[:, h, :] = (0.125 * np.exp((127.0 - i) * lg[h]))[:, None]
    add("kdecP", k)
    c = np.zeros((128, 2))
    for h in range(4):
        c[(h % 2) * 64:(h % 2) * 64 + 64, h // 2] = np.exp(128.0 * lg[h])
    add("cdecP", c)
    t_ = i // 16
    s_ = i % 16
    m = np.zeros((128, 4, 128))
    same = (s_[:, None] == s_[None, :])
    for h in range(4):
        d = t_[None, :] - t_[:, None]
        m[:, h, :] = np.where(same & (d >= 0), np.exp(np.maximum(d, 0) * lg[h]), 0.0)
    add("maskS", m)
    q = np.zeros((128, 2, 128))
    tt = np.arange(128) % 8
    for h in range(4):
        q[(h % 2) * 64:(h % 2) * 64 + 64, h // 2, :] = np.exp((tt + 1.0) * lg[h])[None, :]
    add("qdecS", q)
    k = np.zeros((128, 4, 64))
    for h in range(4):
        k[:, h, :] = (0.125 * np.exp((7.0 - t_) * lg[h]))[:, None]
    add("kdecS", k)
    c = np.zeros((128, 2))
    for h in range(4):
        c[(h % 2) * 64:(h % 2) * 64 + 64, h // 2] = np.exp(8.0 * lg[h])
    add("cdecS", c)
    oh = np.zeros((128, 16))
    oh[i, s_] = 1.0
    add("onehot", oh)
    E = np.zeros((128, 2, 2, 2, 128))
    for kb in range(2):
        jk = kb * 128 + i[:, None]
        iq = 128 + i[None, :]
        dist = iq - jk
        valid = (dist >= 0) & (dist <= 128)
        for kv in range(2):
            for g in range(2):
                E[:, kv, kb, g, :] = np.where(valid, np.exp(-slopes[kv * 2 + g] * dist), 0.0)
    add("Ep", E)
    Es = np.zeros((128, 2, 512))
    d = t_[None, :] - t_[:, None]
    valid = same & (d >= 0)
    for kv in range(2):
        for g in range(2):
            Es[:, kv, g * 128:(g + 1) * 128] = np.where(valid, np.exp(-slopes[kv * 2 + g] * d), 0.0)
            for t in range(8):
                dist = 128 + t - i
                vc_ = (i >= t)
                val = np.where(vc_, np.exp(-slopes[kv * 2 + g] * dist), 0.0)
                for s_i in range(16):
                    Es[:, kv, 256 + s_i * 16 + g * 8 + t] = val
    add("Es", Es)
    sel = np.zeros((128, 4))
    if j > 0:
        sel[:, j - 1] = 1.0
    add("sel", sel)
    cs = np.zeros((128, 4, 2))
    for r in range(4):
        if r < j:
            for h in range(4):
                cs[(h % 2) * 64:(h % 2) * 64 + 64, r, h // 2] = np.exp(128.0 * 16 * (j - 1 - r) * lg[h])
    add("coefS", cs)
    add("hasprev", np.full((128, 1), 1.0 if j > 0 else 0.0))
    add("ident", np.eye(128))
    return np.concatenate(parts, 1).astype(np.float32)


_const_tables(0)
NCONST = sum(v[1] for v in C_OFF.values())

G_S, G_KV, G_CF, G_SC, G_ROWS = 0, 128, 256, 286, 288


def build():
    nc = bass.Bass("TRN2", target_bir_lowering=False)

    def din(name, shape):
        return nc.dram_tensor(name, list(shape), F32, kind="ExternalInput").ap()

    def dout(name, shape):
        return nc.dram_tensor(name, list(shape), F32, kind="ExternalOutput").ap()

    xp = din("xp", [2048, 1024]); xs = din("xs", [128, 1024])
    cP = din("cP", [128, 1024]); cS = din("cS", [128, 1024])
    ada_w = din("ada_w", [2, 1024, 6144]); ada_b = din("ada_b", [2, 1, 6144])
    n1g = din("n1g", [2, 1, 1024]); n2g = din("n2g", [2, 1, 1024])
    w_in = din("w_in", [2, 1024, 2816]); w_out = din("w_out", [2, 1024, 1024])
    w_ff1 = din("w_ff1", [2, 1024, 4096]); w_ff2 = din("w_ff2", [2, 4096, 1024])
    qng = din("qng", [2, 1, 64]); kng = din("kng", [2, 1, 64]); sinks = din("sinks", [2, 1, 4])
    cdwT = din("cdwT", [2, 128, 2, 31]); sdwT = din("sdwT", [2, 128, 2, 3])
    lng = din("lng", [2, 1, 256]); lnb = din("lnb", [2, 1, 256])
    retS = din("retS", [2, 128, 2, 16, 64])
    kcT = din("kcT", [2, 128, 16, 128]); vc = din("vc", [2, 128, 16, 128])
    kc_o = din("kc_o", [2, 16, 128, 128]); vc_o = din("vc_o", [2, 16, 128, 128])
    confT = din("confT", [2, 128, 2, 480]); conf_o = din("conf_o", [2, 16, 30, 256])
    scT = din("scT", [2, 128, 2, 32])
    consts = din("consts", [128, NCONST])

    yp = dout("yp", [2048, 1024]); ys = dout("ys", [128, 1024])
    o_retp = dout("o_retp", [2, 128, 2, 64]); o_kp = dout("o_kp", [2, 128, 128]); o_vp = dout("o_vp", [2, 128, 128])
    o_confp = dout("o_confp", [2, 30, 256]); o_scp = dout("o_scp", [2, 2, 256])
    o_rets = dout("o_rets", [2, 128, 2, 16, 64]); o_ks = dout("o_ks", [2, 16, 128, 128]); o_vs = dout("o_vs", [2, 16, 128, 128])
    o_confs = dout("o_confs", [2, 16, 30, 256]); o_scs = dout("o_scs", [2, 16, 2, 256])

    gin = [nc.dram_tensor(f"gin{l}", [G_ROWS, 256], F32, kind="Internal").ap() for l in range(2)]
    gout = [nc.dram_tensor(f"gout{l}", [4 * G_ROWS, 256], F32, kind="Internal").ap() for l in range(2)]

    S = Sched(nc)
    es = contextlib.ExitStack()
    with es:
        def sb(name, shape, dt=F32):
            return es.enter_context(nc.sbuf_tensor(name, list(shape), dt))

        def ps(name, shape, dt=F32):
            return es.enter_context(nc.psum_tensor(name, list(shape), dt))

        def V(fn, r=(), w=(), **k): return S.op("dve", fn, r, w, **k)
        def A(fn, r=(), w=(), **k): return S.op("act", fn, r, w, **k)
        def G(fn, r=(), w=(), **k): return S.op("pool", fn, r, w, **k)
        def T(fn, r=(), w=(), **k): return S.op("pe", fn, r, w, **k)
        def D(fn, r=(), w=(), sem="ld", **k): return S.op("sp", fn, r, w, dma_sem=sem, **k)
        def DG(fn, r=(), w=(), sem="ldg", **k): return S.op("pool", fn, r, w, dma_sem=sem, **k)

        x = sb("x", [128, NT, 1024])
        gateP = sb("gateP", [128, 1024])
        modPT = sb("modPT", [128, 16])
        R = sb("R", [128, 36352], BF)
        win = R[:, 0:22528].rearrange("p (k n) -> p k n", k=8)
        wout = R[:, 22528:30720].rearrange("p (k n) -> p k n", k=8)
        z = R[:, 30720:36352].bitcast(F32)
        h2T = R[:, 0:9216].rearrange("p (k n) -> p k n", k=8)
        W1g = [R[:, 9216 + i * 4096: 9216 + (i + 1) * 4096].rearrange("p (k n) -> p k n", k=8) for i in range(2)]
        W2g = [R[:, 17408 + i * 4096: 17408 + (i + 1) * 4096].rearrange("p (k n) -> p k n", k=4) for i in range(2)]
        uT = [R[:, 25600 + i * 2048: 25600 + (i + 1) * 2048].rearrange("p (k n) -> p k n", k=4) for i in range(2)]
        adaw = W1g
        MIX_NAMES = [f"win_{n}" for n in range(6)] + ["wout_0", "wout_1"] + [f"z{n}" for n in range(6)]
        FFN_NAMES = ["h2T", "W1g0", "W1g1", "W2g0", "W2g1", "uT0", "uT1"]

        identf = sb("identf", [128, 128]); identb = sb("identb", [128, 128], BF)
        cst = {}
        for nm, dt in [("maskP", F32), ("qdecP", BF), ("kdecP", F32), ("cdecP", F32), ("maskS", F32), ("qdecS", BF),
                       ("kdecS", F32), ("cdecS", F32), ("onehot", BF), ("Ep", BF), ("Es", BF),
                       ("sel", F32), ("coefS", F32), ("hasprev", F32)]:
            cst[nm] = sb("c_" + nm, [128, C_OFF[nm][1]], dt)
        cPT = sb("cPT", [128, 8, 128], BF); cST = sb("cST", [128, 8, 128], BF)
        adab = sb("adab", [1, 512]); ones1 = sb("ones1", [1, 128])
        gqk = sb("gqk", [128, 128])
        esink = sb("esink", [128, 4])
        lngb = sb("lngb", [128, 512])
        cw = sb("cw", [128, 2, 31]); sw = sb("sw", [128, 2, 3])
        hb = sb("hb", [128, 1024], BF)
        hT = sb("hT", [128, 8, 128], BF)
        ocs = hT[:].rearrange("p a b -> p (a b)").bitcast(F32).rearrange("p (a b) -> p a b", a=4)
        tmpf = sb("tmpf", [128, 1024])
        gsel = tmpf[:].rearrange("p (a b) -> p a b", a=4)
        st = sb("st", [128, 64])
        tr = sb("tr", [128, 6, 128], BF)
        trf = tr[:].rearrange("p a b -> p (a b)")
        qdT = sb("qdT", [128, 2, 128], BF)
        kTa = [sb(f"kTa{i}", [128, 128], BF) for i in range(2)]
        vaug = [sb(f"vaug{i}", [128, 2, 66], BF) for i in range(2)]
        rb = sb("rb", [128, 4, 256], BF)
        attm = sb("attm", [128, 512], BF)
        qn = sb("qn", [128, 256], BF)
        knf = sb("knf", [128, 128]); knb = sb("knb", [128, 128], BF)
        eepp = sb("eepp", [128, 2048], BF)
        ee = eepp[:, 0:1024]; pp = eepp[:, 1024:2048]
        EEPP = ["ee0", "ee1", "pp0", "pp1"]
        ef32 = eepp[:].bitcast(F32)
        t512 = [ef32[:, 0:512], ef32[:, 512:1024]]
        gb = ef32
        S0b = eepp[:].rearrange("p (a s e) -> p a s e", a=2, s=16)
        mixf = sb("mixf", [128, 1024])
        sg = mixf[:, 0:256]; u = mixf[:, 256:512]; vsf = mixf[:, 512:768]; t256 = mixf[:, 768:1024]
        gateS = mixf
        MIXF = ["sg", "u", "vsf", "_sa"]
        t256b = sb("t256b", [128, 256])
        extu = sb("extu", [128, 2, 608]); extv = sb("extv", [128, 2, 160])
        accs = sb("accs", [128, 4, 128])
        acc = accs[:, 0:2, :]; accv = accs[:, 2:4, :]
        vbdf = accs[:].rearrange("p a b -> p (a b)").bitcast(BF)
        vbd = vbdf.rearrange("p (s e) -> p s e", s=16)
        Sst = sb("Sst", [128, 2, 64]); Sbf = sb("Sbf", [128, 2, 64], BF)
        samp = sb("samp", [128, 4160], BF)
        S0 = samp[:, 0:4096].bitcast(F32).rearrange("p (a s e) -> p a s e", a=2, s=16)
        kcTb = samp[:, 0:2048].rearrange("p (s t) -> p s t", s=16)
        vcb = samp[:, 2048:4160].rearrange("p (s k e) -> p s k e", s=16, k=2)
        trs = sb("trs", [128, 4, 128], BF)
        modS_d = [[nc.dram_tensor(f"modS_{l}_{h}", [128, 3072], F32, kind="Internal").ap() for h in range(2)] for l in range(2)]

        B0 = ps("B0", [128, 512]); B1 = ps("B1", [128, 512])
        B2 = ps("B2", [128, 1024], BF)
        B3 = ps("B3", [128, 512])
        B4 = ps("B4", [128, 512]); B5 = ps("B5", [128, 512])
        B6 = ps("B6", [128, 512]); B7 = ps("B7", [128, 512])
        ZB = [B0, B1]
        B3N = ["B3a", "B3b"]; B7N = ["B7a", "B7b"]

        def cv(name):
            return cst[name]

        TF = ["tmpf0", "tmpf1"]
        SSTN = ["Sst00", "Sst01", "Sst10", "Sst11"]
        HB = ["hb", "cat0", "cat1", "cat2", "cat3"]
        last = None
        for nm in cst:
            o, n = C_OFF[nm]
            if cst[nm].dtype == BF:
                last = DG(lambda e, nm=nm, o=o, n=n: e.dma_start(out=cst[nm][:], in_=consts[:, o:o + n], allow_slow_non_contiguous=(n == 1)), w=["c_" + nm], sem="ldc")
            else:
                last = DG(lambda e, nm=nm, o=o, n=n: e.dma_start(out=cst[nm][:], in_=consts[:, o:o + n], allow_slow_non_contiguous=(n == 1)), w=["c_" + nm], sem="ldc")
        for nm in cst:
            S.res["c_" + nm]['w'] = last
        o_id, n_id = C_OFF["ident"]
        D(lambda e: e.dma_start(out=identf[:], in_=consts[:, o_id:o_id + n_id]), w=["identf"], sem="ld_identf")
        V(lambda e: e.tensor_copy(out=identb[:], in_=identf[:]), r=["identf"], w=["identb"])
        G(lambda e: e.memset(ones1[:], 1.0), w=["ones1"])
        for i in range(2):
            G(lambda e, i=i: e.memset(vaug[i][:], 1.0), w=[f"vaug{i}"])
        lastx = None
        for ti in range(16):
            lastx = D(lambda e, ti=ti: e.dma_start(out=x[:, ti, :], in_=xp[ti * 128:(ti + 1) * 128, :]), w=[f"x{ti}"], sem="ldx")
        lastx = D(lambda e: e.dma_start(out=x[:, 16, :], in_=xs), w=["x16"], sem="ldx")
        for ti in range(17):
            S.res[f"x{ti}"]['w'] = lastx

        for (cd, cT, nm) in [(cP, cPT, "cPT"), (cS, cST, "cST")]:
            D(lambda e, cd=cd: e.dma_start(out=tmpf[:], in_=cd), w=TF, sem="ld_tmpf")
            A(lambda e: e.activation(out=ef32[:], in_=tmpf[:], func=AF.Exp, scale=-1.0), r=TF, w=EEPP)
            G(lambda e: e.tensor_scalar_add(out=ef32[:], in0=ef32[:], scalar1=1.0), r=EEPP, w=EEPP)
            V(lambda e: e.reciprocal(out=mixf[:], in_=ef32[:]), r=EEPP, w=MIXF)
            V(lambda e: e.tensor_tensor(out=hb[:], in0=tmpf[:], in1=mixf[:], op=ALU.mult), r=TF + MIXF, w=["hb"])
            for k in range(8):
                T(lambda e, k=k: e.transpose(B2[:, k * 128:(k + 1) * 128], hb[:, k * 128:(k + 1) * 128], identb[:]), r=["hb", "identb"], w=["B2"])
            A(lambda e, cT=cT: e.activation(out=cT[:].rearrange("p k n -> p (k n)"), in_=B2[:], func=AF.Copy), r=["B2"], w=[nm])

        def rsqrt_small(dst, src, scale, rn, wn):
            A(lambda e: e.activation(out=dst, in_=src, func=AF.Ln, scale=scale, bias=EPS), r=rn, w=wn)
            A(lambda e: e.activation(out=dst, in_=dst, func=AF.Exp, scale=-0.5), r=wn, w=wn)

        def mod_phase(l, half):
            g_d = n1g if half == 0 else n2g
            D(lambda e: e.dma_start(out=gb[:], in_=g_d[l].partition_broadcast(128)), w=EEPP, sem="ld_gb")
            for n6 in range(6):
                n = half * 6 + n6
                b = n6 % 2
                DG(lambda e, n=n, b=b: e.dma_start(out=adaw[b][:], in_=ada_w[l][:, n * 512:(n + 1) * 512].rearrange("(k p) n -> p k n", p=128)),
                   w=[f"W1g{b}"], sem=f"ldf{b}", after=MIX_NAMES)
                D(lambda e, n=n: e.dma_start(out=adab[:], in_=ada_b[l][:, n * 512:(n + 1) * 512]), w=["adab"], sem="ld_adab")
                kind = n6 // 2
                c0 = (n6 % 2) * 512
                for gi, (cT, cn) in enumerate([(cPT, "cPT"), (cST, "cST")]):
                    bank = ZB[gi]
                    bn = f"B{gi}"
                    for k in range(8):
                        T(lambda e, k=k, cT=cT, bank=bank, b=b: e.matmul(bank[:], lhsT=cT[:, k, :], rhs=adaw[b][:, k, :], start=(k == 0), stop=False),
                          r=[cn, f"W1g{b}"], w=[bn])
                    T(lambda e, bank=bank: e.matmul(bank[:], lhsT=ones1[:], rhs=adab[:], start=False, stop=True), r=["ones1", "adab"], w=[bn])
                    if gi == 1:
                        stg = tmpf[:, 512:1024]
                        if kind == 1:
                            V(lambda e, bank=bank, c0=c0, stg=stg: e.scalar_tensor_tensor(out=stg, in0=bank[:], scalar=1.0, in1=gb[:, c0:c0 + 512], op0=ALU.add, op1=ALU.mult),
                              r=[bn] + EEPP, w=["tmpf1"])
                        else:
                            A(lambda e, bank=bank, stg=stg: e.activation(out=stg, in_=bank[:], func=AF.Copy), r=[bn], w=["tmpf1"])
                        D(lambda e, c0=c0, kind=kind, stg=stg: e.dma_start(out=modS_d[l][half][:, kind * 1024 + c0:kind * 1024 + c0 + 512], in_=stg), r=["tmpf1"], w=["modS_d"], sem="st_mod")
                    else:
                        if kind == 2:
                            A(lambda e, bank=bank, c0=c0: e.activation(out=gateP[:, c0:c0 + 512], in_=bank[:], func=AF.Copy), r=[bn], w=["gateP"])
                        else:
                            if kind == 1:
                                V(lambda e, bank=bank, c0=c0: e.scalar_tensor_tensor(out=tmpf[:, 0:512], in0=bank[:], scalar=1.0, in1=gb[:, c0:c0 + 512], op0=ALU.add, op1=ALU.mult),
                                  r=[bn] + EEPP, w=["tmpf0"])
                            else:
                                A(lambda e, bank=bank: e.activation(out=tmpf[:, 0:512], in_=bank[:], func=AF.Copy), r=[bn], w=["tmpf0"])
                            for q4 in range(4):
                                T(lambda e, q4=q4: e.transpose(B3[:, q4 * 128:(q4 + 1) * 128], tmpf[:, q4 * 128:(q4 + 1) * 128], identf[:]), r=["tmpf0", "identf"], w=B3N)
                            col = kind * 8 + (n6 % 2) * 4
                            V(lambda e, col=col: e.tensor_copy(out=modPT[:, col:col + 4], in_=B3[:].rearrange("p (a b) -> p a b", a=4)[:, :, 0]), r=B3N, w=["modPT"])

        def emit_h(ti, l, half):
            xn = f"x{ti}"
            A(lambda e: e.activation(out=hb[:], in_=x[:, ti, :], func=AF.Square, accum_out=st[:, 0:1]), r=[xn], w=HB + ["st0"])
            rsqrt_small(st[:, 1:2], st[:, 0:1], 1.0 / 1024, ["st0"], ["st1"])
            if ti < 16:
                V(lambda e: e.tensor_scalar(out=hb[:], in0=x[:, ti, :], scalar1=st[:, 1:2], scalar2=None, op0=ALU.mult), r=[xn, "st1"], w=HB)
            else:
                D(lambda e: e.dma_start(out=tmpf[:], in_=modS_d[l][half][:, 1024:2048]), r=["modS_d"], w=TF, sem="ld_tmpf")
                D(lambda e: e.dma_start(out=ef32[:], in_=modS_d[l][half][:, 0:1024]), r=["modS_d"], w=EEPP, sem="ld_gb")
                V(lambda e: e.scalar_tensor_tensor(out=tmpf[:], in0=x[:, ti, :], scalar=st[:, 1:2], in1=tmpf[:], op0=ALU.mult, op1=ALU.mult),
                  r=[xn, "st1"] + TF, w=TF)
                G(lambda e: e.tensor_tensor(out=hb[:], in0=tmpf[:], in1=ef32[:], op=ALU.add), r=TF + EEPP, w=HB)
            for k in range(8):
                T(lambda e, k=k: e.transpose(B2[:, k * 128:(k + 1) * 128], hb[:, k * 128:(k + 1) * 128], identb[:]), r=["hb", "identb"], w=["B2"])

        def evac_hT(ti, dst, dname, after=()):
            if ti < 16:
                for k in range(8):
                    if k % 2 == 0:
                        A(lambda e, k=k: e.activation(out=dst[:, k, :], in_=B2[:, k * 128:(k + 1) * 128], func=AF.Identity, scale=modPT[:, 8 + k:9 + k], bias=modPT[:, k:k + 1]),
                          r=["B2", "modPT"], w=[dname], after=after)
                    else:
                        V(lambda e, k=k: e.tensor_scalar(out=dst[:, k, :], in0=B2[:, k * 128:(k + 1) * 128], scalar1=modPT[:, 8 + k:9 + k], scalar2=modPT[:, k:k + 1], op0=ALU.mult, op1=ALU.add),
                          r=["B2", "modPT"], w=[dname], after=after)
            else:
                A(lambda e: e.activation(out=dst, in_=B2[:].rearrange("p (k n) -> p k n", k=8), func=AF.Copy), r=["B2"], w=[dname], after=after)

        def emit_z(chunks):
            for n in chunks:
                c0 = n * 512
                w_ = min(512, 2816 - c0)
                bank = ZB[n % 2]; bn = f"B{n % 2}"
                for k in range(8):
                    T(lambda e, k=k, bank=bank, c0=c0, w_=w_: e.matmul(bank[:, 0:w_], lhsT=hT[:, k, :], rhs=win[:, k, c0:c0 + w_], start=(k == 0), stop=(k == 7)),
                      r=["hT", f"win_{n}"], w=[bn])
                A(lambda e, bank=bank, c0=c0, w_=w_: e.activation(out=z[:, c0:c0 + w_], in_=bank[:, 0:w_], func=AF.Copy), r=[bn], w=[f"z{n}"], after=FFN_NAMES)

        def sigmoid_parts(src, rn):
            A(lambda e: e.activation(out=t256, in_=src, func=AF.Exp, scale=-1.0), r=rn, w=["_sa"])
            G(lambda e: e.tensor_scalar_add(out=t256, in0=t256, scalar1=1.0), r=["_sa"], w=["_sa"])
            V(lambda e: e.reciprocal(out=t256b[:], in_=t256), r=["_sa"], w=["_sb"])

        def emit_local_ret(ti):
            kd_c = cst["kdecP"] if ti < 16 else cst["kdecS"]
            kdn = "c_kdecP" if ti < 16 else "c_kdecS"
            A(lambda e: e.activation(out=rb[:, 0, :], in_=z[:, 0:256], func=AF.Copy), r=["z0"], w=["rb0"])
            G(lambda e: e.tensor_scalar(out=rb[:, 1, :], in0=z[:, 256:512], scalar1=0.125, scalar2=None, op0=ALU.mult), r=["z0"], w=["rb1"])
            G(lambda e: e.tensor_tensor(out=rb[:, 2, :], in0=z[:, 256:512], in1=kd_c[:], op=ALU.mult), r=["z0", kdn], w=["rb2"])
            A(lambda e: e.activation(out=rb[:, 3, :], in_=z[:, 512:768], func=AF.Copy), r=["z1"], w=["rb3"])

        def emit_su():
            for pr in range(2):
                T(lambda e, pr=pr: e.matmul(B6[:, 256 + pr * 128:256 + (pr + 1) * 128], lhsT=rb[:, 2, pr * 128:(pr + 1) * 128], rhs=rb[:, 3, pr * 128:(pr + 1) * 128], start=True, stop=True),
                  r=["rb2", "rb3"], w=["SU"])

        def state_update(cdec, cn):
            for pr in range(2):
                for hf in range(2):
                    rs = slice(hf * 64, hf * 64 + 64)
                    V(lambda e, pr=pr, hf=hf, rs=rs: e.scalar_tensor_tensor(out=Sst[rs, pr, :], in0=Sst[rs, pr, :], scalar=cdec[rs, pr:pr + 1],
                                                                                in1=B6[rs, 256 + pr * 128 + hf * 64:256 + pr * 128 + hf * 64 + 64], op0=ALU.mult, op1=ALU.add),
                      r=[f"Sst{pr}{hf}", "SU", cn], w=[f"Sst{pr}{hf}"])
            A(lambda e: e.activation(out=Sbf[:], in_=Sst[:], func=AF.Copy), r=SSTN, w=["Sbf"])

        def emit_local_rest(ti, cur):
            sigmoid_parts(z[:, 768:1024], ["z1"])
            V(lambda e: e.tensor_tensor(out=sg, in0=z[:, 768:1024], in1=t256b[:], op=ALU.mult), r=["z1", "_sb"], w=["sg"])
            V(lambda e: e.tensor_tensor(out=tmpf[:, 0:384], in0=z[:, 1024:1408], in1=z[:, 1024:1408], op=ALU.mult), r=["z2"], w=["tmpf0"])
            V(lambda e: e.tensor_reduce(out=st[:, 8:14], in_=tmpf[:, 0:384].rearrange("p (a b) -> p a b", a=6), axis=AX.X, op=ALU.add), r=["tmpf0"], w=["st8"])
            rsqrt_small(st[:, 8:14], st[:, 8:14], 1.0 / 64, ["st8"], ["st8"])
            V(lambda e: e.tensor_tensor(out=tmpf[:, 0:256].rearrange("p (a b) -> p a b", a=4), in0=z[:, 1024:1280].rearrange("p (a b) -> p a b", a=4),
                                        in1=st[:, 8:12][:, :, None].broadcast_to([128, 4, 64]), op=ALU.mult), r=["z2", "st8"], w=["tmpf0"])
            for k2 in range(2):
                for g2 in range(2):
                    hh = k2 * 2 + g2
                    V(lambda e, k2=k2, g2=g2, hh=hh: e.tensor_tensor(out=qn[:, g2 * 128 + k2 * 64:g2 * 128 + k2 * 64 + 64], in0=tmpf[:, hh * 64:(hh + 1) * 64], in1=gqk[:, 0:64], op=ALU.mult),
                      r=["tmpf0", "gqk"], w=["qn"])
            V(lambda e: e.tensor_tensor(out=tmpf[:, 256:384].rearrange("p (a b) -> p a b", a=2), in0=z[:, 1280:1408].rearrange("p (a b) -> p a b", a=2),
                                        in1=st[:, 12:14][:, :, None].broadcast_to([128, 2, 64]), op=ALU.mult), r=["z2", "st8"], w=["tmpf0"])
            for k2 in range(2):
                V(lambda e, k2=k2: e.tensor_tensor(out=knf[:, k2 * 64:(k2 + 1) * 64], in0=tmpf[:, 256 + k2 * 64:256 + (k2 + 1) * 64], in1=gqk[:, 64:128], op=ALU.mult), r=["tmpf0", "gqk"], w=["knf"])
            A(lambda e: e.activation(out=knb[:], in_=knf[:], func=AF.Copy), r=["knf"], w=["knb"])
            A(lambda e: e.activation(out=vaug[cur][:, :, 0:64], in_=z[:, 1408:1536].rearrange("p (a b) -> p a b", a=2), func=AF.Copy), r=["z2"], w=[f"vaug{cur}"])
            sigmoid_parts(z[:, 1792:2048], ["z3"])
            V(lambda e: e.tensor_tensor(out=u, in0=z[:, 1536:1792], in1=t256b[:], op=ALU.mult), r=["z3", "_sb"], w=["u"])
            G(lambda e: e.tensor_tensor(out=vsf, in0=z[:, 2304:2560], in1=z[:, 2560:2816], op=ALU.mult), r=["z4", "z5"], w=["vsf"])

        def emit_transposes(ti, cur):
            P_ = ti < 16
            hu = 30 if P_ else 480
            hv = 2 if P_ else 32
            srcs = [(rb[:, 0, 0:128], "rb0"), (rb[:, 0, 128:256], "rb0"), (rb[:, 1, 0:128], "rb1"), (rb[:, 1, 128:256], "rb1"),
                    (qn[:, 0:128], "qn"), (qn[:, 128:256], "qn"), (knb[:], "knb")]
            for i, (ap, nm) in enumerate(srcs):
                T(lambda e, i=i, ap=ap: e.transpose(B2[:, i * 128:(i + 1) * 128], ap, identb[:]), r=[nm, "identb"], w=["B2"])
            A(lambda e: e.activation(out=tr[:].rearrange("p a b -> p (a b)"), in_=B2[:, 0:768], func=AF.Copy), r=["B2"], w=["tr"])
            A(lambda e: e.activation(out=kTa[cur][:], in_=B2[:, 768:896], func=AF.Copy), r=["B2"], w=[f"kTa{cur}"])
            if P_:
                V(lambda e: e.tensor_tensor(out=qdT[:].rearrange("p a b -> p (a b)"), in0=B2[:, 0:256], in1=cst["qdecP"][:], op=ALU.mult), r=["B2", "c_qdecP"], w=["qdT"])
            for c in range(2):
                T(lambda e, c=c: e.transpose(B3[:, c * 128:(c + 1) * 128], u[:, c * 128:(c + 1) * 128], identf[:]), r=["u", "identf"], w=B3N)
            for c in range(2):
                T(lambda e, c=c: e.transpose(B3[:, 256 + c * 128:256 + (c + 1) * 128], vsf[:, c * 128:(c + 1) * 128], identf[:]), r=["vsf", "identf"], w=B3N)
            A(lambda e: e.activation(out=extu[:, :, hu:hu + 128], in_=B3[:, 0:256].rearrange("p (a b) -> p a b", a=2), func=AF.Copy), r=B3N, w=["extu_new"])
            A(lambda e: e.activation(out=extv[:, :, hv:hv + 128], in_=B3[:, 256:512].rearrange("p (a b) -> p a b", a=2), func=AF.Copy), r=B3N, w=["extv_new"])

        def emit_conv(ti):
            P_ = ti < 16
            stp = 1 if P_ else 16
            for jj in range(31):
                for c in range(2):
                    dst = acc if jj % 2 == 0 else accv
                    dn = f"acc{c}" if jj % 2 == 0 else f"accv{c}"
                    if jj < 2:
                        V(lambda e, c=c, jj=jj, dst=dst: e.tensor_scalar(out=dst[:, c, :], in0=extu[:, c, jj * stp:jj * stp + 128], scalar1=cw[:, c, jj:jj + 1], scalar2=None, op0=ALU.mult),
                          r=["extu_new", "extu_halo", "cw"], w=[dn])
                    else:
                        V(lambda e, c=c, jj=jj, dst=dst: e.scalar_tensor_tensor(out=dst[:, c, :], in0=extu[:, c, jj * stp:jj * stp + 128], scalar=cw[:, c, jj:jj + 1], in1=dst[:, c, :], op0=ALU.mult, op1=ALU.add),
                          r=["extu_new", "extu_halo", "cw", dn], w=[dn])
            for c in range(2):
                V(lambda e, c=c: e.tensor_tensor(out=acc[:, c, :], in0=acc[:, c, :], in1=accv[:, c, :], op=ALU.add), r=[f"acc{c}", f"accv{c}"], w=[f"acc{c}"])
            for jj in range(3):
                for c in range(2):
                    if jj == 0:
                        V(lambda e, c=c: e.tensor_scalar(out=accv[:, c, :], in0=extv[:, c, 0:128], scalar1=sw[:, c, 0:1], scalar2=None, op0=ALU.mult),
                          r=["extv_new", "extv_halo", "sw"], w=[f"accv{c}"])
                    else:
                        V(lambda e, c=c, jj=jj: e.scalar_tensor_tensor(out=accv[:, c, :], in0=extv[:, c, jj * stp:jj * stp + 128], scalar=sw[:, c, jj:jj + 1], in1=accv[:, c, :], op0=ALU.mult, op1=ALU.add),
                          r=["extv_new", "extv_halo", "sw", f"accv{c}"], w=[f"accv{c}"])
            if ti == 0: ck(4.41)
            if P_:
                G(lambda e: e.tensor_copy(out=extu[:, :, 0:30], in_=extu[:, :, 128:158]), r=["extu_new", "acc0", "acc1"], w=["extu_halo"])
                G(lambda e: e.tensor_copy(out=extv[:, :, 0:2], in_=extv[:, :, 128:130]), r=["extv_new", "accv0", "accv1"], w=["extv_halo"])
            if ti == 0: ck(4.42)
            for c in range(2):
                T(lambda e, c=c: e.transpose(B3[:, c * 128:(c + 1) * 128], acc[:, c, :], identf[:]), r=[f"acc{c}", "identf"], w=B3N)
            for c in range(2):
                T(lambda e, c=c: e.transpose(B3[:, 256 + c * 128:256 + (c + 1) * 128], accv[:, c, :], identf[:]), r=[f"accv{c}", "identf"], w=B3N)
            V(lambda e: e.tensor_tensor(out=hb[:, 768:1024], in0=z[:, 2048:2304], in1=B3[:, 256:512], op=ALU.mult), r=["z4"] + B3N, w=["cat3"], after=["hb"])
            if ti == 0: ck(4.43)
            V(lambda e: e.tensor_copy(out=t256, in_=B3[:, 0:256]), r=B3N, w=["_sa"])
            if ti == 0: ck(4.431)
            V(lambda e: e.tensor_reduce(out=st[:, 56:58], in_=t256.rearrange("p (a b) -> p a b", a=2), axis=AX.X, op=ALU.add), r=["_sa"], w=["st56"])
            if ti == 0: ck(4.432)
            A(lambda e: e.activation(out=t256b[:], in_=t256, func=AF.Square), r=["_sa"], w=["_sb"])
            V(lambda e: e.tensor_reduce(out=st[:, 58:60], in_=t256b[:].rearrange("p (a b) -> p a b", a=2), axis=AX.X, op=ALU.add), r=["_sb"], w=["st58"])
            if ti == 0: ck(4.435)
            V(lambda e: e.tensor_tensor(out=st[:, 16:17], in0=st[:, 56:57], in1=st[:, 57:58], op=ALU.add), r=["st56"], w=["st16"])
            V(lambda e: e.tensor_tensor(out=st[:, 17:18], in0=st[:, 58:59], in1=st[:, 59:60], op=ALU.add), r=["st58"], w=["st17"])
            V(lambda e: e.tensor_scalar(out=st[:, 18:19], in0=st[:, 16:17], scalar1=1.0 / 256, scalar2=None, op0=ALU.mult), r=["st16"], w=["st18"])
            V(lambda e: e.tensor_tensor(out=st[:, 19:20], in0=st[:, 18:19], in1=st[:, 18:19], op=ALU.mult), r=["st18"], w=["st19"])
            V(lambda e: e.scalar_tensor_tensor(out=st[:, 20:21], in0=st[:, 17:18], scalar=1.0 / 256, in1=st[:, 19:20], op0=ALU.mult, op1=ALU.subtract), r=["st17", "st19"], w=["st20"])
            if ti == 0: ck(4.437)
            rsqrt_small(st[:, 21:22], st[:, 20:21], 1.0, ["st20"], ["st21"])
            if ti == 0: ck(4.44)
            V(lambda e: e.tensor_scalar(out=t256, in0=t256, scalar1=st[:, 18:19], scalar2=st[:, 21:22], op0=ALU.subtract, op1=ALU.mult), r=["_sa", "st18", "st21"], w=["_sa"])
            V(lambda e: e.tensor_tensor(out=t256, in0=t256, in1=lngb[:, 0:256], op=ALU.mult), r=["_sa", "lngb"], w=["_sa"])
            V(lambda e: e.tensor_tensor(out=t256, in0=t256, in1=lngb[:, 256:512], op=ALU.add), r=["_sa", "lngb"], w=["_sa"])
            A(lambda e: e.activation(out=t256b[:], in_=t256, func=AF.Exp, scale=-1.0), r=["_sa"], w=["_sb"])
            G(lambda e: e.tensor_scalar_add(out=t256b[:], in0=t256b[:], scalar1=1.0), r=["_sb"], w=["_sb"])
            V(lambda e: e.reciprocal(out=t256b[:], in_=t256b[:]), r=["_sb"], w=["_sb"])
            V(lambda e: e.tensor_tensor(out=hb[:, 512:768], in0=t256, in1=t256b[:], op=ALU.mult), r=["_sa", "_sb"], w=["cat2"], after=["hb"])

        def emit_groupnorm_out(o_ap, rnames):
            o3 = o_ap.rearrange("p (a b) -> p a b", a=4)
            tq = tmpf[:, 512:768]
            tq3 = tq.rearrange("p (a b) -> p a b", a=4)
            V(lambda e: e.tensor_reduce(out=st[:, 24:28], in_=o3, axis=AX.X, op=ALU.add), r=rnames, w=["st24"])
            A(lambda e: e.activation(out=tq, in_=o_ap, func=AF.Square), r=rnames, w=["tmpf1"])
            V(lambda e: e.tensor_reduce(out=st[:, 28:32], in_=tq3, axis=AX.X, op=ALU.add), r=["tmpf1"], w=["st28"])
            V(lambda e: e.tensor_scalar(out=st[:, 32:36], in0=st[:, 24:28], scalar1=1.0 / 64, scalar2=None, op0=ALU.mult), r=["st24"], w=["st32"])
            V(lambda e: e.tensor_tensor(out=st[:, 36:40], in0=st[:, 32:36], in1=st[:, 32:36], op=ALU.mult), r=["st32"], w=["st36"])
            V(lambda e: e.scalar_tensor_tensor(out=st[:, 40:44], in0=st[:, 28:32], scalar=1.0 / 64, in1=st[:, 36:40], op0=ALU.mult, op1=ALU.subtract), r=["st28", "st36"], w=["st40"])
            rsqrt_small(st[:, 44:48], st[:, 40:44], 1.0, ["st40"], ["st44"])
            V(lambda e: e.tensor_tensor(out=tq3, in0=o3, in1=st[:, 32:36][:, :, None].broadcast_to([128, 4, 64]), op=ALU.subtract), r=rnames + ["st32"], w=["tmpf1"])
            V(lambda e: e.tensor_tensor(out=tq3, in0=tq3, in1=st[:, 44:48][:, :, None].broadcast_to([128, 4, 64]), op=ALU.mult), r=["tmpf1", "st44"], w=["tmpf1"])
            V(lambda e: e.tensor_tensor(out=hb[:, 0:256], in0=tq, in1=sg, op=ALU.mult), r=["tmpf1", "sg"], w=["cat0"], after=["hb"])

        def emit_attn_finish(oa_ap, rnames):
            V(lambda e: e.tensor_tensor(out=st[:, 48:52], in0=oa_ap[:, :, 64], in1=esink[:], op=ALU.add), r=rnames + ["esink"], w=["st48"])
            V(lambda e: e.reciprocal(out=st[:, 52:56], in_=st[:, 48:52]), r=["st48"], w=["st52"])
            V(lambda e: e.tensor_tensor(out=hb[:, 256:512].rearrange("p (a b) -> p a b", a=4), in0=oa_ap[:, :, 0:64], in1=st[:, 52:56][:, :, None].broadcast_to([128, 4, 64]), op=ALU.mult),
              r=rnames + ["st52"], w=["cat1"], after=["hb"])

        def emit_out(ti):
            for k in range(8):
                T(lambda e, k=k: e.transpose(B2[:, k * 128:(k + 1) * 128], hb[:, k * 128:(k + 1) * 128], identb[:]), r=["cat0", "cat1", "cat2", "cat3", "identb"], w=["B2"])
            A(lambda e: e.activation(out=hT[:].rearrange("p a b -> p (a b)"), in_=B2[:], func=AF.Copy), r=["B2"], w=["hT"])
            gate = gateP if ti < 16 else gateS
            gn = ["gateP"] if ti < 16 else MIXF
            for hf in range(2):
                bank = [B4, B5][hf]; bn = f"B{4 + hf}"
                for k in range(8):
                    T(lambda e, k=k, bank=bank, hf=hf: e.matmul(bank[:], lhsT=hT[:, k, :], rhs=wout[:, k, hf * 512:(hf + 1) * 512], start=(k == 0), stop=(k == 7)), r=["hT", f"wout_{hf}"], w=[bn])
                V(lambda e, bank=bank, hf=hf, gate=gate: e.tensor_tensor(out=tmpf[:, hf * 512:(hf + 1) * 512], in0=bank[:], in1=gate[:, hf * 512:(hf + 1) * 512], op=ALU.mult), r=[bn] + gn, w=[f"tmpf{hf}"])
                G(lambda e, hf=hf: e.tensor_tensor(out=x[:, ti, hf * 512:(hf + 1) * 512], in0=x[:, ti, hf * 512:(hf + 1) * 512], in1=tmpf[:, hf * 512:(hf + 1) * 512], op=ALU.add),
                  r=[f"tmpf{hf}", f"x{ti}"], w=[f"x{ti}"])
            if os.environ.get("KDUMPMIX") and ti < 16:
                D(lambda e: e.dma_start(out=yp[ti * 128:(ti + 1) * 128, :], in_=tmpf[:]), r=TF, w=["yp"], sem="st_x")

        import os
        STOP = float(os.environ.get("KSTOP", "99"))

        class _Stop(Exception):
            pass

        def ck(n):
            if STOP <= n:
                raise _Stop()
        try:
          ck(1)
          def emit_layer(l):
            mod_phase(l, 0)
            if os.environ.get("KDUMPGATE") and l == 0:
                D(lambda e: e.dma_start(out=yp[0:128, :], in_=gateP[:]), r=["gateP"], w=["yp"], sem="st_x")
                V(lambda e: e.tensor_copy(out=tmpf[:, 0:16], in_=modPT[:]), r=["modPT"], w=["tmpf0"])
                D(lambda e: e.dma_start(out=yp[128:256, 0:16], in_=tmpf[:, 0:16]), r=["tmpf0"], w=["yp"], sem="st_x")
            ck(2)
            for n in range(6):
                c0 = n * 512; w_ = min(512, 2816 - c0)
                DG(lambda e, c0=c0, w_=w_: e.dma_start(out=win[:, :, c0:c0 + w_], in_=w_in[l][:, c0:c0 + w_].rearrange("(k p) n -> p k n", p=128)),
                   w=[f"win_{n}"], sem="ldw", after=FFN_NAMES)
            lastw = None
            for hf in range(2):
                lastw = DG(lambda e, hf=hf: e.dma_start(out=wout[:, :, hf * 512:(hf + 1) * 512], in_=w_out[l][:, hf * 512:(hf + 1) * 512].rearrange("(k p) n -> p k n", p=128)),
                           w=[f"wout_{hf}"], sem="ldw", after=FFN_NAMES)
            for nm_ in [f"win_{n}" for n in range(6)] + ["wout_0", "wout_1"]:
                S.res[nm_]['w'] = lastw
            D(lambda e: e.dma_start(out=gqk[:, 0:64], in_=qng[l].partition_broadcast(128)), w=["gqk"], sem="ld_gqk")
            D(lambda e: e.dma_start(out=gqk[:, 64:128], in_=kng[l].partition_broadcast(128)), w=["gqk"], sem="ld_gqk")
            D(lambda e: e.dma_start(out=esink[:], in_=sinks[l].partition_broadcast(128)), w=["esink"], sem="ld_esink")
            A(lambda e: e.activation(out=esink[:], in_=esink[:], func=AF.Exp), r=["esink"], w=["esink"])
            D(lambda e: e.dma_start(out=lngb[:, 0:256], in_=lng[l].partition_broadcast(128)), w=["lngb"], sem="ld_lngb")
            D(lambda e: e.dma_start(out=lngb[:, 256:512], in_=lnb[l].partition_broadcast(128)), w=["lngb"], sem="ld_lngb")
            D(lambda e: e.dma_start(out=cw[:], in_=cdwT[l]), w=["cw"], sem="ld_cw")
            D(lambda e: e.dma_start(out=sw[:], in_=sdwT[l]), w=["sw"], sem="ld_sw")

            V(lambda e: e.memset(Sst[:], 0.0), w=SSTN)
            for ti in range(16):
                emit_h(ti, l, 0)
                evac_hT(ti, hT[:], "hT")
                emit_z([0, 1] if ti < 15 else [0, 1, 2, 3, 4, 5])
                emit_local_ret(ti)
                emit_su()
                state_update(cst["cdecP"], "c_cdecP")
                if ti == 15:
                    emit_local_rest(ti, 0)
                    D(lambda e: e.dma_start(out=gin[l][G_S:G_S + 128, 0:128], in_=Sst[:].rearrange("p a b -> p (a b)")), r=SSTN, w=["gin"], sem="stg")
                    V(lambda e: e.memset(t256b[:, 0:128], 0.0), w=["_sb"])
                    D(lambda e: e.dma_start(out=gin[l][G_S:G_S + 128, 128:256], in_=t256b[:, 0:128]), r=["_sb"], w=["gin"], sem="stg")
                    D(lambda e: e.dma_start(out=gin[l][G_KV:G_KV + 128, 0:128], in_=knf[:]), r=["knf"], w=["gin"], sem="stg")
                    D(lambda e: e.dma_start(out=gin[l][G_KV:G_KV + 128, 128:256], in_=z[:, 1408:1536]), r=["z2"], w=["gin"], sem="stg")
                    D(lambda e: e.dma_start(out=gin[l][G_CF:G_CF + 30, :], in_=u[98:128, :]), r=["u"], w=["gin"], sem="stg")
                    D(lambda e: e.dma_start(out=gin[l][G_SC:G_SC + 2, :], in_=vsf[126:128, :]), r=["vsf"], w=["gin"], sem="stg")
                    D(lambda e: e.dma_start(out=o_kp[l], in_=knf[:]), r=["knf"], w=["o_kp"], sem="st_knf")
                    D(lambda e: e.dma_start(out=o_vp[l], in_=z[:, 1408:1536]), r=["z2"], w=["o_vp"], sem="st_z2")
                    D(lambda e: e.dma_start(out=o_confp[l], in_=u[98:128, :]), r=["u"], w=["o_confp"], sem="st_u")
                    D(lambda e: e.dma_start(out=o_scp[l], in_=vsf[126:128, :]), r=["vsf"], w=["o_scp"], sem="st_vsf")
            ck(3)
            S.op("pool", lambda e: e.collective_compute("AllGather", ALU.bypass, replica_groups=[[0, 1, 2, 3], [4, 5, 6, 7]], ins=[gin[l]], outs=[gout[l]]),
                 reads=["gin"], writes=["gout"], dma_sem="cc", inc=1)
            gv = gout[l].rearrange("(r p) n -> p r n", p=G_ROWS)
            D(lambda e: e.dma_start(out=gsel[:], in_=gv[G_S:G_S + 128, :, :]), r=["gout"], w=TF, sem="ld_tmpf")
            for pr in range(2):
                for r4 in range(4):
                    src = gsel[:, r4, pr * 64:(pr + 1) * 64]
                    if r4 == 0:
                        V(lambda e, pr=pr, src=src: e.tensor_scalar(out=Sst[:, pr, :], in0=src, scalar1=cst["coefS"][:, pr:pr + 1], scalar2=None, op0=ALU.mult),
                          r=TF + ["c_coefS"], w=[f"Sst{pr}0", f"Sst{pr}1"])
                    else:
                        V(lambda e, pr=pr, src=src, r4=r4: e.scalar_tensor_tensor(out=Sst[:, pr, :], in0=src, scalar=cst["coefS"][:, r4 * 2 + pr:r4 * 2 + pr + 1], in1=Sst[:, pr, :], op0=ALU.mult, op1=ALU.add),
                          r=TF + ["c_coefS", f"Sst{pr}0", f"Sst{pr}1"], w=[f"Sst{pr}0", f"Sst{pr}1"])
            A(lambda e: e.activation(out=Sbf[:], in_=Sst[:], func=AF.Copy), r=SSTN, w=["Sbf"])

            def select_rows(rows0, nrows, dst, dname):
                D(lambda e: e.dma_start(out=gsel[0:nrows, :, :], in_=gv[rows0:rows0 + nrows, :, :]), r=["gout"], w=TF, sem="ld_tmpf")
                for r4 in range(4):
                    if r4 == 0:
                        V(lambda e: e.tensor_scalar(out=dst, in0=gsel[0:nrows, 0, :], scalar1=cst["sel"][0:nrows, 0:1], scalar2=None, op0=ALU.mult), r=TF + ["c_sel"], w=[dname])
                    else:
                        V(lambda e, r4=r4: e.scalar_tensor_tensor(out=dst, in0=gsel[0:nrows, r4, :], scalar=cst["sel"][0:nrows, r4:r4 + 1], in1=dst, op0=ALU.mult, op1=ALU.add),
                          r=TF + ["c_sel", dname], w=[dname])
            hsel = ef32[:, 0:256]
            select_rows(G_KV, 128, hsel, "ee0")
            A(lambda e: e.activation(out=knb[:], in_=hsel[:, 0:128], func=AF.Copy), r=["ee0"], w=["knb"])
            A(lambda e: e.activation(out=vaug[1][:, :, 0:64], in_=hsel[:, 128:256].rearrange("p (a b) -> p a b", a=2), func=AF.Copy), r=["ee0"], w=["vaug1"])
            T(lambda e: e.transpose(B2[:, 0:128], knb[:], identb[:]), r=["knb", "identb"], w=["B2"])
            A(lambda e: e.activation(out=kTa[1][:], in_=B2[:, 0:128], func=AF.Copy), r=["B2"], w=["kTa1"])
            select_rows(G_CF, 30, hsel[0:30, :], "ee0")
            for c in range(2):
                T(lambda e, c=c: e.transpose(B3[:, c * 32:c * 32 + 30], hsel[0:30, c * 128:(c + 1) * 128], identf[0:30, 0:30]), r=["ee0", "identf"], w=B3N)
            A(lambda e: e.activation(out=extu[:, :, 0:30], in_=B3[:, 0:64].rearrange("p (a b) -> p a b", a=2)[:, :, 0:30], func=AF.Copy), r=B3N, w=["extu_halo"], after=["extu_new"])
            select_rows(G_SC, 2, hsel[0:2, :], "ee0")
            for c in range(2):
                T(lambda e, c=c: e.transpose(B3[:, 64 + c * 32:64 + c * 32 + 2], hsel[0:2, c * 128:(c + 1) * 128], identf[0:2, 0:2]), r=["ee0", "identf"], w=B3N)
            A(lambda e: e.activation(out=extv[:, :, 0:2], in_=B3[:, 64:128].rearrange("p (a b) -> p a b", a=2)[:, :, 0:2], func=AF.Copy), r=B3N, w=["extv_halo"], after=["extv_new"])

            ck(4)
            for ti in range(16):
                if ti == 1:
                    ck(5)
                cur = ti % 2
                prv = 1 - cur
                emit_h(ti, l, 0)
                evac_hT(ti, hT[:], "hT")
                emit_z([0, 1, 2, 3, 4, 5])
                emit_local_ret(ti)
                emit_local_rest(ti, cur)
                if ti == 0: ck(4.1)
                emit_transposes(ti, cur)
                if ti == 0: ck(4.2)
                for h in range(4):
                    rs = slice((h % 2) * 64, (h % 2) * 64 + 64)
                    bank, bn = (B7, "B7a") if h % 2 == 0 else (B3, "B3a")
                    T(lambda e, h=h, rs=rs, bank=bank: e.matmul(bank[:, (h // 2) * 128:(h // 2 + 1) * 128], lhsT=tr[rs, 2 + h // 2, :], rhs=tr[rs, h // 2, :], start=True, stop=True), r=["tr"], w=[bn])
                attm4 = attm[:].rearrange("p (pr hf i) -> p pr hf i", pr=2, hf=2)
                mask4 = cst["maskP"][:].rearrange("p (pr hf i) -> p pr hf i", pr=2, hf=2)
                for hf, (bank, bn) in enumerate([(B7, "B7a"), (B3, "B3a")]):
                    V(lambda e, hf=hf, bank=bank, attm4=attm4, mask4=mask4: e.tensor_tensor(out=attm4[:, :, hf, :], in0=bank[:, 0:256].rearrange("p (pr i) -> p pr i", pr=2), in1=mask4[:, :, hf, :], op=ALU.mult),
                      r=[bn, "c_maskP"], w=["attm"])
                if ti == 0: ck(4.25)
                for h in range(4):
                    T(lambda e, h=h: e.matmul(B6[:, h * 64:(h + 1) * 64], lhsT=attm[:, h * 128:(h + 1) * 128], rhs=rb[:, 3, h * 64:(h + 1) * 64], start=True, stop=True), r=["attm", "rb3"], w=["o"])
                for h in (0, 2, 1, 3):
                    rs = slice((h % 2) * 64, (h % 2) * 64 + 64)
                    bank, bn = (B7, "B7b") if h % 2 == 0 else (B3, "B3b")
                    T(lambda e, h=h, rs=rs, bank=bank: e.matmul(bank[:, 256 + (h // 2) * 64:256 + (h // 2 + 1) * 64], lhsT=qdT[rs, h // 2, :], rhs=Sbf[rs, h // 2, :], start=True, stop=True), r=["qdT", "Sbf"], w=[bn])
                emit_su()
                state_update(cst["cdecP"], "c_cdecP")
                osum_p = tmpf[:, 0:256]
                os4 = osum_p.rearrange("p (pr hf e) -> p pr hf e", pr=2, hf=2)
                for hf, (bank, bn) in enumerate([(B7, "B7b"), (B3, "B3b")]):
                    A(lambda e, hf=hf, bank=bank: e.activation(out=os4[:, :, hf, :], in_=bank[:, 256:384].rearrange("p (pr e) -> p pr e", pr=2), func=AF.Copy), r=[bn], w=["tmpf0"])
                V(lambda e: e.tensor_tensor(out=osum_p, in0=osum_p, in1=B6[:, 0:256], op=ALU.add), r=["tmpf0", "o"], w=["tmpf0"])
                emit_groupnorm_out(osum_p, ["tmpf0"])
                if ti == 0: ck(4.3)
                for kv, (bank, bn) in enumerate([(B4, "B4"), (B5, "B5")]):
                    rs = slice(kv * 64, kv * 64 + 64)
                    for kb, (kt, ktn) in enumerate([(kTa[prv], f"kTa{prv}"), (kTa[cur], f"kTa{cur}")]):
                        T(lambda e, bank=bank, kb=kb, rs=rs, kt=kt: e.matmul(bank[:, kb * 256:(kb + 1) * 256], lhsT=kt[rs, :], rhs=trf[rs, 512:768], start=True, stop=True),
                          r=[ktn, "tr"], w=[bn])
                    A(lambda e, bank=bank, kv=kv: e.activation(out=ee[:, kv * 512:(kv + 1) * 512], in_=bank[:], func=AF.Exp, scale=0.125), r=[bn], w=[f"ee{kv}"])
                    if ti == 0:
                        V(lambda e, kv=kv: e.scalar_tensor_tensor(out=pp[:, kv * 512:kv * 512 + 256], in0=ee[:, kv * 512:kv * 512 + 256], scalar=cst["hasprev"][:, 0:1], in1=cst["Ep"][:, kv * 512:kv * 512 + 256], op0=ALU.mult, op1=ALU.mult),
                          r=[f"ee{kv}", "c_Ep", "c_hasprev"], w=[f"pp{kv}"])
                        V(lambda e, kv=kv: e.tensor_tensor(out=pp[:, kv * 512 + 256:(kv + 1) * 512], in0=ee[:, kv * 512 + 256:(kv + 1) * 512], in1=cst["Ep"][:, kv * 512 + 256:(kv + 1) * 512], op=ALU.mult),
                          r=[f"ee{kv}", "c_Ep"], w=[f"pp{kv}"])
                    else:
                        V(lambda e, kv=kv: e.tensor_tensor(out=pp[:, kv * 512:(kv + 1) * 512], in0=ee[:, kv * 512:(kv + 1) * 512], in1=cst["Ep"][:, kv * 512:(kv + 1) * 512], op=ALU.mult),
                          r=[f"ee{kv}", "c_Ep"], w=[f"pp{kv}"])
                oa = B7[:, 0:320].rearrange("p (a b) -> p a b", a=4)
                for kv in range(2):
                    for g in range(2):
                        hh = kv * 2 + g
                        for kb, va, van in [(0, vaug[prv], f"vaug{prv}"), (1, vaug[cur], f"vaug{cur}")]:
                            c0 = kv * 512 + kb * 256 + g * 128
                            T(lambda e, kb=kb, kv=kv, hh=hh, va=va, c0=c0: e.matmul(B7[:, hh * 80:hh * 80 + 66], lhsT=pp[:, c0:c0 + 128], rhs=va[:, kv, 0:66],
                                                                                   start=(kb == 0), stop=(kb == 1)), r=[f"pp{kv}", van], w=B7N)
                emit_attn_finish(oa, B7N)
                if ti == 0: ck(4.4)
                emit_conv(ti)
                if ti == 0: ck(4.5)
                if os.environ.get("KDUMPCAT"):
                    A(lambda e: e.activation(out=tmpf[:], in_=hb[:], func=AF.Copy), r=["cat0", "cat1", "cat2", "cat3"], w=TF)
                    D(lambda e, ti=ti: e.dma_start(out=yp[ti * 128:(ti + 1) * 128, :], in_=tmpf[:]), r=TF, w=["yp"], sem="st_x")
                emit_out(ti)
                if ti == 15:
                    D(lambda e: e.dma_start(out=o_retp[l], in_=Sst[:]), r=SSTN, w=["o_retp"], sem="st_Sst")

            ck(6)
            ti = 16
            D(lambda e: e.dma_start(out=S0, in_=retS[l]), w=["S0"], sem="ld_S0", after=["kcTb", "vcb", "vcb1"])
            D(lambda e: e.dma_start(out=extu[:, :, 0:480], in_=confT[l]), w=["extu_halo"], sem="ld_extu", after=["extu_new"])
            D(lambda e: e.dma_start(out=extv[:, :, 0:32], in_=scT[l]), w=["extv_halo"], sem="ld_extv", after=["extv_new"])
            emit_h(ti, l, 0)
            evac_hT(ti, hT[:], "hT")
            emit_z([0, 1, 2, 3, 4, 5])
            A(lambda e: e.activation(out=S0b, in_=S0, func=AF.Copy), r=["S0"], w=EEPP)
            emit_local_ret(ti)
            emit_local_rest(ti, 0)
            emit_transposes(ti, 0)
            ck(6.1)
            for a_ in (range(2) if "trs" not in os.environ.get("KSKIP", "") else []):
                V(lambda e, a_=a_: e.tensor_tensor(out=trs[:, a_, :].rearrange("p (s t) -> p s t", s=16), in0=tr[:, a_, :].rearrange("p (t s) -> p s t", t=8),
                                                   in1=cst["qdecS"][:, a_ * 128:(a_ + 1) * 128].rearrange("p (s t) -> p s t", s=16), op=ALU.mult), r=["tr", "c_qdecS"], w=["trs"])
                V(lambda e, a_=a_: e.tensor_copy(out=trs[:, 2 + a_, :].rearrange("p (s t) -> p s t", s=16), in_=tr[:, 4 + a_, :].rearrange("p (t s) -> p s t", t=8)), r=["tr"], w=["trs"])
            for h in range(4):
                rs = slice((h % 2) * 64, (h % 2) * 64 + 64)
                bank, bn = (B7, "B7a") if h % 2 == 0 else (B3, "B3a")
                T(lambda e, h=h, rs=rs, bank=bank: e.matmul(bank[:, (h // 2) * 128:(h // 2 + 1) * 128], lhsT=tr[rs, 2 + h // 2, :], rhs=tr[rs, h // 2, :], start=True, stop=True), r=["tr"], w=[bn])
            attm4 = attm[:].rearrange("p (pr hf i) -> p pr hf i", pr=2, hf=2)
            mask4 = cst["maskS"][:].rearrange("p (pr hf i) -> p pr hf i", pr=2, hf=2)
            for hf, (bank, bn) in (enumerate([(B7, "B7a"), (B3, "B3a")]) if "attm" not in os.environ.get("KSKIP", "") else []):
                V(lambda e, hf=hf, bank=bank, attm4=attm4, mask4=mask4: e.tensor_tensor(out=attm4[:, :, hf, :], in0=bank[:, 0:256].rearrange("p (pr i) -> p pr i", pr=2), in1=mask4[:, :, hf, :], op=ALU.mult),
                  r=[bn, "c_maskS"], w=["attm"])
            for h in range(4):
                T(lambda e, h=h: e.matmul(B6[:, h * 64:(h + 1) * 64], lhsT=attm[:, h * 128:(h + 1) * 128], rhs=rb[:, 3, h * 64:(h + 1) * 64], start=True, stop=True), r=["attm", "rb3"], w=["o"])
            ck(6.11)
            for h in range(4):
                rs = slice((h % 2) * 64, (h % 2) * 64 + 64)
                bank, bn = (B4, "B4") if h % 2 == 0 else (B5, "B5")
                for s_ in range(16):
                    c0 = (h // 2) * 128 + s_ * 8
                    T(lambda e, h=h, s_=s_, rs=rs, bank=bank, c0=c0: e.matmul(bank[0:64, c0:c0 + 8], lhsT=S0b[rs, h // 2, s_, :], rhs=trs[rs, h // 2, s_ * 8:(s_ + 1) * 8], start=True, stop=True),
                      r=EEPP + ["trs"], w=[bn])
            ck(6.12)
            for h in range(4):
                bank, bn = (B4, "B4") if h % 2 == 0 else (B5, "B5")
                V(lambda e, h=h, bank=bank: e.tensor_copy(out=ocs[0:64, h, :].rearrange("p (t s) -> p t s", t=8), in_=bank[0:64, (h // 2) * 128:(h // 2 + 1) * 128].rearrange("p (s t) -> p t s", s=16)), r=[bn], w=["hT"])
            for h in range(4):
                T(lambda e, h=h: e.transpose(B3[:, 256 + h * 64:256 + (h + 1) * 64], ocs[0:64, h, :], identf[0:64, 0:64]), r=["hT", "identf"], w=["B3b"])
            osum = tmpf[:, 0:256]
            A(lambda e: e.activation(out=osum, in_=B3[:, 256:512], func=AF.Copy), r=["B3b"], w=["tmpf0"])
            V(lambda e: e.tensor_tensor(out=osum, in0=osum, in1=B6[:, 0:256], op=ALU.add), r=["tmpf0", "o"], w=["tmpf0"])
            ck(6.13)
            for h in range(4):
                rs = slice((h % 2) * 64, (h % 2) * 64 + 64)
                pr = h // 2
                V(lambda e, h=h: e.tensor_tensor(out=vbd, in0=rb[:, 3, h * 64:(h + 1) * 64][:, None, :].broadcast_to([128, 16, 64]),
                                                 in1=cst["onehot"][:][:, :, None].broadcast_to([128, 16, 64]), op=ALU.mult), r=["rb3", "c_onehot"], w=["acc0", "acc1", "accv0", "accv1"])
                for q2 in range(2):
                    bank = [B4, B5][q2]; bn = f"B{4 + q2}"
                    T(lambda e, q2=q2, bank=bank, pr=pr: e.matmul(bank[:], lhsT=rb[:, 2, pr * 128:(pr + 1) * 128], rhs=vbdf[:, q2 * 512:(q2 + 1) * 512], start=True, stop=True),
                      r=["rb2", "acc0", "acc1", "accv0", "accv1"], w=[bn])
                    V(lambda e, q2=q2, bank=bank, rs=rs, pr=pr: e.scalar_tensor_tensor(out=S0[rs, pr, q2 * 8:(q2 + 1) * 8, :].rearrange("p a b -> p (a b)"), in0=S0[rs, pr, q2 * 8:(q2 + 1) * 8, :].rearrange("p a b -> p (a b)"),
                                                                                       scalar=cst["cdecS"][rs, pr:pr + 1], in1=bank[rs, :], op0=ALU.mult, op1=ALU.add),
                      r=["S0", bn, "c_cdecS"], w=["S0"])
            ck(6.14)
            D(lambda e: e.dma_start(out=o_rets[l], in_=S0), r=["S0"], w=["o_rets"], sem="st_S0")
            ck(6.15)
            emit_groupnorm_out(osum, ["tmpf0"])
            ck(6.2)
            DG(lambda e: e.dma_start(out=kcTb, in_=kcT[l]), w=["kcTb"], sem="ldk", after=["S0"])
            G(lambda e: e.memset(samp[:, 2048:4160], 1.0), w=["vcb", "vcb1"], after=["S0"])
            lastk = DG(lambda e: e.dma_start(out=vcb[:, :, :, 0:64], in_=vc[l].rearrange("p s (k d) -> p s k d", k=2)), w=["vcb"], sem="ldk", after=["S0"])
            S.res["kcTb"]['w'] = lastk
            for kv, (bank, bn) in enumerate([(B4, "B4"), (B5, "B5")]):
                rs = slice(kv * 64, kv * 64 + 64)
                T(lambda e, bank=bank, rs=rs: e.matmul(bank[:, 0:256], lhsT=kTa[0][rs, :], rhs=trf[rs, 512:768], start=True, stop=True), r=["kTa0", "tr"], w=[bn])
                for s_ in range(16):
                    for g in range(2):
                        c0 = 256 + s_ * 16 + g * 8
                        T(lambda e, s_=s_, g=g, rs=rs, c0=c0, bank=bank: e.matmul(bank[:, c0:c0 + 8], lhsT=kcTb[rs, s_, :], rhs=trs[rs, 2 + g, s_ * 8:(s_ + 1) * 8], start=True, stop=True), r=["kcTb", "trs"], w=[bn])
                A(lambda e, bank=bank, kv=kv: e.activation(out=ee[:, kv * 512:(kv + 1) * 512], in_=bank[:], func=AF.Exp, scale=0.125), r=[bn], w=[f"ee{kv}"])
                V(lambda e, kv=kv: e.tensor_tensor(out=pp[:, kv * 512:(kv + 1) * 512], in0=ee[:, kv * 512:(kv + 1) * 512], in1=cst["Es"][:, kv * 512:(kv + 1) * 512], op=ALU.mult), r=[f"ee{kv}", "c_Es"], w=[f"pp{kv}"])
            for kv in range(2):
                for g in range(2):
                    hh = kv * 2 + g
                    c0 = kv * 512 + g * 128
                    T(lambda e, kv=kv, hh=hh, c0=c0: e.matmul(B7[:, hh * 80:hh * 80 + 66], lhsT=pp[:, c0:c0 + 128], rhs=vaug[0][:, kv, 0:66], start=True, stop=True),
                      r=[f"pp{kv}", "vaug0"], w=B7N)
            for s_ in range(16):
                for kv in range(2):
                    c0 = kv * 256 + s_ * 16
                    T(lambda e, s_=s_, kv=kv, c0=c0: e.matmul(B6[0:65, c0:c0 + 16], lhsT=vcb[:, s_, kv, 0:65], rhs=pp[:, kv * 512 + 256 + s_ * 16:kv * 512 + 256 + (s_ + 1) * 16], start=True, stop=True),
                      r=["vcb", "vcb1", f"pp{kv}"], w=["o", "SU"])
            for k_ in range(2):
                for g_ in range(2):
                    V(lambda e, k_=k_, g_=g_: e.tensor_copy(out=ocs[0:65, k_ * 2 + g_, :].rearrange("p (t s) -> p t s", t=8),
                                                             in_=B6[0:65, k_ * 256:(k_ + 1) * 256].rearrange("p (s g t) -> p g t s", s=16, g=2)[:, g_, :, :]), r=["o", "SU"], w=["hT"])
            for hh in range(4):
                T(lambda e, hh=hh: e.transpose(B3[:, 256 + hh * 64:256 + (hh + 1) * 64], ocs[0:64, hh, :], identf[0:64, 0:64]), r=["hT", "identf"], w=["B3b"])
            for hh in range(4):
                T(lambda e, hh=hh: e.transpose(B4[:, hh * 2:hh * 2 + 1], ocs[64:65, hh, :], identf[64:65, 64:65]), r=["hT", "identf"], w=["B4"])
            oas = tmpf[:, 512:832].rearrange("p (a b) -> p a b", a=4)
            A(lambda e: e.activation(out=oas, in_=B7[:, 0:320].rearrange("p (a b) -> p a b", a=4), func=AF.Copy), r=B7N, w=["tmpf1"])
            V(lambda e: e.tensor_tensor(out=oas[:, :, 0:64], in0=oas[:, :, 0:64], in1=B3[:, 256:512].rearrange("p (a b) -> p a b", a=4), op=ALU.add), r=["tmpf1", "B3b"], w=["tmpf1"])
            V(lambda e: e.tensor_tensor(out=oas[:, :, 64], in0=oas[:, :, 64], in1=B4[:, 0:8].rearrange("p (a b) -> p a b", a=4)[:, :, 0], op=ALU.add), r=["tmpf1", "B4"], w=["tmpf1"])
            emit_attn_finish(oas, ["tmpf1"])
            ck(6.3)
            emit_conv(ti)
            ck(6.4)
            D(lambda e: e.dma_start(out=o_ks[l][:, 0:120, :], in_=kc_o[l][:, 8:128, :]), w=["o_ks"], sem="st_d2d_k")
            D(lambda e: e.dma_start(out=o_vs[l][:, 0:120, :], in_=vc_o[l][:, 8:128, :]), w=["o_vs"], sem="st_d2d_v")
            D(lambda e: e.dma_start(out=o_confs[l][:, 0:22, :], in_=conf_o[l][:, 8:30, :]), w=["o_confs"], sem="st_d2d_c")
            for t in range(8):
                D(lambda e, t=t: e.dma_start(out=o_ks[l][:, 120 + t, :], in_=knf[t * 16:(t + 1) * 16, :]), r=["knf"], w=["o_ks"], sem="st_knf")
                D(lambda e, t=t: e.dma_start(out=o_vs[l][:, 120 + t, :], in_=z[t * 16:(t + 1) * 16, 1408:1536]), r=["z2"], w=["o_vs"], sem="st_z2")
                D(lambda e, t=t: e.dma_start(out=o_confs[l][:, 22 + t, :], in_=u[t * 16:(t + 1) * 16, :]), r=["u"], w=["o_confs"], sem="st_u")
                if t >= 6:
                    D(lambda e, t=t: e.dma_start(out=o_scs[l][:, t - 6, :], in_=vsf[t * 16:(t + 1) * 16, :]), r=["vsf"], w=["o_scs"], sem="st_vsf")
            D(lambda e: e.dma_start(out=gateS[:], in_=modS_d[l][0][:, 2048:3072]), r=["modS_d"], w=MIXF, sem="ld_mixf")
            emit_out(ti)

            ck(7)
            mod_phase(l, 1)
            D(lambda e: e.dma_start(out=gateS[:], in_=modS_d[l][1][:, 2048:3072]), r=["modS_d"], w=MIXF, sem="ld_mixf")
            blocks = [list(range(0, 8)), list(range(8, 17))]
            ubc = [0]
            for bi, tiles in enumerate(blocks):
                for i, ti in enumerate(tiles):
                    emit_h(ti, l, 1)
                    evac_hT(ti, h2T[:, :, i * 128:(i + 1) * 128], "h2T", after=MIX_NAMES)
                subs = [tiles[0:4], tiles[4:8]] + ([tiles[8:9]] if len(tiles) > 8 else [])
                for g8 in range(8):
                    b = g8 % 2
                    DG(lambda e, g8=g8, b=b: e.dma_start(out=W1g[b][:], in_=w_ff1[l][:, g8 * 512:(g8 + 1) * 512].rearrange("(k p) n -> p k n", p=128)),
                       w=[f"W1g{b}"], sem=f"ldf{b}", after=MIX_NAMES)
                    DG(lambda e, g8=g8, b=b: e.dma_start(out=W2g[b][:], in_=w_ff2[l][g8 * 512:(g8 + 1) * 512, :].rearrange("(k p) n -> p k n", p=128)),
                       w=[f"W2g{b}"], sem=f"ldg{b}", after=MIX_NAMES)
                    for si, sub in enumerate(subs):
                        n = len(sub) * 128
                        o0 = (sub[0] - tiles[0]) * 128
                        ub = ubc[0] % 2
                        ubc[0] += 1
                        for fc in range(4):
                            bank = [B0, B1, B3, B6][fc]
                            bn = [["B0"], ["B1"], B3N, ["o", "SU"]][fc]
                            tf = tmpf[:, (fc % 2) * 512:(fc % 2) * 512 + n]
                            tfn = f"tmpf{fc % 2}"
                            for k in range(8):
                                T(lambda e, k=k, fc=fc, bank=bank, b=b, o0=o0, n=n: e.matmul(bank[:, 0:n], lhsT=W1g[b][:, k, fc * 128:(fc + 1) * 128], rhs=h2T[:, k, o0:o0 + n], start=(k == 0), stop=(k == 7)),
                                  r=[f"W1g{b}", "h2T"], w=bn)
                            A(lambda e, bank=bank, n=n, tf=tf: e.activation(out=tf, in_=bank[:, 0:n], func=AF.Relu), r=bn, w=[tfn])
                            G(lambda e, fc=fc, ub=ub, n=n, tf=tf: e.tensor_tensor(out=uT[ub][:, fc, 0:n], in0=tf, in1=tf, op=ALU.mult), r=[tfn], w=[f"uT{ub}"], after=MIX_NAMES)
                        for i, ti in enumerate(sub):
                            gate = gateP if ti < 16 else gateS
                            gn = ["gateP"] if ti < 16 else MIXF
                            for hf in range(2):
                                bank = [B4, B5][hf]; bn = f"B{4 + hf}"
                                tn = ["ee0", "ee1"] if hf == 0 else ["pp0", "pp1"]
                                for fc in range(4):
                                    T(lambda e, fc=fc, bank=bank, hf=hf, i=i, ub=ub, b=b: e.matmul(bank[:], lhsT=uT[ub][:, fc, i * 128:(i + 1) * 128], rhs=W2g[b][:, fc, hf * 512:(hf + 1) * 512], start=(fc == 0), stop=(fc == 3)),
                                      r=[f"uT{ub}", f"W2g{b}"], w=[bn])
                                V(lambda e, bank=bank, hf=hf, gate=gate: e.tensor_tensor(out=t512[hf], in0=bank[:], in1=gate[:, hf * 512:(hf + 1) * 512], op=ALU.mult), r=[bn] + gn, w=tn)
                                G(lambda e, hf=hf, ti=ti: e.tensor_tensor(out=x[:, ti, hf * 512:(hf + 1) * 512], in0=x[:, ti, hf * 512:(hf + 1) * 512], in1=t512[hf], op=ALU.add),
                                  r=tn + [f"x{ti}"], w=[f"x{ti}"])
            ck(8)
            if l == DEPTH - 1:
                for ti in range(16):
                    D(lambda e, ti=ti: e.dma_start(out=yp[ti * 128:(ti + 1) * 128, :], in_=x[:, ti, :]), r=[f"x{ti}"], w=["yp"], sem="st_x")
                D(lambda e: e.dma_start(out=ys, in_=x[:, 16, :]), r=["x16"], w=["ys"], sem="st_x")
          for _l in range(DEPTH):
            emit_layer(_l)
        except _Stop:
            if os.environ.get("KDUMPX"):
                for ti in range(16):
                    D(lambda e, ti=ti: e.dma_start(out=yp[ti * 128:(ti + 1) * 128, :], in_=x[:, ti, :]), r=[f"x{ti}"], w=["yp"], sem="st_x")
            for _i in range(int(os.environ.get("KDUMMY", "0"))):
                A(lambda e: e.activation(out=st[:, 61:62], in_=st[:, 60:61], func=AF.Copy), r=["st60"], w=["st61"])
        S.wait_all("sp")
        S.emit(es)
    return nc


_NC = None


def _host_inputs(inp):
    f = lambda a: np.ascontiguousarray(np.asarray(a, dtype=np.float32))
    maps = []
    shared = {
        "ada_w": f(inp["ada_w"]), "ada_b": f(inp["ada_b"]).reshape(2, 1, 6144),
        "n1g": f(inp["norm1_g"]).reshape(2, 1, 1024), "n2g": f(inp["norm2_g"]).reshape(2, 1, 1024),
        "w_in": f(inp["w_in"]), "w_out": f(inp["w_out"]), "w_ff1": f(inp["w_ff1"]), "w_ff2": f(inp["w_ff2"]),
        "qng": f(inp["q_norm_g"]).reshape(2, 1, 64), "kng": f(inp["k_norm_g"]).reshape(2, 1, 64), "sinks": f(inp["attn_sinks"]).reshape(2, 1, 4),
        "cdwT": f(np.asarray(inp["conf_dw"]).reshape(2, 31, 2, 128).transpose(0, 3, 2, 1)),
        "sdwT": f(np.asarray(inp["sconv_dw"]).reshape(2, 3, 2, 128).transpose(0, 3, 2, 1)),
        "lng": f(inp["conf_ln_g"]).reshape(2, 1, 256), "lnb": f(inp["conf_ln_b"]).reshape(2, 1, 256),
    }
    xp = np.asarray(inp["x_prompt"]); xs = np.asarray(inp["x_sample"])
    for c in range(8):
        b, j = c // 4, c % 4
        sl = slice(16 * c, 16 * c + 16)
        m = dict(shared)
        m["xp"] = f(xp[b, j * 2048:(j + 1) * 2048])
        m["xs"] = f(xs[sl].transpose(1, 0, 2).reshape(128, 1024))
        m["cP"] = f(np.broadcast_to(np.asarray(inp["c_prompt"])[b][None, :], (128, 1024)))
        m["cS"] = f(np.broadcast_to(np.asarray(inp["c_sample"])[sl][None, :, :], (8, 16, 1024)).reshape(128, 1024))
        sr = np.asarray(inp["state_ret"])[:, sl]
        sr = sr.reshape(2, 16, 2, 2, 64, 64)
        m["retS"] = f(sr.transpose(0, 3, 4, 2, 1, 5).reshape(2, 128, 2, 16, 64))
        kc = np.asarray(inp["cache_swa_k"])[:, sl].reshape(2, 16, 128, 128)
        vcc = np.asarray(inp["cache_swa_v"])[:, sl].reshape(2, 16, 128, 128)
        m["kcT"] = f(kc.transpose(0, 3, 1, 2)); m["vc"] = f(vcc.transpose(0, 2, 1, 3))
        m["kc_o"] = f(kc); m["vc_o"] = f(vcc)
        cf = np.asarray(inp["state_conf"])[:, sl]
        m["confT"] = f(cf.reshape(2, 16, 30, 2, 128).transpose(0, 4, 3, 2, 1).reshape(2, 128, 2, 480))
        m["conf_o"] = f(cf)
        sc = np.asarray(inp["state_sconv"])[:, sl]
        m["scT"] = f(sc.reshape(2, 16, 2, 2, 128).transpose(0, 4, 3, 2, 1).reshape(2, 128, 2, 32))
        m["consts"] = _const_tables(c)
        maps.append(m)
    return maps


def kernel(**inp):
    global _NC
    if _NC is None:
        _NC = build()
    maps = _host_inputs(inp)
    res = run_bass_kernel_spmd(_NC, maps, core_ids=list(range(8)))
    R = res.results
    y_prompt = np.stack([np.concatenate([R[b * 4 + j]["yp"] for j in range(4)], 0) for b in range(2)])
    y_sample = np.concatenate([R[c]["ys"].reshape(8, 16, 1024).transpose(1, 0, 2) for c in range(8)], 0)

    def last(name):
        return [R[3][name], R[7][name]]
    ret_p = np.stack([r.reshape(2, 2, 64, 2, 64).transpose(0, 3, 1, 2, 4).reshape(2, 4, 64, 64) for r in last("o_retp")], 1)
    k_p = np.stack([r.reshape(2, 128, 2, 64) for r in last("o_kp")], 1)
    v_p = np.stack([r.reshape(2, 128, 2, 64) for r in last("o_vp")], 1)
    conf_p = np.stack(last("o_confp"), 1)
    sconv_p = np.stack(last("o_scp"), 1)
    ret_s = np.concatenate([R[c]["o_rets"].reshape(2, 2, 64, 2, 16, 64).transpose(0, 4, 3, 1, 2, 5).reshape(2, 16, 4, 64, 64) for c in range(8)], 1)
    k_s = np.concatenate([R[c]["o_ks"].reshape(2, 16, 128, 2, 64) for c in range(8)], 1)
    v_s = np.concatenate([R[c]["o_vs"].reshape(2, 16, 128, 2, 64) for c in range(8)], 1)
    conf_s = np.concatenate([R[c]["o_confs"] for c in range(8)], 1)
    sconv_s = np.concatenate([R[c]["o_scs"] for c in range(8)], 1)
    outs = (y_prompt, y_sample, ret_p, k_p, v_p, conf_p, sconv_p, ret_s, k_s, v_s, conf_s, sconv_s)
    return tuple(np.ascontiguousarray(o, dtype=np.float32) for o in outs)
```

```python
import contextlib
import numpy as np
import concourse.bass as bass
import concourse.mybir as mybir
from concourse.bass_utils import run_bass_kernel_spmd

F32 = mybir.dt.float32
BF = mybir.dt.bfloat16
AF = mybir.ActivationFunctionType
ALU = mybir.AluOpType
AX = mybir.AxisListType

ENGS = ("pe", "act", "dve", "pool", "sp")
EPOCH = 6000
EPS = 1e-6
NT = 17
DEPTH = 2


class Sched:
    def __init__(self, nc):
        self.nc = nc
        self.streams = {e: [] for e in ENGS}
        self.res = {}
        self.seen = {e: {} for e in ENGS}
        self.dma_count = {}
        self.flag = {e: set() for e in ENGS}

    def _need(self, eng, tok, waits):
        if tok is None:
            return
        if tok[0] == 'E':
            _, src, idx = tok
            if src == eng:
                if eng in ("pe", "sp"):
                    return
                if idx < len(self.streams[eng]) - 2:
                    return
            key = ('E', src)
        else:
            _, sem, idx = tok
            key = ('D', sem)
        if self.seen[eng].get(key, -1) >= idx:
            return
        waits[key] = max(waits.get(key, -1), idx)

    def op(self, eng, fn, reads=(), writes=(), dma_sem=None, inc=16, after=()):
        waits = {}
        for r in reads:
            st = self.res.get(r)
            if st is not None:
                self._need(eng, st['w'], waits)
        for w in list(writes) + list(after):
            st = self.res.get(w)
            if st is not None:
                self._need(eng, st['w'], waits)
                for t in st['r']:
                    self._need(eng, t, waits)
        wl = []
        for key, v in waits.items():
            self.seen[eng][key] = v
            if key[0] == 'E':
                self.flag[key[1]].add(v)
            wl.append((key, v))
        idx = len(self.streams[eng])
        if dma_sem is not None:
            c = self.dma_count.get(dma_sem, 0) + inc
            self.dma_count[dma_sem] = c
            tok = ('D', dma_sem, c)
        else:
            tok = ('E', eng, idx)
        self.streams[eng].append({'fn': fn, 'waits': wl, 'dma_sem': dma_sem, 'inc': inc})
        for r in reads:
            st = self.res.setdefault(r, {'w': None, 'r': []})
            st['r'].append(tok)
            if len(st['r']) > 64:
                st['r'] = _compact(st['r'])
        for w in writes:
            self.res[w] = {'w': tok, 'r': []}
        return tok

    def wait_all(self, eng):
        waits = {}
        for r, st in self.res.items():
            self._need(eng, st['w'], waits)
            for t in st['r']:
                self._need(eng, t, waits)
        wl = []
        for key, v in waits.items():
            self.seen[eng][key] = v
            if key[0] == 'E':
                self.flag[key[1]].add(v)
            wl.append((key, v))
        self.streams[eng].append({'fn': None, 'waits': wl, 'dma_sem': None, 'inc': 0})

    def emit(self, es):
        nc = self.nc
        val = {}
        nep = {}
        for e in ENGS:
            c = 0
            for i in range(len(self.streams[e])):
                if i in self.flag[e]:
                    val[(e, i)] = (c // EPOCH, c % EPOCH + 1)
                    c += 1
            nep[e] = (c + EPOCH - 1) // EPOCH
        esem = {}
        for e in ENGS:
            for k in range(nep[e]):
                esem[(e, k)] = es.enter_context(nc.semaphore(f"s_{e}_{k}"))
        dsem = {}
        for s in self.dma_count:
            dsem[s] = es.enter_context(nc.semaphore(f"d_{s}"))
        block = es.enter_context(nc.Block())
        engmap = {"pe": block.tensor, "act": block.scalar, "dve": block.vector,
                  "pool": block.gpsimd, "sp": block.sync}

        def mk(e):
            def body(eng):
                for i, o in enumerate(self.streams[e]):
                    for key, v in o['waits']:
                        if key[0] == 'E':
                            ep, vv = val[(key[1], v)]
                            eng.wait_ge(esem[(key[1], ep)], vv)
                        else:
                            eng.wait_ge(dsem[key[1]], v)
                    if o['fn'] is None:
                        continue
                    ins = o['fn'](eng)
                    if o['dma_sem'] is not None:
                        ins.then_inc(dsem[o['dma_sem']], o['inc'])
                    elif (e, i) in val:
                        ep, vv = val[(e, i)]
                        ins.then_inc(esem[(e, ep)], 1)
            return body
        for e in ENGS:
            if self.streams[e]:
                engmap[e](mk(e))


def _compact(toks):
    best = {}
    for t in toks:
        k = (t[0], t[1])
        if k not in best or best[k][2] < t[2]:
            best[k] = t
    return list(best.values())


C_OFF = {}


def _const_tables(core):
    j = core % 4
    gam = 1.0 - np.exp2(-5.0 - np.arange(4, dtype=np.float64))
    lg = np.log(gam)
    slopes = np.exp2(-8.0 * (np.arange(4, dtype=np.float64) + 1.0) / 4)
    parts = []

    def add(name, arr):
        arr = np.asarray(arr, np.float64).reshape(128, -1)
        C_OFF[name] = (sum(p.shape[1] for p in parts), arr.shape[1])
        parts.append(arr)

    i = np.arange(128)
    m = np.zeros((128, 4, 128))
    for h in range(4):
        d = i[None, :] - i[:, None]
        m[:, h, :] = np.where(d >= 0, np.exp(np.maximum(d, 0) * lg[h]), 0.0)
    add("maskP", m)
    q = np.zeros((128, 2, 128))
    for h in range(4):
        q[(h % 2) * 64:(h % 2) * 64 + 64, h // 2, :] = np.exp((i + 1.0) * lg[h])[None, :]
    add("qdecP", q)
    k = np.zeros((128, 4, 64))
    for h in range(4):
        k[:, h, :] = (0.125 * np.exp((127.0 - i) * lg[h]))[:, None]
    add("kdecP", k)
    c = np.zeros((128, 2))
    for h in range(4):
        c[(h % 2) * 64:(h % 2) * 64 + 64, h // 2] = np.exp(128.0 * lg[h])
    add("cdecP", c)
    t_ = i // 16
    s_ = i % 16
    m = np.zeros((128, 4, 128))
    same = (s_[:, None] == s_[None, :])
    for h in range(4):
        d = t_[None, :] - t_[:, None]
        m[:, h, :] = np.where(same & (d >= 0), np.exp(np.maximum(d, 0) * lg[h]), 0.0)
    add("maskS", m)
    q = np.zeros((128, 2, 128))
    tt = np.arange(128) % 8
    for h in range(4):
        q[(h % 2) * 64:(h % 2) * 64 + 64, h // 2, :] = np.exp((tt + 1.0) * lg[h])[None, :]
    add("qdecS", q)
    k = np.zeros((128, 4, 64))
    for h in range(4):
        k[:, h, :] = (0.125 * np.exp((7.0 - t_) * lg[h]))[:, None]
    add("kdecS", k)
    c = np.zeros((128, 2))
    for h in range(4):
        c[(h % 2) * 64:(h % 2) * 64 + 64, h // 2] = np.exp(8.0 * lg[h])
    add("cdecS", c)
    oh = np.zeros((128, 16))
    oh[i, s_] = 1.0
    add("onehot", oh)
    E = np.zeros((128, 2, 2, 2, 128))
    for kb in range(2):
        jk = kb * 128 + i[:, None]
        iq = 128 + i[None, :]
        dist = iq - jk
        valid = (dist >= 0) & (dist <= 128)
        for kv in range(2):
            for g in range(2):
                E[:, kv, kb, g, :] = np.where(valid, np.exp(-slopes[kv * 2 + g] * dist), 0.0)
    add("Ep", E)
    Es = np.zeros((128, 2, 512))
    d = t_[None, :] - t_[:, None]
    valid = same & (d >= 0)
    for kv in range(2):
        for g in range(2):
            Es[:, kv, g * 128:(g + 1) * 128] = np.where(valid, np.exp(-slopes[kv * 2 + g] * d), 0.0)
            for t in range(8):
                dist = 128 + t - i
                vc_ = (i >= t)
                val = np.where(vc_, np.exp(-slopes[kv * 2 + g] * dist), 0.0)
                for s_i in range(16):
                    Es[:, kv, 256 + s_i * 16 + g * 8 + t] = val
    add("Es", Es)
    sel = np.zeros((128, 4))
    if j > 0:
        sel[:, j - 1] = 1.0
    add("sel", sel)
    cs = np.zeros((128, 4, 2))
    for r in range(4):
        if r < j:
            for h in range(4):
                cs[(h % 2) * 64:(h % 2) * 64 + 64, r, h // 2] = np.exp(128.0 * 16 * (j - 1 - r) * lg[h])
    add("coefS", cs)
    add("hasprev", np.full((128, 1), 1.0 if j > 0 else 0.0))
    add("ident", np.eye(128))
    return np.concatenate(parts, 1).astype(np.float32)


_const_tables(0)
NCONST = sum(v[1] for v in C_OFF.values())

G_S, G_KV, G_CF, G_SC, G_ROWS = 0, 128, 256, 286, 288


def build():
    nc = bass.Bass("TRN2", target_bir_lowering=False)

    def din(name, shape):
        return nc.dram_tensor(name, list(shape), F32, kind="ExternalInput").ap()

    def dout(name, shape):
        return nc.dram_tensor(name, list(shape), F32, kind="ExternalOutput").ap()

    xp = din("xp", [2048, 1024]); xs = din("xs", [128, 1024])
    cP = din("cP", [128, 1024]); cS = din("cS", [128, 1024])
    ada_w = din("ada_w", [2, 1024, 6144]); ada_b = din("ada_b", [2, 1, 6144])
    n1g = din("n1g", [2, 1, 1024]); n2g = din("n2g", [2, 1, 1024])
    w_in = din("w_in", [2, 1024, 2816]); w_out = din("w_out", [2, 1024, 1024])
    w_ff1 = din("w_ff1", [2, 1024, 4096]); w_ff2 = din("w_ff2", [2, 4096, 1024])
    qng = din("qng", [2, 1, 64]); kng = din("kng", [2, 1, 64]); sinks = din("sinks", [2, 1, 4])
    cdwT = din("cdwT", [2, 128, 2, 31]); sdwT = din("sdwT", [2, 128, 2, 3])
    lng = din("lng", [2, 1, 256]); lnb = din("lnb", [2, 1, 256])
    retS = din("retS", [2, 128, 2, 16, 64])
    kcT = din("kcT", [2, 128, 16, 128]); vc = din("vc", [2, 128, 16, 128])
    kc_o = din("kc_o", [2, 16, 128, 128]); vc_o = din("vc_o", [2, 16, 128, 128])
    confT = din("confT", [2, 128, 2, 480]); conf_o = din("conf_o", [2, 16, 30, 256])
    scT = din("scT", [2, 128, 2, 32])
    consts = din("consts", [128, NCONST])

    yp = dout("yp", [2048, 1024]); ys = dout("ys", [128, 1024])
    o_retp = dout("o_retp", [2, 128, 2, 64]); o_kp = dout("o_kp", [2, 128, 128]); o_vp = dout("o_vp", [2, 128, 128])
    o_confp = dout("o_confp", [2, 30, 256]); o_scp = dout("o_scp", [2, 2, 256])
    o_rets = dout("o_rets", [2, 128, 2, 16, 64]); o_ks = dout("o_ks", [2, 16, 128, 128]); o_vs = dout("o_vs", [2, 16, 128, 128])
    o_confs = dout("o_confs", [2, 16, 30, 256]); o_scs = dout("o_scs", [2, 16, 2, 256])

    gin = [nc.dram_tensor(f"gin{l}", [G_ROWS, 256], F32, kind="Internal").ap() for l in range(2)]
    gout = [nc.dram_tensor(f"gout{l}", [4 * G_ROWS, 256], F32, kind="Internal").ap() for l in range(2)]

    S = Sched(nc)
    es = contextlib.ExitStack()
    with es:
        def sb(name, shape, dt=F32):
            return es.enter_context(nc.sbuf_tensor(name, list(shape), dt))

        def ps(name, shape, dt=F32):
            return es.enter_context(nc.psum_tensor(name, list(shape), dt))

        def V(fn, r=(), w=(), **k): return S.op("dve", fn, r, w, **k)
        def A(fn, r=(), w=(), **k): return S.op("act", fn, r, w, **k)
        def G(fn, r=(), w=(), **k): return S.op("pool", fn, r, w, **k)
        def T(fn, r=(), w=(), **k): return S.op("pe", fn, r, w, **k)
        def D(fn, r=(), w=(), sem="ld", **k): return S.op("sp", fn, r, w, dma_sem=sem, **k)
        def DG(fn, r=(), w=(), sem="ldg", **k): return S.op("pool", fn, r, w, dma_sem=sem, **k)

        x = sb("x", [128, NT, 1024])
        gateP = sb("gateP", [128, 1024])
        modPT = sb("modPT", [128, 16])
        R = sb("R", [128, 36352], BF)
        win = R[:, 0:22528].rearrange("p (k n) -> p k n", k=8)
        wout = R[:, 22528:30720].rearrange("p (k n) -> p k n", k=8)
        z = R[:, 30720:36352].bitcast(F32)
        h2T = R[:, 0:9216].rearrange("p (k n) -> p k n", k=8)
        W1g = [R[:, 9216 + i * 4096: 9216 + (i + 1) * 4096].rearrange("p (k n) -> p k n", k=8) for i in range(2)]
        W2g = [R[:, 17408 + i * 4096: 17408 + (i + 1) * 4096].rearrange("p (k n) -> p k n", k=4) for i in range(2)]
        uT = [R[:, 25600 + i * 2048: 25600 + (i + 1) * 2048].rearrange("p (k n) -> p k n", k=4) for i in range(2)]
        adaw = W1g
        MIX_NAMES = [f"win_{n}" for n in range(6)] + ["wout_0", "wout_1"] + [f"z{n}" for n in range(6)]
        FFN_NAMES = ["h2T", "W1g0", "W1g1", "W2g0", "W2g1", "uT0", "uT1"]

        identf = sb("identf", [128, 128]); identb = sb("identb", [128, 128], BF)
        cst = {}
        for nm, dt in [("maskP", F32), ("qdecP", BF), ("kdecP", F32), ("cdecP", F32), ("maskS", F32), ("qdecS", BF),
                       ("kdecS", F32), ("cdecS", F32), ("onehot", BF), ("Ep", BF), ("Es", BF),
                       ("sel", F32), ("coefS", F32), ("hasprev", F32)]:
            cst[nm] = sb("c_" + nm, [128, C_OFF[nm][1]], dt)
        cPT = sb("cPT", [128, 8, 128], BF); cST = sb("cST", [128, 8, 128], BF)
        adab = sb("adab", [1, 512]); ones1 = sb("ones1", [1, 128])
        gqk = sb("gqk", [128, 128])
        esink = sb("esink", [128, 4])
        lngb = sb("lngb", [128, 512])
        cw = sb("cw", [128, 2, 31]); sw = sb("sw", [128, 2, 3])
        hb = sb("hb", [128, 1024], BF)
        hT = sb("hT", [128, 8, 128], BF)
        ocs = hT[:].rearrange("p a b -> p (a b)").bitcast(F32).rearrange("p (a b) -> p a b", a=4)
        tmpf = sb("tmpf", [128, 1024])
        gsel = tmpf[:].rearrange("p (a b) -> p a b", a=4)
        st = sb("st", [128, 64])
        tr = sb("tr", [128, 6, 128], BF)
        trf = tr[:].rearrange("p a b -> p (a b)")
        qdT = sb("qdT", [128, 2, 128], BF)
        kTa = [sb(f"kTa{i}", [128, 128], BF) for i in range(2)]
        vaug = [sb(f"vaug{i}", [128, 2, 66], BF) for i in range(2)]
        rb = sb("rb", [128, 4, 256], BF)
        attm = sb("attm", [128, 512], BF)
        qn = sb("qn", [128, 256], BF)
        knf = sb("knf", [128, 128]); knb = sb("knb", [128, 128], BF)
        eepp = sb("eepp", [128, 2048], BF)
        ee = eepp[:, 0:1024]; pp = eepp[:, 1024:2048]
        EEPP = ["ee0", "ee1", "pp0", "pp1"]
        ef32 = eepp[:].bitcast(F32)
        t512 = [ef32[:, 0:512], ef32[:, 512:1024]]
        gb = ef32
        S0b = eepp[:].rearrange("p (a s e) -> p a s e", a=2, s=16)
        mixf = sb("mixf", [128, 1024])
        sg = mixf[:, 0:256]; u = mixf[:, 256:512]; vsf = mixf[:, 512:768]; t256 = mixf[:, 768:1024]
        gateS = mixf
        MIXF = ["sg", "u", "vsf", "_sa"]
        t256b = sb("t256b", [128, 256])
        extu = sb("extu", [128, 2, 608]); extv = sb("extv", [128, 2, 160])
        accs = sb("accs", [128, 4, 128])
        acc = accs[:, 0:2, :]; accv = accs[:, 2:4, :]
        vbdf = accs[:].rearrange("p a b -> p (a b)").bitcast(BF)
        vbd = vbdf.rearrange("p (s e) -> p s e", s=16)
        Sst = sb("Sst", [128, 2, 64]); Sbf = sb("Sbf", [128, 2, 64], BF)
        samp = sb("samp", [128, 4160], BF)
        S0 = samp[:, 0:4096].bitcast(F32).rearrange("p (a s e) -> p a s e", a=2, s=16)
        kcTb = samp[:, 0:2048].rearrange("p (s t) -> p s t", s=16)
        vcb = samp[:, 2048:4160].rearrange("p (s k e) -> p s k e", s=16, k=2)
        trs = sb("trs", [128, 4, 128], BF)
        modS_d = [[nc.dram_tensor(f"modS_{l}_{h}", [128, 3072], F32, kind="Internal").ap() for h in range(2)] for l in range(2)]

        B0 = ps("B0", [128, 512]); B1 = ps("B1", [128, 512])
        B2 = ps("B2", [128, 1024], BF)
        B3 = ps("B3", [128, 512])
        B4 = ps("B4", [128, 512]); B5 = ps("B5", [128, 512])
        B6 = ps("B6", [128, 512]); B7 = ps("B7", [128, 512])
        ZB = [B0, B1]
        B3N = ["B3a", "B3b"]; B7N = ["B7a", "B7b"]

        def cv(name):
            return cst[name]

        TF = ["tmpf0", "tmpf1"]
        SSTN = ["Sst00", "Sst01", "Sst10", "Sst11"]
        HB = ["hb", "cat0", "cat1", "cat2", "cat3"]
        last = None
        for nm in cst:
            o, n = C_OFF[nm]
            if cst[nm].dtype == BF:
                last = DG(lambda e, nm=nm, o=o, n=n: e.dma_start(out=cst[nm][:], in_=consts[:, o:o + n], allow_slow_non_contiguous=(n == 1)), w=["c_" + nm], sem="ldc")
            else:
                last = DG(lambda e, nm=nm, o=o, n=n: e.dma_start(out=cst[nm][:], in_=consts[:, o:o + n], allow_slow_non_contiguous=(n == 1)), w=["c_" + nm], sem="ldc")
        for nm in cst:
            S.res["c_" + nm]['w'] = last
        o_id, n_id = C_OFF["ident"]
        D(lambda e: e.dma_start(out=identf[:], in_=consts[:, o_id:o_id + n_id]), w=["identf"], sem="ld_identf")
        V(lambda e: e.tensor_copy(out=identb[:], in_=identf[:]), r=["identf"], w=["identb"])
        G(lambda e: e.memset(ones1[:], 1.0), w=["ones1"])
        for i in range(2):
            G(lambda e, i=i: e.memset(vaug[i][:], 1.0), w=[f"vaug{i}"])
        lastx = None
        for ti in range(16):
            lastx = D(lambda e, ti=ti: e.dma_start(out=x[:, ti, :], in_=xp[ti * 128:(ti + 1) * 128, :]), w=[f"x{ti}"], sem="ldx")
        lastx = D(lambda e: e.dma_start(out=x[:, 16, :], in_=xs), w=["x16"], sem="ldx")
        for ti in range(17):
            S.res[f"x{ti}"]['w'] = lastx

        for (cd, cT, nm) in [(cP, cPT, "cPT"), (cS, cST, "cST")]:
            D(lambda e, cd=cd: e.dma_start(out=tmpf[:], in_=cd), w=TF, sem="ld_tmpf")
            A(lambda e: e.activation(out=ef32[:], in_=tmpf[:], func=AF.Exp, scale=-1.0), r=TF, w=EEPP)
            G(lambda e: e.tensor_scalar_add(out=ef32[:], in0=ef32[:], scalar1=1.0), r=EEPP, w=EEPP)
            V(lambda e: e.reciprocal(out=mixf[:], in_=ef32[:]), r=EEPP, w=MIXF)
            V(lambda e: e.tensor_tensor(out=hb[:], in0=tmpf[:], in1=mixf[:], op=ALU.mult), r=TF + MIXF, w=["hb"])
            for k in range(8):
                T(lambda e, k=k: e.transpose(B2[:, k * 128:(k + 1) * 128], hb[:, k * 128:(k + 1) * 128], identb[:]), r=["hb", "identb"], w=["B2"])
            A(lambda e, cT=cT: e.activation(out=cT[:].rearrange("p k n -> p (k n)"), in_=B2[:], func=AF.Copy), r=["B2"], w=[nm])

        def rsqrt_small(dst, src, scale, rn, wn):
            A(lambda e: e.activation(out=dst, in_=src, func=AF.Ln, scale=scale, bias=EPS), r=rn, w=wn)
            A(lambda e: e.activation(out=dst, in_=dst, func=AF.Exp, scale=-0.5), r=wn, w=wn)

        def mod_phase(l, half):
            g_d = n1g if half == 0 else n2g
            D(lambda e: e.dma_start(out=gb[:], in_=g_d[l].partition_broadcast(128)), w=EEPP, sem="ld_gb")
            for n6 in range(6):
                n = half * 6 + n6
                b = n6 % 2
                DG(lambda e, n=n, b=b: e.dma_start(out=adaw[b][:], in_=ada_w[l][:, n * 512:(n + 1) * 512].rearrange("(k p) n -> p k n", p=128)),
                   w=[f"W1g{b}"], sem=f"ldf{b}", after=MIX_NAMES)
                D(lambda e, n=n: e.dma_start(out=adab[:], in_=ada_b[l][:, n * 512:(n + 1) * 512]), w=["adab"], sem="ld_adab")
                kind = n6 // 2
                c0 = (n6 % 2) * 512
                for gi, (cT, cn) in enumerate([(cPT, "cPT"), (cST, "cST")]):
                    bank = ZB[gi]
                    bn = f"B{gi}"
                    for k in range(8):
                        T(lambda e, k=k, cT=cT, bank=bank, b=b: e.matmul(bank[:], lhsT=cT[:, k, :], rhs=adaw[b][:, k, :], start=(k == 0), stop=False),
                          r=[cn, f"W1g{b}"], w=[bn])
                    T(lambda e, bank=bank: e.matmul(bank[:], lhsT=ones1[:], rhs=adab[:], start=False, stop=True), r=["ones1", "adab"], w=[bn])
                    if gi == 1:
                        stg = tmpf[:, 512:1024]
                        if kind == 1:
                            V(lambda e, bank=bank, c0=c0, stg=stg: e.scalar_tensor_tensor(out=stg, in0=bank[:], scalar=1.0, in1=gb[:, c0:c0 + 512], op0=ALU.add, op1=ALU.mult),
                              r=[bn] + EEPP, w=["tmpf1"])
                        else:
                            A(lambda e, bank=bank, stg=stg: e.activation(out=stg, in_=bank[:], func=AF.Copy), r=[bn], w=["tmpf1"])
                        D(lambda e, c0=c0, kind=kind, stg=stg: e.dma_start(out=modS_d[l][half][:, kind * 1024 + c0:kind * 1024 + c0 + 512], in_=stg), r=["tmpf1"], w=["modS_d"], sem="st_mod")
                    else:
                        if kind == 2:
                            A(lambda e, bank=bank, c0=c0: e.activation(out=gateP[:, c0:c0 + 512], in_=bank[:], func=AF.Copy), r=[bn], w=["gateP"])
                        else:
                            if kind == 1:
                                V(lambda e, bank=bank, c0=c0: e.scalar_tensor_tensor(out=tmpf[:, 0:512], in0=bank[:], scalar=1.0, in1=gb[:, c0:c0 + 512], op0=ALU.add, op1=ALU.mult),
                                  r=[bn] + EEPP, w=["tmpf0"])
                            else:
                                A(lambda e, bank=bank: e.activation(out=tmpf[:, 0:512], in_=bank[:], func=AF.Copy), r=[bn], w=["tmpf0"])
                            for q4 in range(4):
                                T(lambda e, q4=q4: e.transpose(B3[:, q4 * 128:(q4 + 1) * 128], tmpf[:, q4 * 128:(q4 + 1) * 128], identf[:]), r=["tmpf0", "identf"], w=B3N)
                            col = kind * 8 + (n6 % 2) * 4
                            V(lambda e, col=col: e.tensor_copy(out=modPT[:, col:col + 4], in_=B3[:].rearrange("p (a b) -> p a b", a=4)[:, :, 0]), r=B3N, w=["modPT"])

        def emit_h(ti, l, half):
            xn = f"x{ti}"
            A(lambda e: e.activation(out=hb[:], in_=x[:, ti, :], func=AF.Square, accum_out=st[:, 0:1]), r=[xn], w=HB + ["st0"])
            rsqrt_small(st[:, 1:2], st[:, 0:1], 1.0 / 1024, ["st0"], ["st1"])
            if ti < 16:
                V(lambda e: e.tensor_scalar(out=hb[:], in0=x[:, ti, :], scalar1=st[:, 1:2], scalar2=None, op0=ALU.mult), r=[xn, "st1"], w=HB)
            else:
                D(lambda e: e.dma_start(out=tmpf[:], in_=modS_d[l][half][:, 1024:2048]), r=["modS_d"], w=TF, sem="ld_tmpf")
                D(lambda e: e.dma_start(out=ef32[:], in_=modS_d[l][half][:, 0:1024]), r=["modS_d"], w=EEPP, sem="ld_gb")
                V(lambda e: e.scalar_tensor_tensor(out=tmpf[:], in0=x[:, ti, :], scalar=st[:, 1:2], in1=tmpf[:], op0=ALU.mult, op1=ALU.mult),
                  r=[xn, "st1"] + TF, w=TF)
                G(lambda e: e.tensor_tensor(out=hb[:], in0=tmpf[:], in1=ef32[:], op=ALU.add), r=TF + EEPP, w=HB)
            for k in range(8):
                T(lambda e, k=k: e.transpose(B2[:, k * 128:(k + 1) * 128], hb[:, k * 128:(k + 1) * 128], identb[:]), r=["hb", "identb"], w=["B2"])

        def evac_hT(ti, dst, dname, after=()):
            if ti < 16:
                for k in range(8):
                    if k % 2 == 0:
                        A(lambda e, k=k: e.activation(out=dst[:, k, :], in_=B2[:, k * 128:(k + 1) * 128], func=AF.Identity, scale=modPT[:, 8 + k:9 + k], bias=modPT[:, k:k + 1]),
                          r=["B2", "modPT"], w=[dname], after=after)
                    else:
                        V(lambda e, k=k: e.tensor_scalar(out=dst[:, k, :], in0=B2[:, k * 128:(k + 1) * 128], scalar1=modPT[:, 8 + k:9 + k], scalar2=modPT[:, k:k + 1], op0=ALU.mult, op1=ALU.add),
                          r=["B2", "modPT"], w=[dname], after=after)
            else:
                A(lambda e: e.activation(out=dst, in_=B2[:].rearrange("p (k n) -> p k n", k=8), func=AF.Copy), r=["B2"], w=[dname], after=after)

        def emit_z(chunks):
            for n in chunks:
                c0 = n * 512
                w_ = min(512, 2816 - c0)
                bank = ZB[n % 2]; bn = f"B{n % 2}"
                for k in range(8):
                    T(lambda e, k=k, bank=bank, c0=c0, w_=w_: e.matmul(bank[:, 0:w_], lhsT=hT[:, k, :], rhs=win[:, k, c0:c0 + w_], start=(k == 0), stop=(k == 7)),
                      r=["hT", f"win_{n}"], w=[bn])
                A(lambda e, bank=bank, c0=c0, w_=w_: e.activation(out=z[:, c0:c0 + w_], in_=bank[:, 0:w_], func=AF.Copy), r=[bn], w=[f"z{n}"], after=FFN_NAMES)

        def sigmoid_parts(src, rn):
            A(lambda e: e.activation(out=t256, in_=src, func=AF.Exp, scale=-1.0), r=rn, w=["_sa"])
            V(lambda e: e.tensor_scalar_add(out=t256, in0=t256, scalar1=1.0), r=["_sa"], w=["_sa"])
            V(lambda e: e.reciprocal(out=t256b[:], in_=t256), r=["_sa"], w=["_sb"])

        def emit_local_ret(ti):
            kd_c = cst["kdecP"] if ti < 16 else cst["kdecS"]
            kdn = "c_kdecP" if ti < 16 else "c_kdecS"
            A(lambda e: e.activation(out=rb[:, 0, :], in_=z[:, 0:256], func=AF.Copy), r=["z0"], w=["rb0"])
            A(lambda e: e.activation(out=rb[:, 1, :], in_=z[:, 256:512], func=AF.Copy, scale=0.125), r=["z0"], w=["rb1"])
            V(lambda e: e.tensor_tensor(out=rb[:, 2, :], in0=z[:, 256:512], in1=kd_c[:], op=ALU.mult), r=["z0", kdn], w=["rb2"])
            A(lambda e: e.activation(out=rb[:, 3, :], in_=z[:, 512:768], func=AF.Copy), r=["z1"], w=["rb3"])

        def emit_su():
            for pr in range(2):
                T(lambda e, pr=pr: e.matmul(B6[:, 256 + pr * 128:256 + (pr + 1) * 128], lhsT=rb[:, 2, pr * 128:(pr + 1) * 128], rhs=rb[:, 3, pr * 128:(pr + 1) * 128], start=True, stop=True),
                  r=["rb2", "rb3"], w=["SU"])

        def state_update(cdec, cn):
            for pr in range(2):
                for hf in range(2):
                    rs = slice(hf * 64, hf * 64 + 64)
                    V(lambda e, pr=pr, hf=hf, rs=rs: e.scalar_tensor_tensor(out=Sst[rs, pr, :], in0=Sst[rs, pr, :], scalar=cdec[rs, pr:pr + 1],
                                                                                in1=B6[rs, 256 + pr * 128 + hf * 64:256 + pr * 128 + hf * 64 + 64], op0=ALU.mult, op1=ALU.add),
                      r=[f"Sst{pr}{hf}", "SU", cn], w=[f"Sst{pr}{hf}"])
            A(lambda e: e.activation(out=Sbf[:], in_=Sst[:], func=AF.Copy), r=SSTN, w=["Sbf"])

        def emit_local_rest(ti, cur):
            sigmoid_parts(z[:, 768:1024], ["z1"])
            V(lambda e: e.tensor_tensor(out=sg, in0=z[:, 768:1024], in1=t256b[:], op=ALU.mult), r=["z1", "_sb"], w=["sg"])
            V(lambda e: e.tensor_tensor(out=tmpf[:, 0:384], in0=z[:, 1024:1408], in1=z[:, 1024:1408], op=ALU.mult), r=["z2"], w=["tmpf0"])
            V(lambda e: e.tensor_reduce(out=st[:, 8:14], in_=tmpf[:, 0:384].rearrange("p (a b) -> p a b", a=6), axis=AX.X, op=ALU.add), r=["tmpf0"], w=["st8"])
            rsqrt_small(st[:, 8:14], st[:, 8:14], 1.0 / 64, ["st8"], ["st8"])
            V(lambda e: e.tensor_tensor(out=tmpf[:, 0:256].rearrange("p (a b) -> p a b", a=4), in0=z[:, 1024:1280].rearrange("p (a b) -> p a b", a=4),
                                        in1=st[:, 8:12][:, :, None].broadcast_to([128, 4, 64]), op=ALU.mult), r=["z2", "st8"], w=["tmpf0"])
            for k2 in range(2):
                for g2 in range(2):
                    hh = k2 * 2 + g2
                    V(lambda e, k2=k2, g2=g2, hh=hh: e.tensor_tensor(out=qn[:, g2 * 128 + k2 * 64:g2 * 128 + k2 * 64 + 64], in0=tmpf[:, hh * 64:(hh + 1) * 64], in1=gqk[:, 0:64], op=ALU.mult),
                      r=["tmpf0", "gqk"], w=["qn"])
            V(lambda e: e.tensor_tensor(out=tmpf[:, 256:384].rearrange("p (a b) -> p a b", a=2), in0=z[:, 1280:1408].rearrange("p (a b) -> p a b", a=2),
                                        in1=st[:, 12:14][:, :, None].broadcast_to([128, 2, 64]), op=ALU.mult), r=["z2", "st8"], w=["tmpf0"])
            for k2 in range(2):
                V(lambda e, k2=k2: e.tensor_tensor(out=knf[:, k2 * 64:(k2 + 1) * 64], in0=tmpf[:, 256 + k2 * 64:256 + (k2 + 1) * 64], in1=gqk[:, 64:128], op=ALU.mult), r=["tmpf0", "gqk"], w=["knf"])
            A(lambda e: e.activation(out=knb[:], in_=knf[:], func=AF.Copy), r=["knf"], w=["knb"])
            A(lambda e: e.activation(out=vaug[cur][:, :, 0:64], in_=z[:, 1408:1536].rearrange("p (a b) -> p a b", a=2), func=AF.Copy), r=["z2"], w=[f"vaug{cur}"])
            sigmoid_parts(z[:, 1792:2048], ["z3"])
            V(lambda e: e.tensor_tensor(out=u, in0=z[:, 1536:1792], in1=t256b[:], op=ALU.mult), r=["z3", "_sb"], w=["u"])
            G(lambda e: e.tensor_tensor(out=vsf, in0=z[:, 2304:2560], in1=z[:, 2560:2816], op=ALU.mult), r=["z4", "z5"], w=["vsf"])

        def emit_transposes(ti, cur):
            P_ = ti < 16
            hu = 30 if P_ else 480
            hv = 2 if P_ else 32
            srcs = [(rb[:, 0, 0:128], "rb0"), (rb[:, 0, 128:256], "rb0"), (rb[:, 1, 0:128], "rb1"), (rb[:, 1, 128:256], "rb1"),
                    (qn[:, 0:128], "qn"), (qn[:, 128:256], "qn"), (knb[:], "knb")]
            for i, (ap, nm) in enumerate(srcs):
                T(lambda e, i=i, ap=ap: e.transpose(B2[:, i * 128:(i + 1) * 128], ap, identb[:]), r=[nm, "identb"], w=["B2"])
            A(lambda e: e.activation(out=tr[:].rearrange("p a b -> p (a b)"), in_=B2[:, 0:768], func=AF.Copy), r=["B2"], w=["tr"])
            A(lambda e: e.activation(out=kTa[cur][:], in_=B2[:, 768:896], func=AF.Copy), r=["B2"], w=[f"kTa{cur}"])
            if P_:
                V(lambda e: e.tensor_tensor(out=qdT[:].rearrange("p a b -> p (a b)"), in0=B2[:, 0:256], in1=cst["qdecP"][:], op=ALU.mult), r=["B2", "c_qdecP"], w=["qdT"])
            for c in range(2):
                T(lambda e, c=c: e.transpose(B3[:, c * 128:(c + 1) * 128], u[:, c * 128:(c + 1) * 128], identf[:]), r=["u", "identf"], w=B3N)
            for c in range(2):
                T(lambda e, c=c: e.transpose(B3[:, 256 + c * 128:256 + (c + 1) * 128], vsf[:, c * 128:(c + 1) * 128], identf[:]), r=["vsf", "identf"], w=B3N)
            A(lambda e: e.activation(out=extu[:, :, hu:hu + 128], in_=B3[:, 0:256].rearrange("p (a b) -> p a b", a=2), func=AF.Copy), r=B3N, w=["extu_new"])
            A(lambda e: e.activation(out=extv[:, :, hv:hv + 128], in_=B3[:, 256:512].rearrange("p (a b) -> p a b", a=2), func=AF.Copy), r=B3N, w=["extv_new"])

        def emit_conv(ti):
            P_ = ti < 16
            stp = 1 if P_ else 16
            for jj in range(31):
                for c in range(2):
                    dst = acc if jj % 2 == 0 else accv
                    dn = f"acc{c}" if jj % 2 == 0 else f"accv{c}"
                    if jj < 2:
                        V(lambda e, c=c, jj=jj, dst=dst: e.tensor_scalar(out=dst[:, c, :], in0=extu[:, c, jj * stp:jj * stp + 128], scalar1=cw[:, c, jj:jj + 1], scalar2=None, op0=ALU.mult),
                          r=["extu_new", "extu_halo", "cw"], w=[dn])
                    else:
                        V(lambda e, c=c, jj=jj, dst=dst: e.scalar_tensor_tensor(out=dst[:, c, :], in0=extu[:, c, jj * stp:jj * stp + 128], scalar=cw[:, c, jj:jj + 1], in1=dst[:, c, :], op0=ALU.mult, op1=ALU.add),
                          r=["extu_new", "extu_halo", "cw", dn], w=[dn])
            for c in range(2):
                V(lambda e, c=c: e.tensor_tensor(out=acc[:, c, :], in0=acc[:, c, :], in1=accv[:, c, :], op=ALU.add), r=[f"acc{c}", f"accv{c}"], w=[f"acc{c}"])
            for jj in range(3):
                for c in range(2):
                    if jj == 0:
                        V(lambda e, c=c: e.tensor_scalar(out=accv[:, c, :], in0=extv[:, c, 0:128], scalar1=sw[:, c, 0:1], scalar2=None, op0=ALU.mult),
                          r=["extv_new", "extv_halo", "sw"], w=[f"accv{c}"])
                    else:
                        V(lambda e, c=c, jj=jj: e.scalar_tensor_tensor(out=accv[:, c, :], in0=extv[:, c, jj * stp:jj * stp + 128], scalar=sw[:, c, jj:jj + 1], in1=accv[:, c, :], op0=ALU.mult, op1=ALU.add),
                          r=["extv_new", "extv_halo", "sw", f"accv{c}"], w=[f"accv{c}"])
            if ti == 0: ck(4.41)
            if P_:
                G(lambda e: e.tensor_copy(out=extu[:, :, 0:30], in_=extu[:, :, 128:158]), r=["extu_new", "acc0", "acc1"], w=["extu_halo"])
                G(lambda e: e.tensor_copy(out=extv[:, :, 0:2], in_=extv[:, :, 128:130]), r=["extv_new", "accv0", "accv1"], w=["extv_halo"])
            if ti == 0: ck(4.42)
            for c in range(2):
                T(lambda e, c=c: e.transpose(B3[:, c * 128:(c + 1) * 128], acc[:, c, :], identf[:]), r=[f"acc{c}", "identf"], w=B3N)
            for c in range(2):
                T(lambda e, c=c: e.transpose(B3[:, 256 + c * 128:256 + (c + 1) * 128], accv[:, c, :], identf[:]), r=[f"accv{c}", "identf"], w=B3N)
            V(lambda e: e.tensor_tensor(out=hb[:, 768:1024], in0=z[:, 2048:2304], in1=B3[:, 256:512], op=ALU.mult), r=["z4"] + B3N, w=["cat3"], after=["hb"])
            if ti == 0: ck(4.43)
            V(lambda e: e.tensor_copy(out=t256, in_=B3[:, 0:256]), r=B3N, w=["_sa"])
            if ti == 0: ck(4.431)
            V(lambda e: e.tensor_reduce(out=st[:, 56:58], in_=t256.rearrange("p (a b) -> p a b", a=2), axis=AX.X, op=ALU.add), r=["_sa"], w=["st56"])
            if ti == 0: ck(4.432)
            A(lambda e: e.activation(out=t256b[:], in_=t256, func=AF.Square), r=["_sa"], w=["_sb"])
            V(lambda e: e.tensor_reduce(out=st[:, 58:60], in_=t256b[:].rearrange("p (a b) -> p a b", a=2), axis=AX.X, op=ALU.add), r=["_sb"], w=["st58"])
            if ti == 0: ck(4.435)
            V(lambda e: e.tensor_tensor(out=st[:, 16:17], in0=st[:, 56:57], in1=st[:, 57:58], op=ALU.add), r=["st56"], w=["st16"])
            V(lambda e: e.tensor_tensor(out=st[:, 17:18], in0=st[:, 58:59], in1=st[:, 59:60], op=ALU.add), r=["st58"], w=["st17"])
            V(lambda e: e.tensor_scalar(out=st[:, 18:19], in0=st[:, 16:17], scalar1=1.0 / 256, scalar2=None, op0=ALU.mult), r=["st16"], w=["st18"])
            V(lambda e: e.tensor_tensor(out=st[:, 19:20], in0=st[:, 18:19], in1=st[:, 18:19], op=ALU.mult), r=["st18"], w=["st19"])
            V(lambda e: e.scalar_tensor_tensor(out=st[:, 20:21], in0=st[:, 17:18], scalar=1.0 / 256, in1=st[:, 19:20], op0=ALU.mult, op1=ALU.subtract), r=["st17", "st19"], w=["st20"])
            if ti == 0: ck(4.437)
            rsqrt_small(st[:, 21:22], st[:, 20:21], 1.0, ["st20"], ["st21"])
            if ti == 0: ck(4.44)
            V(lambda e: e.tensor_scalar(out=t256, in0=t256, scalar1=st[:, 18:19], scalar2=st[:, 21:22], op0=ALU.subtract, op1=ALU.mult), r=["_sa", "st18", "st21"], w=["_sa"])
            V(lambda e: e.tensor_tensor(out=t256, in0=t256, in1=lngb[:, 0:256], op=ALU.mult), r=["_sa", "lngb"], w=["_sa"])
            V(lambda e: e.tensor_tensor(out=t256, in0=t256, in1=lngb[:, 256:512], op=ALU.add), r=["_sa", "lngb"], w=["_sa"])
            A(lambda e: e.activation(out=t256b[:], in_=t256, func=AF.Exp, scale=-1.0), r=["_sa"], w=["_sb"])
            G(lambda e: e.tensor_scalar_add(out=t256b[:], in0=t256b[:], scalar1=1.0), r=["_sb"], w=["_sb"])
            V(lambda e: e.reciprocal(out=t256b[:], in_=t256b[:]), r=["_sb"], w=["_sb"])
            V(lambda e: e.tensor_tensor(out=hb[:, 512:768], in0=t256, in1=t256b[:], op=ALU.mult), r=["_sa", "_sb"], w=["cat2"], after=["hb"])

        def emit_groupnorm_out(o_ap, rnames):
            o3 = o_ap.rearrange("p (a b) -> p a b", a=4)
            tq = tmpf[:, 512:768]
            tq3 = tq.rearrange("p (a b) -> p a b", a=4)
            V(lambda e: e.tensor_reduce(out=st[:, 24:28], in_=o3, axis=AX.X, op=ALU.add), r=rnames, w=["st24"])
            A(lambda e: e.activation(out=tq, in_=o_ap, func=AF.Square), r=rnames, w=["tmpf1"])
            V(lambda e: e.tensor_reduce(out=st[:, 28:32], in_=tq3, axis=AX.X, op=ALU.add), r=["tmpf1"], w=["st28"])
            V(lambda e: e.tensor_scalar(out=st[:, 32:36], in0=st[:, 24:28], scalar1=1.0 / 64, scalar2=None, op0=ALU.mult), r=["st24"], w=["st32"])
            V(lambda e: e.tensor_tensor(out=st[:, 36:40], in0=st[:, 32:36], in1=st[:, 32:36], op=ALU.mult), r=["st32"], w=["st36"])
            V(lambda e: e.scalar_tensor_tensor(out=st[:, 40:44], in0=st[:, 28:32], scalar=1.0 / 64, in1=st[:, 36:40], op0=ALU.mult, op1=ALU.subtract), r=["st28", "st36"], w=["st40"])
            rsqrt_small(st[:, 44:48], st[:, 40:44], 1.0, ["st40"], ["st44"])
            V(lambda e: e.tensor_tensor(out=tq3, in0=o3, in1=st[:, 32:36][:, :, None].broadcast_to([128, 4, 64]), op=ALU.subtract), r=rnames + ["st32"], w=["tmpf1"])
            V(lambda e: e.tensor_tensor(out=tq3, in0=tq3, in1=st[:, 44:48][:, :, None].broadcast_to([128, 4, 64]), op=ALU.mult), r=["tmpf1", "st44"], w=["tmpf1"])
            V(lambda e: e.tensor_tensor(out=hb[:, 0:256], in0=tq, in1=sg, op=ALU.mult), r=["tmpf1", "sg"], w=["cat0"], after=["hb"])

        def emit_attn_finish(oa_ap, rnames):
            V(lambda e: e.tensor_tensor(out=st[:, 48:52], in0=oa_ap[:, :, 64], in1=esink[:], op=ALU.add), r=rnames + ["esink"], w=["st48"])
            V(lambda e: e.reciprocal(out=st[:, 52:56], in_=st[:, 48:52]), r=["st48"], w=["st52"])
            V(lambda e: e.tensor_tensor(out=hb[:, 256:512].rearrange("p (a b) -> p a b", a=4), in0=oa_ap[:, :, 0:64], in1=st[:, 52:56][:, :, None].broadcast_to([128, 4, 64]), op=ALU.mult),
              r=rnames + ["st52"], w=["cat1"], after=["hb"])

        def emit_out(ti):
            for k in range(8):
                T(lambda e, k=k: e.transpose(B2[:, k * 128:(k + 1) * 128], hb[:, k * 128:(k + 1) * 128], identb[:]), r=["cat0", "cat1", "cat2", "cat3", "identb"], w=["B2"])
            A(lambda e: e.activation(out=hT[:].rearrange("p a b -> p (a b)"), in_=B2[:], func=AF.Copy), r=["B2"], w=["hT"])
            gate = gateP if ti < 16 else gateS
            gn = ["gateP"] if ti < 16 else MIXF
            for hf in range(2):
                bank = [B4, B5][hf]; bn = f"B{4 + hf}"
                for k in range(8):
                    T(lambda e, k=k, bank=bank, hf=hf: e.matmul(bank[:], lhsT=hT[:, k, :], rhs=wout[:, k, hf * 512:(hf + 1) * 512], start=(k == 0), stop=(k == 7)), r=["hT", f"wout_{hf}"], w=[bn])
                V(lambda e, bank=bank, hf=hf, gate=gate: e.tensor_tensor(out=tmpf[:, hf * 512:(hf + 1) * 512], in0=bank[:], in1=gate[:, hf * 512:(hf + 1) * 512], op=ALU.mult), r=[bn] + gn, w=[f"tmpf{hf}"])
                G(lambda e, hf=hf: e.tensor_tensor(out=x[:, ti, hf * 512:(hf + 1) * 512], in0=x[:, ti, hf * 512:(hf + 1) * 512], in1=tmpf[:, hf * 512:(hf + 1) * 512], op=ALU.add),
                  r=[f"tmpf{hf}", f"x{ti}"], w=[f"x{ti}"])
            if os.environ.get("KDUMPMIX") and ti < 16:
                D(lambda e: e.dma_start(out=yp[ti * 128:(ti + 1) * 128, :], in_=tmpf[:]), r=TF, w=["yp"], sem="st_x")

        import os
        STOP = float(os.environ.get("KSTOP", "99"))

        class _Stop(Exception):
            pass

        def ck(n):
            if STOP <= n:
                raise _Stop()
        try:
          ck(1)
          def emit_layer(l):
            mod_phase(l, 0)
            if os.environ.get("KDUMPGATE") and l == 0:
                D(lambda e: e.dma_start(out=yp[0:128, :], in_=gateP[:]), r=["gateP"], w=["yp"], sem="st_x")
                V(lambda e: e.tensor_copy(out=tmpf[:, 0:16], in_=modPT[:]), r=["modPT"], w=["tmpf0"])
                D(lambda e: e.dma_start(out=yp[128:256, 0:16], in_=tmpf[:, 0:16]), r=["tmpf0"], w=["yp"], sem="st_x")
            ck(2)
            for n in range(6):
                c0 = n * 512; w_ = min(512, 2816 - c0)
                DG(lambda e, c0=c0, w_=w_: e.dma_start(out=win[:, :, c0:c0 + w_], in_=w_in[l][:, c0:c0 + w_].rearrange("(k p) n -> p k n", p=128)),
                   w=[f"win_{n}"], sem="ldw", after=FFN_NAMES)
            lastw = None
            for hf in range(2):
                lastw = DG(lambda e, hf=hf: e.dma_start(out=wout[:, :, hf * 512:(hf + 1) * 512], in_=w_out[l][:, hf * 512:(hf + 1) * 512].rearrange("(k p) n -> p k n", p=128)),
                           w=[f"wout_{hf}"], sem="ldw", after=FFN_NAMES)
            for nm_ in [f"win_{n}" for n in range(6)] + ["wout_0", "wout_1"]:
                S.res[nm_]['w'] = lastw
            D(lambda e: e.dma_start(out=gqk[:, 0:64], in_=qng[l].partition_broadcast(128)), w=["gqk"], sem="ld_gqk")
            D(lambda e: e.dma_start(out=gqk[:, 64:128], in_=kng[l].partition_broadcast(128)), w=["gqk"], sem="ld_gqk")
            D(lambda e: e.dma_start(out=esink[:], in_=sinks[l].partition_broadcast(128)), w=["esink"], sem="ld_esink")
            A(lambda e: e.activation(out=esink[:], in_=esink[:], func=AF.Exp), r=["esink"], w=["esink"])
            D(lambda e: e.dma_start(out=lngb[:, 0:256], in_=lng[l].partition_broadcast(128)), w=["lngb"], sem="ld_lngb")
            D(lambda e: e.dma_start(out=lngb[:, 256:512], in_=lnb[l].partition_broadcast(128)), w=["lngb"], sem="ld_lngb")
            D(lambda e: e.dma_start(out=cw[:], in_=cdwT[l]), w=["cw"], sem="ld_cw")
            D(lambda e: e.dma_start(out=sw[:], in_=sdwT[l]), w=["sw"], sem="ld_sw")

            V(lambda e: e.memset(Sst[:], 0.0), w=SSTN)
            for ti in range(16):
                emit_h(ti, l, 0)
                evac_hT(ti, hT[:], "hT")
                emit_z([0, 1] if ti < 15 else [0, 1, 2, 3, 4, 5])
                emit_local_ret(ti)
                emit_su()
                state_update(cst["cdecP"], "c_cdecP")
                if ti == 15:
                    emit_local_rest(ti, 0)
                    D(lambda e: e.dma_start(out=gin[l][G_S:G_S + 128, 0:128], in_=Sst[:].rearrange("p a b -> p (a b)")), r=SSTN, w=["gin"], sem="stg")
                    V(lambda e: e.memset(t256b[:, 0:128], 0.0), w=["_sb"])
                    D(lambda e: e.dma_start(out=gin[l][G_S:G_S + 128, 128:256], in_=t256b[:, 0:128]), r=["_sb"], w=["gin"], sem="stg")
                    D(lambda e: e.dma_start(out=gin[l][G_KV:G_KV + 128, 0:128], in_=knf[:]), r=["knf"], w=["gin"], sem="stg")
                    D(lambda e: e.dma_start(out=gin[l][G_KV:G_KV + 128, 128:256], in_=z[:, 1408:1536]), r=["z2"], w=["gin"], sem="stg")
                    D(lambda e: e.dma_start(out=gin[l][G_CF:G_CF + 30, :], in_=u[98:128, :]), r=["u"], w=["gin"], sem="stg")
                    D(lambda e: e.dma_start(out=gin[l][G_SC:G_SC + 2, :], in_=vsf[126:128, :]), r=["vsf"], w=["gin"], sem="stg")
                    D(lambda e: e.dma_start(out=o_kp[l], in_=knf[:]), r=["knf"], w=["o_kp"], sem="st_knf")
                    D(lambda e: e.dma_start(out=o_vp[l], in_=z[:, 1408:1536]), r=["z2"], w=["o_vp"], sem="st_z2")
                    D(lambda e: e.dma_start(out=o_confp[l], in_=u[98:128, :]), r=["u"], w=["o_confp"], sem="st_u")
                    D(lambda e: e.dma_start(out=o_scp[l], in_=vsf[126:128, :]), r=["vsf"], w=["o_scp"], sem="st_vsf")
            ck(3)
            S.op("pool", lambda e: e.collective_compute("AllGather", ALU.bypass, replica_groups=[[0, 1, 2, 3], [4, 5, 6, 7]], ins=[gin[l]], outs=[gout[l]]),
                 reads=["gin"], writes=["gout"], dma_sem="cc", inc=1)
            gv = gout[l].rearrange("(r p) n -> p r n", p=G_ROWS)
            D(lambda e: e.dma_start(out=gsel[:], in_=gv[G_S:G_S + 128, :, :]), r=["gout"], w=TF, sem="ld_tmpf")
            for pr in range(2):
                for r4 in range(4):
                    src = gsel[:, r4, pr * 64:(pr + 1) * 64]
                    if r4 == 0:
                        V(lambda e, pr=pr, src=src: e.tensor_scalar(out=Sst[:, pr, :], in0=src, scalar1=cst["coefS"][:, pr:pr + 1], scalar2=None, op0=ALU.mult),
                          r=TF + ["c_coefS"], w=[f"Sst{pr}0", f"Sst{pr}1"])
                    else:
                        V(lambda e, pr=pr, src=src, r4=r4: e.scalar_tensor_tensor(out=Sst[:, pr, :], in0=src, scalar=cst["coefS"][:, r4 * 2 + pr:r4 * 2 + pr + 1], in1=Sst[:, pr, :], op0=ALU.mult, op1=ALU.add),
                          r=TF + ["c_coefS", f"Sst{pr}0", f"Sst{pr}1"], w=[f"Sst{pr}0", f"Sst{pr}1"])
            A(lambda e: e.activation(out=Sbf[:], in_=Sst[:], func=AF.Copy), r=SSTN, w=["Sbf"])

            def select_rows(rows0, nrows, dst, dname):
                D(lambda e: e.dma_start(out=gsel[0:nrows, :, :], in_=gv[rows0:rows0 + nrows, :, :]), r=["gout"], w=TF, sem="ld_tmpf")
                for r4 in range(4):
                    if r4 == 0:
                        V(lambda e: e.tensor_scalar(out=dst, in0=gsel[0:nrows, 0, :], scalar1=cst["sel"][0:nrows, 0:1], scalar2=None, op0=ALU.mult), r=TF + ["c_sel"], w=[dname])
                    else:
                        V(lambda e, r4=r4: e.scalar_tensor_tensor(out=dst, in0=gsel[0:nrows, r4, :], scalar=cst["sel"][0:nrows, r4:r4 + 1], in1=dst, op0=ALU.mult, op1=ALU.add),
                          r=TF + ["c_sel", dname], w=[dname])
            hsel = ef32[:, 0:256]
            select_rows(G_KV, 128, hsel, "ee0")
            A(lambda e: e.activation(out=knb[:], in_=hsel[:, 0:128], func=AF.Copy), r=["ee0"], w=["knb"])
            A(lambda e: e.activation(out=vaug[1][:, :, 0:64], in_=hsel[:, 128:256].rearrange("p (a b) -> p a b", a=2), func=AF.Copy), r=["ee0"], w=["vaug1"])
            T(lambda e: e.transpose(B2[:, 0:128], knb[:], identb[:]), r=["knb", "identb"], w=["B2"])
            A(lambda e: e.activation(out=kTa[1][:], in_=B2[:, 0:128], func=AF.Copy), r=["B2"], w=["kTa1"])
            select_rows(G_CF, 30, hsel[0:30, :], "ee0")
            for c in range(2):
                T(lambda e, c=c: e.transpose(B3[:, c * 32:c * 32 + 30], hsel[0:30, c * 128:(c + 1) * 128], identf[0:30, 0:30]), r=["ee0", "identf"], w=B3N)
            A(lambda e: e.activation(out=extu[:, :, 0:30], in_=B3[:, 0:64].rearrange("p (a b) -> p a b", a=2)[:, :, 0:30], func=AF.Copy), r=B3N, w=["extu_halo"], after=["extu_new"])
            select_rows(G_SC, 2, hsel[0:2, :], "ee0")
            for c in range(2):
                T(lambda e, c=c: e.transpose(B3[:, 64 + c * 32:64 + c * 32 + 2], hsel[0:2, c * 128:(c + 1) * 128], identf[0:2, 0:2]), r=["ee0", "identf"], w=B3N)
            A(lambda e: e.activation(out=extv[:, :, 0:2], in_=B3[:, 64:128].rearrange("p (a b) -> p a b", a=2)[:, :, 0:2], func=AF.Copy), r=B3N, w=["extv_halo"], after=["extv_new"])

            ck(4)
            for ti in range(16):
                if ti == 1:
                    ck(5)
                cur = ti % 2
                prv = 1 - cur
                emit_h(ti, l, 0)
                evac_hT(ti, hT[:], "hT")
                emit_z([0, 1, 2, 3, 4, 5])
                emit_local_ret(ti)
                emit_local_rest(ti, cur)
                if ti == 0: ck(4.1)
                emit_transposes(ti, cur)
                if ti == 0: ck(4.2)
                for h in range(4):
                    rs = slice((h % 2) * 64, (h % 2) * 64 + 64)
                    bank, bn = (B7, "B7a") if h % 2 == 0 else (B3, "B3a")
                    T(lambda e, h=h, rs=rs, bank=bank: e.matmul(bank[:, (h // 2) * 128:(h // 2 + 1) * 128], lhsT=tr[rs, 2 + h // 2, :], rhs=tr[rs, h // 2, :], start=True, stop=True), r=["tr"], w=[bn])
                attm4 = attm[:].rearrange("p (pr hf i) -> p pr hf i", pr=2, hf=2)
                mask4 = cst["maskP"][:].rearrange("p (pr hf i) -> p pr hf i", pr=2, hf=2)
                for hf, (bank, bn) in enumerate([(B7, "B7a"), (B3, "B3a")]):
                    V(lambda e, hf=hf, bank=bank, attm4=attm4, mask4=mask4: e.tensor_tensor(out=attm4[:, :, hf, :], in0=bank[:, 0:256].rearrange("p (pr i) -> p pr i", pr=2), in1=mask4[:, :, hf, :], op=ALU.mult),
                      r=[bn, "c_maskP"], w=["attm"])
                if ti == 0: ck(4.25)
                for h in range(4):
                    T(lambda e, h=h: e.matmul(B6[:, h * 64:(h + 1) * 64], lhsT=attm[:, h * 128:(h + 1) * 128], rhs=rb[:, 3, h * 64:(h + 1) * 64], start=True, stop=True), r=["attm", "rb3"], w=["o"])
                for h in (0, 2, 1, 3):
                    rs = slice((h % 2) * 64, (h % 2) * 64 + 64)
                    bank, bn = (B7, "B7b") if h % 2 == 0 else (B3, "B3b")
                    T(lambda e, h=h, rs=rs, bank=bank: e.matmul(bank[:, 256 + (h // 2) * 64:256 + (h // 2 + 1) * 64], lhsT=qdT[rs, h // 2, :], rhs=Sbf[rs, h // 2, :], start=True, stop=True), r=["qdT", "Sbf"], w=[bn])
                emit_su()
                state_update(cst["cdecP"], "c_cdecP")
                osum_p = tmpf[:, 0:256]
                os4 = osum_p.rearrange("p (pr hf e) -> p pr hf e", pr=2, hf=2)
                for hf, (bank, bn) in enumerate([(B7, "B7b"), (B3, "B3b")]):
                    A(lambda e, hf=hf, bank=bank: e.activation(out=os4[:, :, hf, :], in_=bank[:, 256:384].rearrange("p (pr e) -> p pr e", pr=2), func=AF.Copy), r=[bn], w=["tmpf0"])
                V(lambda e: e.tensor_tensor(out=osum_p, in0=osum_p, in1=B6[:, 0:256], op=ALU.add), r=["tmpf0", "o"], w=["tmpf0"])
                emit_groupnorm_out(osum_p, ["tmpf0"])
                if ti == 0: ck(4.3)
                for kv, (bank, bn) in enumerate([(B4, "B4"), (B5, "B5")]):
                    rs = slice(kv * 64, kv * 64 + 64)
                    for kb, (kt, ktn) in enumerate([(kTa[prv], f"kTa{prv}"), (kTa[cur], f"kTa{cur}")]):
                        T(lambda e, bank=bank, kb=kb, rs=rs, kt=kt: e.matmul(bank[:, kb * 256:(kb + 1) * 256], lhsT=kt[rs, :], rhs=trf[rs, 512:768], start=True, stop=True),
                          r=[ktn, "tr"], w=[bn])
                    A(lambda e, bank=bank, kv=kv: e.activation(out=ee[:, kv * 512:(kv + 1) * 512], in_=bank[:], func=AF.Exp, scale=0.125), r=[bn], w=[f"ee{kv}"])
                    if ti == 0:
                        V(lambda e, kv=kv: e.scalar_tensor_tensor(out=pp[:, kv * 512:kv * 512 + 256], in0=ee[:, kv * 512:kv * 512 + 256], scalar=cst["hasprev"][:, 0:1], in1=cst["Ep"][:, kv * 512:kv * 512 + 256], op0=ALU.mult, op1=ALU.mult),
                          r=[f"ee{kv}", "c_Ep", "c_hasprev"], w=[f"pp{kv}"])
                        V(lambda e, kv=kv: e.tensor_tensor(out=pp[:, kv * 512 + 256:(kv + 1) * 512], in0=ee[:, kv * 512 + 256:(kv + 1) * 512], in1=cst["Ep"][:, kv * 512 + 256:(kv + 1) * 512], op=ALU.mult),
                          r=[f"ee{kv}", "c_Ep"], w=[f"pp{kv}"])
                    else:
                        V(lambda e, kv=kv: e.tensor_tensor(out=pp[:, kv * 512:(kv + 1) * 512], in0=ee[:, kv * 512:(kv + 1) * 512], in1=cst["Ep"][:, kv * 512:(kv + 1) * 512], op=ALU.mult),
                          r=[f"ee{kv}", "c_Ep"], w=[f"pp{kv}"])
                oa = B7[:, 0:320].rearrange("p (a b) -> p a b", a=4)
                for kv in range(2):
                    for g in range(2):
                        hh = kv * 2 + g
                        for kb, va, van in [(0, vaug[prv], f"vaug{prv}"), (1, vaug[cur], f"vaug{cur}")]:
                            c0 = kv * 512 + kb * 256 + g * 128
                            T(lambda e, kb=kb, kv=kv, hh=hh, va=va, c0=c0: e.matmul(B7[:, hh * 80:hh * 80 + 66], lhsT=pp[:, c0:c0 + 128], rhs=va[:, kv, 0:66],
                                                                                   start=(kb == 0), stop=(kb == 1)), r=[f"pp{kv}", van], w=B7N)
                emit_attn_finish(oa, B7N)
                if ti == 0: ck(4.4)
                emit_conv(ti)
                if ti == 0: ck(4.5)
                if os.environ.get("KDUMPCAT"):
                    A(lambda e: e.activation(out=tmpf[:], in_=hb[:], func=AF.Copy), r=["cat0", "cat1", "cat2", "cat3"], w=TF)
                    D(lambda e, ti=ti: e.dma_start(out=yp[ti * 128:(ti + 1) * 128, :], in_=tmpf[:]), r=TF, w=["yp"], sem="st_x")
                emit_out(ti)
                if ti == 15:
                    D(lambda e: e.dma_start(out=o_retp[l], in_=Sst[:]), r=SSTN, w=["o_retp"], sem="st_Sst")

            ck(6)
            ti = 16
            D(lambda e: e.dma_start(out=S0, in_=retS[l]), w=["S0"], sem="ld_S0", after=["kcTb", "vcb", "vcb1"])
            D(lambda e: e.dma_start(out=extu[:, :, 0:480], in_=confT[l]), w=["extu_halo"], sem="ld_extu", after=["extu_new"])
            D(lambda e: e.dma_start(out=extv[:, :, 0:32], in_=scT[l]), w=["extv_halo"], sem="ld_extv", after=["extv_new"])
            emit_h(ti, l, 0)
            evac_hT(ti, hT[:], "hT")
            emit_z([0, 1, 2, 3, 4, 5])
            A(lambda e: e.activation(out=S0b, in_=S0, func=AF.Copy), r=["S0"], w=EEPP)
            emit_local_ret(ti)
            emit_local_rest(ti, 0)
            emit_transposes(ti, 0)
            ck(6.1)
            for a_ in (range(2) if "trs" not in os.environ.get("KSKIP", "") else []):
                V(lambda e, a_=a_: e.tensor_tensor(out=trs[:, a_, :].rearrange("p (s t) -> p s t", s=16), in0=tr[:, a_, :].rearrange("p (t s) -> p s t", t=8),
                                                   in1=cst["qdecS"][:, a_ * 128:(a_ + 1) * 128].rearrange("p (s t) -> p s t", s=16), op=ALU.mult), r=["tr", "c_qdecS"], w=["trs"])
                V(lambda e, a_=a_: e.tensor_copy(out=trs[:, 2 + a_, :].rearrange("p (s t) -> p s t", s=16), in_=tr[:, 4 + a_, :].rearrange("p (t s) -> p s t", t=8)), r=["tr"], w=["trs"])
            for h in range(4):
                rs = slice((h % 2) * 64, (h % 2) * 64 + 64)
                bank, bn = (B7, "B7a") if h % 2 == 0 else (B3, "B3a")
                T(lambda e, h=h, rs=rs, bank=bank: e.matmul(bank[:, (h // 2) * 128:(h // 2 + 1) * 128], lhsT=tr[rs, 2 + h // 2, :], rhs=tr[rs, h // 2, :], start=True, stop=True), r=["tr"], w=[bn])
            attm4 = attm[:].rearrange("p (pr hf i) -> p pr hf i", pr=2, hf=2)
            mask4 = cst["maskS"][:].rearrange("p (pr hf i) -> p pr hf i", pr=2, hf=2)
            for hf, (bank, bn) in (enumerate([(B7, "B7a"), (B3, "B3a")]) if "attm" not in os.environ.get("KSKIP", "") else []):
                V(lambda e, hf=hf, bank=bank, attm4=attm4, mask4=mask4: e.tensor_tensor(out=attm4[:, :, hf, :], in0=bank[:, 0:256].rearrange("p (pr i) -> p pr i", pr=2), in1=mask4[:, :, hf, :], op=ALU.mult),
                  r=[bn, "c_maskS"], w=["attm"])
            for h in range(4):
                T(lambda e, h=h: e.matmul(B6[:, h * 64:(h + 1) * 64], lhsT=attm[:, h * 128:(h + 1) * 128], rhs=rb[:, 3, h * 64:(h + 1) * 64], start=True, stop=True), r=["attm", "rb3"], w=["o"])
            ck(6.11)
            for h in range(4):
                rs = slice((h % 2) * 64, (h % 2) * 64 + 64)
                bank, bn = (B4, "B4") if h % 2 == 0 else (B5, "B5")
                for s_ in range(16):
                    c0 = (h // 2) * 128 + s_ * 8
                    T(lambda e, h=h, s_=s_, rs=rs, bank=bank, c0=c0: e.matmul(bank[0:64, c0:c0 + 8], lhsT=S0b[rs, h // 2, s_, :], rhs=trs[rs, h // 2, s_ * 8:(s_ + 1) * 8], start=True, stop=True),
                      r=EEPP + ["trs"], w=[bn])
            ck(6.12)
            for h in range(4):
                bank, bn = (B4, "B4") if h % 2 == 0 else (B5, "B5")
                V(lambda e, h=h, bank=bank: e.tensor_copy(out=ocs[0:64, h, :].rearrange("p (t s) -> p t s", t=8), in_=bank[0:64, (h // 2) * 128:(h // 2 + 1) * 128].rearrange("p (s t) -> p t s", s=16)), r=[bn], w=["hT"])
            for h in range(4):
                T(lambda e, h=h: e.transpose(B3[:, 256 + h * 64:256 + (h + 1) * 64], ocs[0:64, h, :], identf[0:64, 0:64]), r=["hT", "identf"], w=["B3b"])
            osum = tmpf[:, 0:256]
            A(lambda e: e.activation(out=osum, in_=B3[:, 256:512], func=AF.Copy), r=["B3b"], w=["tmpf0"])
            V(lambda e: e.tensor_tensor(out=osum, in0=osum, in1=B6[:, 0:256], op=ALU.add), r=["tmpf0", "o"], w=["tmpf0"])
            ck(6.13)
            for h in range(4):
                rs = slice((h % 2) * 64, (h % 2) * 64 + 64)
                pr = h // 2
                V(lambda e, h=h: e.tensor_tensor(out=vbd, in0=rb[:, 3, h * 64:(h + 1) * 64][:, None, :].broadcast_to([128, 16, 64]),
                                                 in1=cst["onehot"][:][:, :, None].broadcast_to([128, 16, 64]), op=ALU.mult), r=["rb3", "c_onehot"], w=["acc0", "acc1", "accv0", "accv1"])
                for q2 in range(2):
                    bank = [B4, B5][q2]; bn = f"B{4 + q2}"
                    T(lambda e, q2=q2, bank=bank, pr=pr: e.matmul(bank[:], lhsT=rb[:, 2, pr * 128:(pr + 1) * 128], rhs=vbdf[:, q2 * 512:(q2 + 1) * 512], start=True, stop=True),
                      r=["rb2", "acc0", "acc1", "accv0", "accv1"], w=[bn])
                    V(lambda e, q2=q2, bank=bank, rs=rs, pr=pr: e.scalar_tensor_tensor(out=S0[rs, pr, q2 * 8:(q2 + 1) * 8, :].rearrange("p a b -> p (a b)"), in0=S0[rs, pr, q2 * 8:(q2 + 1) * 8, :].rearrange("p a b -> p (a b)"),
                                                                                       scalar=cst["cdecS"][rs, pr:pr + 1], in1=bank[rs, :], op0=ALU.mult, op1=ALU.add),
                      r=["S0", bn, "c_cdecS"], w=["S0"])
            ck(6.14)
            D(lambda e: e.dma_start(out=o_rets[l], in_=S0), r=["S0"], w=["o_rets"], sem="st_S0")
            ck(6.15)
            emit_groupnorm_out(osum, ["tmpf0"])
            ck(6.2)
            DG(lambda e: e.dma_start(out=kcTb, in_=kcT[l]), w=["kcTb"], sem="ldk", after=["S0"])
            G(lambda e: e.memset(samp[:, 2048:4160], 1.0), w=["vcb", "vcb1"], after=["S0"])
            lastk = DG(lambda e: e.dma_start(out=vcb[:, :, :, 0:64], in_=vc[l].rearrange("p s (k d) -> p s k d", k=2)), w=["vcb"], sem="ldk", after=["S0"])
            S.res["kcTb"]['w'] = lastk
            for kv, (bank, bn) in enumerate([(B4, "B4"), (B5, "B5")]):
                rs = slice(kv * 64, kv * 64 + 64)
                T(lambda e, bank=bank, rs=rs: e.matmul(bank[:, 0:256], lhsT=kTa[0][rs, :], rhs=trf[rs, 512:768], start=True, stop=True), r=["kTa0", "tr"], w=[bn])
                for s_ in range(16):
                    for g in range(2):
                        c0 = 256 + s_ * 16 + g * 8
                        T(lambda e, s_=s_, g=g, rs=rs, c0=c0, bank=bank: e.matmul(bank[:, c0:c0 + 8], lhsT=kcTb[rs, s_, :], rhs=trs[rs, 2 + g, s_ * 8:(s_ + 1) * 8], start=True, stop=True), r=["kcTb", "trs"], w=[bn])
                A(lambda e, bank=bank, kv=kv: e.activation(out=ee[:, kv * 512:(kv + 1) * 512], in_=bank[:], func=AF.Exp, scale=0.125), r=[bn], w=[f"ee{kv}"])
                V(lambda e, kv=kv: e.tensor_tensor(out=pp[:, kv * 512:(kv + 1) * 512], in0=ee[:, kv * 512:(kv + 1) * 512], in1=cst["Es"][:, kv * 512:(kv + 1) * 512], op=ALU.mult), r=[f"ee{kv}", "c_Es"], w=[f"pp{kv}"])
            for kv in range(2):
                for g in range(2):
                    hh = kv * 2 + g
                    c0 = kv * 512 + g * 128
                    T(lambda e, kv=kv, hh=hh, c0=c0: e.matmul(B7[:, hh * 80:hh * 80 + 66], lhsT=pp[:, c0:c0 + 128], rhs=vaug[0][:, kv, 0:66], start=True, stop=True),
                      r=[f"pp{kv}", "vaug0"], w=B7N)
            for s_ in range(16):
                for kv in range(2):
                    c0 = kv * 256 + s_ * 16
                    T(lambda e, s_=s_, kv=kv, c0=c0: e.matmul(B6[0:65, c0:c0 + 16], lhsT=vcb[:, s_, kv, 0:65], rhs=pp[:, kv * 512 + 256 + s_ * 16:kv * 512 + 256 + (s_ + 1) * 16], start=True, stop=True),
                      r=["vcb", "vcb1", f"pp{kv}"], w=["o", "SU"])
            for k_ in range(2):
                for g_ in range(2):
                    V(lambda e, k_=k_, g_=g_: e.tensor_copy(out=ocs[0:65, k_ * 2 + g_, :].rearrange("p (t s) -> p t s", t=8),
                                                             in_=B6[0:65, k_ * 256:(k_ + 1) * 256].rearrange("p (s g t) -> p g t s", s=16, g=2)[:, g_, :, :]), r=["o", "SU"], w=["hT"])
            for hh in range(4):
                T(lambda e, hh=hh: e.transpose(B3[:, 256 + hh * 64:256 + (hh + 1) * 64], ocs[0:64, hh, :], identf[0:64, 0:64]), r=["hT", "identf"], w=["B3b"])
            for hh in range(4):
                T(lambda e, hh=hh: e.transpose(B4[:, hh * 2:hh * 2 + 1], ocs[64:65, hh, :], identf[64:65, 64:65]), r=["hT", "identf"], w=["B4"])
            oas = tmpf[:, 512:832].rearrange("p (a b) -> p a b", a=4)
            A(lambda e: e.activation(out=oas, in_=B7[:, 0:320].rearrange("p (a b) -> p a b", a=4), func=AF.Copy), r=B7N, w=["tmpf1"])
            V(lambda e: e.tensor_tensor(out=oas[:, :, 0:64], in0=oas[:, :, 0:64], in1=B3[:, 256:512].rearrange("p (a b) -> p a b", a=4), op=ALU.add), r=["tmpf1", "B3b"], w=["tmpf1"])
            V(lambda e: e.tensor_tensor(out=oas[:, :, 64], in0=oas[:, :, 64], in1=B4[:, 0:8].rearrange("p (a b) -> p a b", a=4)[:, :, 0], op=ALU.add), r=["tmpf1", "B4"], w=["tmpf1"])
            emit_attn_finish(oas, ["tmpf1"])
            ck(6.3)
            emit_conv(ti)
            ck(6.4)
            D(lambda e: e.dma_start(out=o_ks[l][:, 0:120, :], in_=kc_o[l][:, 8:128, :]), w=["o_ks"], sem="st_d2d_k")
            D(lambda e: e.dma_start(out=o_vs[l][:, 0:120, :], in_=vc_o[l][:, 8:128, :]), w=["o_vs"], sem="st_d2d_v")
            D(lambda e: e.dma_start(out=o_confs[l][:, 0:22, :], in_=conf_o[l][:, 8:30, :]), w=["o_confs"], sem="st_d2d_c")
            for t in range(8):
                D(lambda e, t=t: e.dma_start(out=o_ks[l][:, 120 + t, :], in_=knf[t * 16:(t + 1) * 16, :]), r=["knf"], w=["o_ks"], sem="st_knf")
                D(lambda e, t=t: e.dma_start(out=o_vs[l][:, 120 + t, :], in_=z[t * 16:(t + 1) * 16, 1408:1536]), r=["z2"], w=["o_vs"], sem="st_z2")
                D(lambda e, t=t: e.dma_start(out=o_confs[l][:, 22 + t, :], in_=u[t * 16:(t + 1) * 16, :]), r=["u"], w=["o_confs"], sem="st_u")
                if t >= 6:
                    D(lambda e, t=t: e.dma_start(out=o_scs[l][:, t - 6, :], in_=vsf[t * 16:(t + 1) * 16, :]), r=["vsf"], w=["o_scs"], sem="st_vsf")
            D(lambda e: e.dma_start(out=gateS[:], in_=modS_d[l][0][:, 2048:3072]), r=["modS_d"], w=MIXF, sem="ld_mixf")
            emit_out(ti)

            ck(7)
            mod_phase(l, 1)
            D(lambda e: e.dma_start(out=gateS[:], in_=modS_d[l][1][:, 2048:3072]), r=["modS_d"], w=MIXF, sem="ld_mixf")
            blocks = [list(range(0, 8)), list(range(8, 17))]
            ubc = [0]
            for bi, tiles in enumerate(blocks):
                for i, ti in enumerate(tiles):
                    emit_h(ti, l, 1)
                    evac_hT(ti, h2T[:, :, i * 128:(i + 1) * 128], "h2T", after=MIX_NAMES)
                subs = [tiles[0:4], tiles[4:8]] + ([tiles[8:9]] if len(tiles) > 8 else [])
                for g8 in range(8):
                    b = g8 % 2
                    DG(lambda e, g8=g8, b=b: e.dma_start(out=W1g[b][:], in_=w_ff1[l][:, g8 * 512:(g8 + 1) * 512].rearrange("(k p) n -> p k n", p=128)),
                       w=[f"W1g{b}"], sem=f"ldf{b}", after=MIX_NAMES)
                    DG(lambda e, g8=g8, b=b: e.dma_start(out=W2g[b][:], in_=w_ff2[l][g8 * 512:(g8 + 1) * 512, :].rearrange("(k p) n -> p k n", p=128)),
                       w=[f"W2g{b}"], sem=f"ldg{b}", after=MIX_NAMES)
                    for si, sub in enumerate(subs):
                        n = len(sub) * 128
                        o0 = (sub[0] - tiles[0]) * 128
                        ub = ubc[0] % 2
                        ubc[0] += 1
                        for fc in range(4):
                            bank = [B0, B1, B3, B6][fc]
                            bn = [["B0"], ["B1"], B3N, ["o", "SU"]][fc]
                            tf = tmpf[:, (fc % 2) * 512:(fc % 2) * 512 + n]
                            tfn = f"tmpf{fc % 2}"
                            for k in range(8):
                                T(lambda e, k=k, fc=fc, bank=bank, b=b, o0=o0, n=n: e.matmul(bank[:, 0:n], lhsT=W1g[b][:, k, fc * 128:(fc + 1) * 128], rhs=h2T[:, k, o0:o0 + n], start=(k == 0), stop=(k == 7)),
                                  r=[f"W1g{b}", "h2T"], w=bn)
                            A(lambda e, bank=bank, n=n, tf=tf: e.activation(out=tf, in_=bank[:, 0:n], func=AF.Relu), r=bn, w=[tfn])
                            G(lambda e, fc=fc, ub=ub, n=n, tf=tf: e.tensor_tensor(out=uT[ub][:, fc, 0:n], in0=tf, in1=tf, op=ALU.mult), r=[tfn], w=[f"uT{ub}"], after=MIX_NAMES)
                        for i, ti in enumerate(sub):
                            gate = gateP if ti < 16 else gateS
                            gn = ["gateP"] if ti < 16 else MIXF
                            for hf in range(2):
                                bank = [B4, B5][hf]; bn = f"B{4 + hf}"
                                tn = ["ee0", "ee1"] if hf == 0 else ["pp0", "pp1"]
                                for fc in range(4):
                                    T(lambda e, fc=fc, bank=bank, hf=hf, i=i, ub=ub, b=b: e.matmul(bank[:], lhsT=uT[ub][:, fc, i * 128:(i + 1) * 128], rhs=W2g[b][:, fc, hf * 512:(hf + 1) * 512], start=(fc == 0), stop=(fc == 3)),
                                      r=[f"uT{ub}", f"W2g{b}"], w=[bn])
                                V(lambda e, bank=bank, hf=hf, gate=gate: e.tensor_tensor(out=t512[hf], in0=bank[:], in1=gate[:, hf * 512:(hf + 1) * 512], op=ALU.mult), r=[bn] + gn, w=tn)
                                G(lambda e, hf=hf, ti=ti: e.tensor_tensor(out=x[:, ti, hf * 512:(hf + 1) * 512], in0=x[:, ti, hf * 512:(hf + 1) * 512], in1=t512[hf], op=ALU.add),
                                  r=tn + [f"x{ti}"], w=[f"x{ti}"])
            ck(8)
            if l == DEPTH - 1:
                for ti in range(16):
                    D(lambda e, ti=ti: e.dma_start(out=yp[ti * 128:(ti + 1) * 128, :], in_=x[:, ti, :]), r=[f"x{ti}"], w=["yp"], sem="st_x")
                D(lambda e: e.dma_start(out=ys, in_=x[:, 16, :]), r=["x16"], w=["ys"], sem="st_x")
          for _l in range(DEPTH):
            emit_layer(_l)
        except _Stop:
            if os.environ.get("KDUMPX"):
                for ti in range(16):
                    D(lambda e, ti=ti: e.dma_start(out=yp[ti * 128:(ti + 1) * 128, :], in_=x[:, ti, :]), r=[f"x{ti}"], w=["yp"], sem="st_x")
            for _i in range(int(os.environ.get("KDUMMY", "0"))):
                A(lambda e: e.activation(out=st[:, 61:62], in_=st[:, 60:61], func=AF.Copy), r=["st60"], w=["st61"])
        S.wait_all("sp")
        S.emit(es)
    return nc


_NC = None


def _host_inputs(inp):
    f = lambda a: np.ascontiguousarray(np.asarray(a, dtype=np.float32))
    maps = []
    shared = {
        "ada_w": f(inp["ada_w"]), "ada_b": f(inp["ada_b"]).reshape(2, 1, 6144),
        "n1g": f(inp["norm1_g"]).reshape(2, 1, 1024), "n2g": f(inp["norm2_g"]).reshape(2, 1, 1024),
        "w_in": f(inp["w_in"]), "w_out": f(inp["w_out"]), "w_ff1": f(inp["w_ff1"]), "w_ff2": f(inp["w_ff2"]),
        "qng": f(inp["q_norm_g"]).reshape(2, 1, 64), "kng": f(inp["k_norm_g"]).reshape(2, 1, 64), "sinks": f(inp["attn_sinks"]).reshape(2, 1, 4),
        "cdwT": f(np.asarray(inp["conf_dw"]).reshape(2, 31, 2, 128).transpose(0, 3, 2, 1)),
        "sdwT": f(np.asarray(inp["sconv_dw"]).reshape(2, 3, 2, 128).transpose(0, 3, 2, 1)),
        "lng": f(inp["conf_ln_g"]).reshape(2, 1, 256), "lnb": f(inp["conf_ln_b"]).reshape(2, 1, 256),
    }
    xp = np.asarray(inp["x_prompt"]); xs = np.asarray(inp["x_sample"])
    for c in range(8):
        b, j = c // 4, c % 4
        sl = slice(16 * c, 16 * c + 16)
        m = dict(shared)
        m["xp"] = f(xp[b, j * 2048:(j + 1) * 2048])
        m["xs"] = f(xs[sl].transpose(1, 0, 2).reshape(128, 1024))
        m["cP"] = f(np.broadcast_to(np.asarray(inp["c_prompt"])[b][None, :], (128, 1024)))
        m["cS"] = f(np.broadcast_to(np.asarray(inp["c_sample"])[sl][None, :, :], (8, 16, 1024)).reshape(128, 1024))
        sr = np.asarray(inp["state_ret"])[:, sl]
        sr = sr.reshape(2, 16, 2, 2, 64, 64)
        m["retS"] = f(sr.transpose(0, 3, 4, 2, 1, 5).reshape(2, 128, 2, 16, 64))
        kc = np.asarray(inp["cache_swa_k"])[:, sl].reshape(2, 16, 128, 128)
        vcc = np.asarray(inp["cache_swa_v"])[:, sl].reshape(2, 16, 128, 128)
        m["kcT"] = f(kc.transpose(0, 3, 1, 2)); m["vc"] = f(vcc.transpose(0, 2, 1, 3))
        m["kc_o"] = f(kc); m["vc_o"] = f(vcc)
        cf = np.asarray(inp["state_conf"])[:, sl]
        m["confT"] = f(cf.reshape(2, 16, 30, 2, 128).transpose(0, 4, 3, 2, 1).reshape(2, 128, 2, 480))
        m["conf_o"] = f(cf)
        sc = np.asarray(inp["state_sconv"])[:, sl]
        m["scT"] = f(sc.reshape(2, 16, 2, 2, 128).transpose(0, 4, 3, 2, 1).reshape(2, 128, 2, 32))
        m["consts"] = _const_tables(c)
        maps.append(m)
    return maps


def kernel(**inp):
    global _NC
    if _NC is None:
        _NC = build()
    maps = _host_inputs(inp)
    res = run_bass_kernel_spmd(_NC, maps, core_ids=list(range(8)))
    R = res.results
    y_prompt = np.stack([np.concatenate([R[b * 4 + j]["yp"] for j in range(4)], 0) for b in range(2)])
    y_sample = np.concatenate([R[c]["ys"].reshape(8, 16, 1024).transpose(1, 0, 2) for c in range(8)], 0)

    def last(name):
        return [R[3][name], R[7][name]]
    ret_p = np.stack([r.reshape(2, 2, 64, 2, 64).transpose(0, 3, 1, 2, 4).reshape(2, 4, 64, 64) for r in last("o_retp")], 1)
    k_p = np.stack([r.reshape(2, 128, 2, 64) for r in last("o_kp")], 1)
    v_p = np.stack([r.reshape(2, 128, 2, 64) for r in last("o_vp")], 1)
    conf_p = np.stack(last("o_confp"), 1)
    sconv_p = np.stack(last("o_scp"), 1)
    ret_s = np.concatenate([R[c]["o_rets"].reshape(2, 2, 64, 2, 16, 64).transpose(0, 4, 3, 1, 2, 5).reshape(2, 16, 4, 64, 64) for c in range(8)], 1)
    k_s = np.concatenate([R[c]["o_ks"].reshape(2, 16, 128, 2, 64) for c in range(8)], 1)
    v_s = np.concatenate([R[c]["o_vs"].reshape(2, 16, 128, 2, 64) for c in range(8)], 1)
    conf_s = np.concatenate([R[c]["o_confs"] for c in range(8)], 1)
    sconv_s = np.concatenate([R[c]["o_scs"] for c in range(8)], 1)
    outs = (y_prompt, y_sample, ret_p, k_p, v_p, conf_p, sconv_p, ret_s, k_s, v_s, conf_s, sconv_s)
    return tuple(np.ascontiguousarray(o, dtype=np.float32) for o in outs)
```

```python
import contextlib
import numpy as np
import concourse.bass as bass
import concourse.mybir as mybir
from concourse.bass_utils import run_bass_kernel_spmd

F32 = mybir.dt.float32
BF = mybir.dt.bfloat16
AF = mybir.ActivationFunctionType
ALU = mybir.AluOpType
AX = mybir.AxisListType

ENGS = ("pe", "act", "dve", "pool", "sp")
EPOCH = 6000
EPS = 1e-6
NT = 17
DEPTH = 2


class Sched:
    def __init__(self, nc):
        self.nc = nc
        self.streams = {e: [] for e in ENGS}
        self.res = {}
        self.seen = {e: {} for e in ENGS}
        self.dma_count = {}
        self.flag = {e: set() for e in ENGS}

    def _need(self, eng, tok, waits):
        if tok is None:
            return
        if tok[0] == 'E':
            _, src, idx = tok
            if src == eng:
                if eng in ("pe", "sp"):
                    return
                if idx < len(self.streams[eng]) - 2:
                    return
            key = ('E', src)
        else:
            _, sem, idx = tok
            key = ('D', sem)
        if self.seen[eng].get(key, -1) >= idx:
            return
        waits[key] = max(waits.get(key, -1), idx)

    def op(self, eng, fn, reads=(), writes=(), dma_sem=None, inc=16, after=()):
        waits = {}
        for r in reads:
            st = self.res.get(r)
            if st is not None:
                self._need(eng, st['w'], waits)
        for w in list(writes) + list(after):
            st = self.res.get(w)
            if st is not None:
                self._need(eng, st['w'], waits)
                for t in st['r']:
                    self._need(eng, t, waits)
        wl = []
        for key, v in waits.items():
            self.seen[eng][key] = v
            if key[0] == 'E':
                self.flag[key[1]].add(v)
            wl.append((key, v))
        idx = len(self.streams[eng])
        if dma_sem is not None:
            c = self.dma_count.get(dma_sem, 0) + inc
            self.dma_count[dma_sem] = c
            tok = ('D', dma_sem, c)
        else:
            tok = ('E', eng, idx)
        self.streams[eng].append({'fn': fn, 'waits': wl, 'dma_sem': dma_sem, 'inc': inc})
        for r in reads:
            st = self.res.setdefault(r, {'w': None, 'r': []})
            st['r'].append(tok)
            if len(st['r']) > 64:
                st['r'] = _compact(st['r'])
        for w in writes:
            self.res[w] = {'w': tok, 'r': []}
        return tok

    def wait_all(self, eng):
        waits = {}
        for r, st in self.res.items():
            self._need(eng, st['w'], waits)
            for t in st['r']:
                self._need(eng, t, waits)
        wl = []
        for key, v in waits.items():
            self.seen[eng][key] = v
            if key[0] == 'E':
                self.flag[key[1]].add(v)
            wl.append((key, v))
        self.streams[eng].append({'fn': None, 'waits': wl, 'dma_sem': None, 'inc': 0})

    def emit(self, es):
        nc = self.nc
        val = {}
        nep = {}
        for e in ENGS:
            c = 0
            for i in range(len(self.streams[e])):
                if i in self.flag[e]:
                    val[(e, i)] = (c // EPOCH, c % EPOCH + 1)
                    c += 1
            nep[e] = (c + EPOCH - 1) // EPOCH
        esem = {}
        for e in ENGS:
            for k in range(nep[e]):
                esem[(e, k)] = es.enter_context(nc.semaphore(f"s_{e}_{k}"))
        dsem = {}
        for s in self.dma_count:
            dsem[s] = es.enter_context(nc.semaphore(f"d_{s}"))
        block = es.enter_context(nc.Block())
        engmap = {"pe": block.tensor, "act": block.scalar, "dve": block.vector,
                  "pool": block.gpsimd, "sp": block.sync}

        def mk(e):
            def body(eng):
                for i, o in enumerate(self.streams[e]):
                    for key, v in o['waits']:
                        if key[0] == 'E':
                            ep, vv = val[(key[1], v)]
                            eng.wait_ge(esem[(key[1], ep)], vv)
                        else:
                            eng.wait_ge(dsem[key[1]], v)
                    if o['fn'] is None:
                        continue
                    ins = o['fn'](eng)
                    if o['dma_sem'] is not None:
                        ins.then_inc(dsem[o['dma_sem']], o['inc'])
                    elif (e, i) in val:
                        ep, vv = val[(e, i)]
                        ins.then_inc(esem[(e, ep)], 1)
            return body
        for e in ENGS:
            if self.streams[e]:
                engmap[e](mk(e))


def _compact(toks):
    best = {}
    for t in toks:
        k = (t[0], t[1])
        if k not in best or best[k][2] < t[2]:
            best[k] = t
    return list(best.values())


C_OFF = {}


def _const_tables(core):
    j = core % 4
    gam = 1.0 - np.exp2(-5.0 - np.arange(4, dtype=np.float64))
    lg = np.log(gam)
    slopes = np.exp2(-8.0 * (np.arange(4, dtype=np.float64) + 1.0) / 4)
    parts = []

    def add(name, arr):
        arr = np.asarray(arr, np.float64).reshape(128, -1)
        C_OFF[name] = (sum(p.shape[1] for p in parts), arr.shape[1])
        parts.append(arr)

    i = np.arange(128)
    m = np.zeros((128, 4, 128))
    for h in range(4):
        d = i[None, :] - i[:, None]
        m[:, h, :] = np.where(d >= 0, np.exp(np.maximum(d, 0) * lg[h]), 0.0)
    add("maskP", m)
    q = np.zeros((128, 2, 128))
    for h in range(4):
        q[(h % 2) * 64:(h % 2) * 64 + 64, h // 2, :] = np.exp((i + 1.0) * lg[h])[None, :]
    add("qdecP", q)
    k = np.zeros((128, 4, 64))
    for h in range(4):
        k[:, h, :] = (0.125 * np.exp((127.0 - i) * lg[h]))[:, None]
    add("kdecP", k)
    c = np.zeros((128, 2))
    for h in range(4):
        c[(h % 2) * 64:(h % 2) * 64 + 64, h // 2] = np.exp(128.0 * lg[h])
    add("cdecP", c)
    t_ = i // 16
    s_ = i % 16
    m = np.zeros((128, 4, 128))
    same = (s_[:, None] == s_[None, :])
    for h in range(4):
        d = t_[None, :] - t_[:, None]
        m[:, h, :] = np.where(same & (d >= 0), np.exp(np.maximum(d, 0) * lg[h]), 0.0)
    add("maskS", m)
    q = np.zeros((128, 2, 128))
    tt = np.arange(128) % 8
    for h in range(4):
        q[(h % 2) * 64:(h % 2) * 64 + 64, h // 2, :] = np.exp((tt + 1.0) * lg[h])[None, :]
    add("qdecS", q)
    k = np.zeros((128, 4, 64))
    for h in range(4):
        k[:, h, :] = (0.125 * np.exp((7.0 - t_) * lg[h]))[:, None]
    add("kdecS", k)
    c = np.zeros((128, 2))
    for h in range(4):
        c[(h % 2) * 64:(h % 2) * 64 + 64, h // 2] = np.exp(8.0 * lg[h])
    add("cdecS", c)
    oh = np.zeros((128, 16))
    oh[i, s_] = 1.0
    add("onehot", oh)
    E = np.zeros((128, 2, 2, 2, 128))
    for kb in range(2):
        jk = kb * 128 + i[:, None]
        iq = 128 + i[None, :]
        dist = iq - jk
        valid = (dist >= 0) & (dist <= 128)
        for kv in range(2):
            for g in range(2):
                E[:, kv, kb, g, :] = np.where(valid, np.exp(-slopes[kv * 2 + g] * dist), 0.0)
    add("Ep", E)
    Es = np.zeros((128, 2, 512))
    d = t_[None, :] - t_[:, None]
    valid = same & (d >= 0)
    for kv in range(2):
        for g in range(2):
            Es[:, kv, g * 128:(g + 1) * 128] = np.where(valid, np.exp(-slopes[kv * 2 + g] * d), 0.0)
            for t in range(8):
                dist = 128 + t - i
                vc_ = (i >= t)
                val = np.where(vc_, np.exp(-slopes[kv * 2 + g] * dist), 0.0)
                for s_i in range(16):
                    Es[:, kv, 256 + s_i * 16 + g * 8 + t] = val
    add("Es", Es)
    sel = np.zeros((128, 4))
    if j > 0:
        sel[:, j - 1] = 1.0
    add("sel", sel)
    cs = np.zeros((128, 4, 2))
    for r in range(4):
        if r < j:
            for h in range(4):
                cs[(h % 2) * 64:(h % 2) * 64 + 64, r, h // 2] = np.exp(128.0 * 16 * (j - 1 - r) * lg[h])
    add("coefS", cs)
    add("hasprev", np.full((128, 1), 1.0 if j > 0 else 0.0))
    add("ident", np.eye(128))
    return np.concatenate(parts, 1).astype(np.float32)


_const_tables(0)
NCONST = sum(v[1] for v in C_OFF.values())

G_S, G_KV, G_CF, G_SC, G_ROWS = 0, 128, 256, 286, 288


def build():
    nc = bass.Bass("TRN2", target_bir_lowering=False)

    def din(name, shape):
        return nc.dram_tensor(name, list(shape), F32, kind="ExternalInput").ap()

    def dout(name, shape):
        return nc.dram_tensor(name, list(shape), F32, kind="ExternalOutput").ap()

    xp = din("xp", [2048, 1024]); xs = din("xs", [128, 1024])
    cP = din("cP", [128, 1024]); cS = din("cS", [128, 1024])
    ada_w = din("ada_w", [2, 1024, 6144]); ada_b = din("ada_b", [2, 1, 6144])
    n1g = din("n1g", [2, 1, 1024]); n2g = din("n2g", [2, 1, 1024])
    w_in = din("w_in", [2, 1024, 2816]); w_out = din("w_out", [2, 1024, 1024])
    w_ff1 = din("w_ff1", [2, 1024, 4096]); w_ff2 = din("w_ff2", [2, 4096, 1024])
    qng = din("qng", [2, 1, 64]); kng = din("kng", [2, 1, 64]); sinks = din("sinks", [2, 1, 4])
    cdwT = din("cdwT", [2, 128, 2, 31]); sdwT = din("sdwT", [2, 128, 2, 3])
    lng = din("lng", [2, 1, 256]); lnb = din("lnb", [2, 1, 256])
    retS = din("retS", [2, 128, 2, 16, 64])
    kcT = din("kcT", [2, 128, 16, 128]); vc = din("vc", [2, 128, 16, 128])
    kc_o = din("kc_o", [2, 16, 128, 128]); vc_o = din("vc_o", [2, 16, 128, 128])
    confT = din("confT", [2, 128, 2, 480]); conf_o = din("conf_o", [2, 16, 30, 256])
    scT = din("scT", [2, 128, 2, 32])
    consts = din("consts", [128, NCONST])

    yp = dout("yp", [2048, 1024]); ys = dout("ys", [128, 1024])
    o_retp = dout("o_retp", [2, 128, 2, 64]); o_kp = dout("o_kp", [2, 128, 128]); o_vp = dout("o_vp", [2, 128, 128])
    o_confp = dout("o_confp", [2, 30, 256]); o_scp = dout("o_scp", [2, 2, 256])
    o_rets = dout("o_rets", [2, 128, 2, 16, 64]); o_ks = dout("o_ks", [2, 16, 128, 128]); o_vs = dout("o_vs", [2, 16, 128, 128])
    o_confs = dout("o_confs", [2, 16, 30, 256]); o_scs = dout("o_scs", [2, 16, 2, 256])

    gin = [nc.dram_tensor(f"gin{l}", [G_ROWS, 256], F32, kind="Internal").ap() for l in range(2)]
    gout = [nc.dram_tensor(f"gout{l}", [4 * G_ROWS, 256], F32, kind="Internal").ap() for l in range(2)]

    S = Sched(nc)
    es = contextlib.ExitStack()
    with es:
        def sb(name, shape, dt=F32):
            return es.enter_context(nc.sbuf_tensor(name, list(shape), dt))

        def ps(name, shape, dt=F32):
            return es.enter_context(nc.psum_tensor(name, list(shape), dt))

        def V(fn, r=(), w=(), **k): return S.op("dve", fn, r, w, **k)
        def A(fn, r=(), w=(), **k): return S.op("act", fn, r, w, **k)
        def G(fn, r=(), w=(), **k): return S.op("pool", fn, r, w, **k)
        def T(fn, r=(), w=(), **k): return S.op("pe", fn, r, w, **k)
        def D(fn, r=(), w=(), sem="ld", **k): return S.op("sp", fn, r, w, dma_sem=sem, **k)
        def DG(fn, r=(), w=(), sem="ldg", **k): return S.op("pool", fn, r, w, dma_sem=sem, **k)

        x = sb("x", [128, NT, 1024])
        gateP = sb("gateP", [128, 1024])
        modPT = sb("modPT", [128, 16])
        R = sb("R", [128, 36352], BF)
        win = R[:, 0:22528].rearrange("p (k n) -> p k n", k=8)
        wout = R[:, 22528:30720].rearrange("p (k n) -> p k n", k=8)
        z = R[:, 30720:36352].bitcast(F32)
        h2T = R[:, 0:9216].rearrange("p (k n) -> p k n", k=8)
        W1g = [R[:, 9216 + i * 4096: 9216 + (i + 1) * 4096].rearrange("p (k n) -> p k n", k=8) for i in range(2)]
        W2g = [R[:, 17408 + i * 4096: 17408 + (i + 1) * 4096].rearrange("p (k n) -> p k n", k=4) for i in range(2)]
        uT = [R[:, 25600 + i * 2048: 25600 + (i + 1) * 2048].rearrange("p (k n) -> p k n", k=4) for i in range(2)]
        adaw = W1g
        MIX_NAMES = [f"win_{n}" for n in range(6)] + ["wout_0", "wout_1"] + [f"z{n}" for n in range(6)]
        FFN_NAMES = ["h2T", "W1g0", "W1g1", "W2g0", "W2g1", "uT0", "uT1"]

        identf = sb("identf", [128, 128]); identb = sb("identb", [128, 128], BF)
        cst = {}
        for nm, dt in [("maskP", F32), ("qdecP", BF), ("kdecP", F32), ("cdecP", F32), ("maskS", F32), ("qdecS", BF),
                       ("kdecS", F32), ("cdecS", F32), ("onehot", BF), ("Ep", BF), ("Es", BF),
                       ("sel", F32), ("coefS", F32), ("hasprev", F32)]:
            cst[nm] = sb("c_" + nm, [128, C_OFF[nm][1]], dt)
        cPT = sb("cPT", [128, 8, 128], BF); cST = sb("cST", [128, 8, 128], BF)
        adab = sb("adab", [1, 512]); ones1 = sb("ones1", [1, 128])
        gqk = sb("gqk", [128, 128])
        esink = sb("esink", [128, 4])
        lngb = sb("lngb", [128, 512])
        cw = sb("cw", [128, 2, 31]); sw = sb("sw", [128, 2, 3])
        hb = sb("hb", [128, 1024], BF)
        hT = sb("hT", [128, 8, 128], BF)
        ocs = hT[:].rearrange("p a b -> p (a b)").bitcast(F32).rearrange("p (a b) -> p a b", a=4)
        tmpf = sb("tmpf", [128, 1024])
        gsel = tmpf[:].rearrange("p (a b) -> p a b", a=4)
        st = sb("st", [128, 64])
        tr = sb("tr", [128, 6, 128], BF)
        trf = tr[:].rearrange("p a b -> p (a b)")
        qdT = sb("qdT", [128, 2, 128], BF)
        kTa = [sb(f"kTa{i}", [128, 128], BF) for i in range(2)]
        vaug = [sb(f"vaug{i}", [128, 2, 66], BF) for i in range(2)]
        rb = sb("rb", [128, 4, 256], BF)
        attm = sb("attm", [128, 512], BF)
        qn = sb("qn", [128, 256], BF)
        knf = sb("knf", [128, 128]); knb = sb("knb", [128, 128], BF)
        eepp = sb("eepp", [128, 2048], BF)
        ee = eepp[:, 0:1024]; pp = eepp[:, 1024:2048]
        EEPP = ["ee0", "ee1", "pp0", "pp1"]
        ef32 = eepp[:].bitcast(F32)
        t512 = [ef32[:, 0:512], ef32[:, 512:1024]]
        gb = ef32
        S0b = eepp[:].rearrange("p (a s e) -> p a s e", a=2, s=16)
        mixf = sb("mixf", [128, 1024])
        sg = mixf[:, 0:256]; u = mixf[:, 256:512]; vsf = mixf[:, 512:768]; t256 = mixf[:, 768:1024]
        gateS = mixf
        MIXF = ["sg", "u", "vsf", "_sa"]
        t256b = sb("t256b", [128, 256])
        extu = sb("extu", [128, 2, 608]); extv = sb("extv", [128, 2, 160])
        accs = sb("accs", [128, 4, 128])
        acc = accs[:, 0:2, :]; accv = accs[:, 2:4, :]
        vbdf = accs[:].rearrange("p a b -> p (a b)").bitcast(BF)
        vbd = vbdf.rearrange("p (s e) -> p s e", s=16)
        Sst = sb("Sst", [128, 2, 64]); Sbf = sb("Sbf", [128, 2, 64], BF)
        samp = sb("samp", [128, 4160], BF)
        S0 = samp[:, 0:4096].bitcast(F32).rearrange("p (a s e) -> p a s e", a=2, s=16)
        kcTb = samp[:, 0:2048].rearrange("p (s t) -> p s t", s=16)
        vcb = samp[:, 2048:4160].rearrange("p (s k e) -> p s k e", s=16, k=2)
        trs = sb("trs", [128, 4, 128], BF)
        modS_d = [[nc.dram_tensor(f"modS_{l}_{h}", [128, 3072], F32, kind="Internal").ap() for h in range(2)] for l in range(2)]

        B0 = ps("B0", [128, 512]); B1 = ps("B1", [128, 512])
        B2 = ps("B2", [128, 1024], BF)
        B3 = ps("B3", [128, 512])
        B4 = ps("B4", [128, 512]); B5 = ps("B5", [128, 512])
        B6 = ps("B6", [128, 512]); B7 = ps("B7", [128, 512])
        ZB = [B0, B1]
        B3N = ["B3a", "B3b"]; B7N = ["B7a", "B7b"]

        def cv(name):
            return cst[name]

        TF = ["tmpf0", "tmpf1"]
        SSTN = ["Sst00", "Sst01", "Sst10", "Sst11"]
        HB = ["hb", "cat0", "cat1", "cat2", "cat3"]
        last = None
        for nm in cst:
            o, n = C_OFF[nm]
            if cst[nm].dtype == BF:
                last = DG(lambda e, nm=nm, o=o, n=n: e.dma_start(out=cst[nm][:], in_=consts[:, o:o + n], allow_slow_non_contiguous=(n == 1)), w=["c_" + nm], sem="ldc")
            else:
                last = DG(lambda e, nm=nm, o=o, n=n: e.dma_start(out=cst[nm][:], in_=consts[:, o:o + n], allow_slow_non_contiguous=(n == 1)), w=["c_" + nm], sem="ldc")
        for nm in cst:
            S.res["c_" + nm]['w'] = last
        o_id, n_id = C_OFF["ident"]
        D(lambda e: e.dma_start(out=identf[:], in_=consts[:, o_id:o_id + n_id]), w=["identf"], sem="ld_identf")
        V(lambda e: e.tensor_copy(out=identb[:], in_=identf[:]), r=["identf"], w=["identb"])
        G(lambda e: e.memset(ones1[:], 1.0), w=["ones1"])
        for i in range(2):
            G(lambda e, i=i: e.memset(vaug[i][:], 1.0), w=[f"vaug{i}"])
        lastx = None
        for ti in range(16):
            lastx = D(lambda e, ti=ti: e.dma_start(out=x[:, ti, :], in_=xp[ti * 128:(ti + 1) * 128, :]), w=[f"x{ti}"], sem="ldx")
        lastx = D(lambda e: e.dma_start(out=x[:, 16, :], in_=xs), w=["x16"], sem="ldx")
        for ti in range(17):
            S.res[f"x{ti}"]['w'] = lastx

        for (cd, cT, nm) in [(cP, cPT, "cPT"), (cS, cST, "cST")]:
            D(lambda e, cd=cd: e.dma_start(out=tmpf[:], in_=cd), w=TF, sem="ld_tmpf")
            A(lambda e: e.activation(out=ef32[:], in_=tmpf[:], func=AF.Exp, scale=-1.0), r=TF, w=EEPP)
            G(lambda e: e.tensor_scalar_add(out=ef32[:], in0=ef32[:], scalar1=1.0), r=EEPP, w=EEPP)
            V(lambda e: e.reciprocal(out=mixf[:], in_=ef32[:]), r=EEPP, w=MIXF)
            V(lambda e: e.tensor_tensor(out=hb[:], in0=tmpf[:], in1=mixf[:], op=ALU.mult), r=TF + MIXF, w=["hb"])
            for k in range(8):
                T(lambda e, k=k: e.transpose(B2[:, k * 128:(k + 1) * 128], hb[:, k * 128:(k + 1) * 128], identb[:]), r=["hb", "identb"], w=["B2"])
            A(lambda e, cT=cT: e.activation(out=cT[:].rearrange("p k n -> p (k n)"), in_=B2[:], func=AF.Copy), r=["B2"], w=[nm])

        def rsqrt_small(dst, src, scale, rn, wn):
            A(lambda e: e.activation(out=dst, in_=src, func=AF.Ln, scale=scale, bias=EPS), r=rn, w=wn)
            A(lambda e: e.activation(out=dst, in_=dst, func=AF.Exp, scale=-0.5), r=wn, w=wn)

        def mod_phase(l, half):
            g_d = n1g if half == 0 else n2g
            D(lambda e: e.dma_start(out=gb[:], in_=g_d[l].partition_broadcast(128)), w=EEPP, sem="ld_gb")
            for n6 in range(6):
                n = half * 6 + n6
                b = n6 % 2
                DG(lambda e, n=n, b=b: e.dma_start(out=adaw[b][:], in_=ada_w[l][:, n * 512:(n + 1) * 512].rearrange("(k p) n -> p k n", p=128)),
                   w=[f"W1g{b}"], sem=f"ldf{b}", after=MIX_NAMES)
                D(lambda e, n=n: e.dma_start(out=adab[:], in_=ada_b[l][:, n * 512:(n + 1) * 512]), w=["adab"], sem="ld_adab")
                kind = n6 // 2
                c0 = (n6 % 2) * 512
                for gi, (cT, cn) in enumerate([(cPT, "cPT"), (cST, "cST")]):
                    bank = ZB[gi]
                    bn = f"B{gi}"
                    for k in range(8):
                        T(lambda e, k=k, cT=cT, bank=bank, b=b: e.matmul(bank[:], lhsT=cT[:, k, :], rhs=adaw[b][:, k, :], start=(k == 0), stop=False),
                          r=[cn, f"W1g{b}"], w=[bn])
                    T(lambda e, bank=bank: e.matmul(bank[:], lhsT=ones1[:], rhs=adab[:], start=False, stop=True), r=["ones1", "adab"], w=[bn])
                    if gi == 1:
                        stg = tmpf[:, 512:1024]
                        if kind == 1:
                            V(lambda e, bank=bank, c0=c0, stg=stg: e.scalar_tensor_tensor(out=stg, in0=bank[:], scalar=1.0, in1=gb[:, c0:c0 + 512], op0=ALU.add, op1=ALU.mult),
                              r=[bn] + EEPP, w=["tmpf1"])
                        else:
                            A(lambda e, bank=bank, stg=stg: e.activation(out=stg, in_=bank[:], func=AF.Copy), r=[bn], w=["tmpf1"])
                        D(lambda e, c0=c0, kind=kind, stg=stg: e.dma_start(out=modS_d[l][half][:, kind * 1024 + c0:kind * 1024 + c0 + 512], in_=stg), r=["tmpf1"], w=["modS_d"], sem="st_mod")
                    else:
                        if kind == 2:
                            A(lambda e, bank=bank, c0=c0: e.activation(out=gateP[:, c0:c0 + 512], in_=bank[:], func=AF.Copy), r=[bn], w=["gateP"])
                        else:
                            if kind == 1:
                                V(lambda e, bank=bank, c0=c0: e.scalar_tensor_tensor(out=tmpf[:, 0:512], in0=bank[:], scalar=1.0, in1=gb[:, c0:c0 + 512], op0=ALU.add, op1=ALU.mult),
                                  r=[bn] + EEPP, w=["tmpf0"])
                            else:
                                A(lambda e, bank=bank: e.activation(out=tmpf[:, 0:512], in_=bank[:], func=AF.Copy), r=[bn], w=["tmpf0"])
                            for q4 in range(4):
                                T(lambda e, q4=q4: e.transpose(B3[:, q4 * 128:(q4 + 1) * 128], tmpf[:, q4 * 128:(q4 + 1) * 128], identf[:]), r=["tmpf0", "identf"], w=B3N)
                            col = kind * 8 + (n6 % 2) * 4
                            V(lambda e, col=col: e.tensor_copy(out=modPT[:, col:col + 4], in_=B3[:].rearrange("p (a b) -> p a b", a=4)[:, :, 0]), r=B3N, w=["modPT"])

        def emit_h(ti, l, half):
            xn = f"x{ti}"
            A(lambda e: e.activation(out=hb[:], in_=x[:, ti, :], func=AF.Square, accum_out=st[:, 0:1]), r=[xn], w=HB + ["st0"])
            rsqrt_small(st[:, 1:2], st[:, 0:1], 1.0 / 1024, ["st0"], ["st1"])
            if ti < 16:
                V(lambda e: e.tensor_scalar(out=hb[:], in0=x[:, ti, :], scalar1=st[:, 1:2], scalar2=None, op0=ALU.mult), r=[xn, "st1"], w=HB)
            else:
                D(lambda e: e.dma_start(out=tmpf[:], in_=modS_d[l][half][:, 1024:2048]), r=["modS_d"], w=TF, sem="ld_tmpf")
                D(lambda e: e.dma_start(out=ef32[:], in_=modS_d[l][half][:, 0:1024]), r=["modS_d"], w=EEPP, sem="ld_gb")
                V(lambda e: e.scalar_tensor_tensor(out=tmpf[:], in0=x[:, ti, :], scalar=st[:, 1:2], in1=tmpf[:], op0=ALU.mult, op1=ALU.mult),
                  r=[xn, "st1"] + TF, w=TF)
                G(lambda e: e.tensor_tensor(out=hb[:], in0=tmpf[:], in1=ef32[:], op=ALU.add), r=TF + EEPP, w=HB)
            for k in range(8):
                T(lambda e, k=k: e.transpose(B2[:, k * 128:(k + 1) * 128], hb[:, k * 128:(k + 1) * 128], identb[:]), r=["hb", "identb"], w=["B2"])

        def evac_hT(ti, dst, dname, after=()):
            if ti < 16:
                for k in range(8):
                    if k % 2 == 0:
                        A(lambda e, k=k: e.activation(out=dst[:, k, :], in_=B2[:, k * 128:(k + 1) * 128], func=AF.Identity, scale=modPT[:, 8 + k:9 + k], bias=modPT[:, k:k + 1]),
                          r=["B2", "modPT"], w=[dname], after=after)
                    else:
                        V(lambda e, k=k: e.tensor_scalar(out=dst[:, k, :], in0=B2[:, k * 128:(k + 1) * 128], scalar1=modPT[:, 8 + k:9 + k], scalar2=modPT[:, k:k + 1], op0=ALU.mult, op1=ALU.add),
                          r=["B2", "modPT"], w=[dname], after=after)
            else:
                A(lambda e: e.activation(out=dst, in_=B2[:].rearrange("p (k n) -> p k n", k=8), func=AF.Copy), r=["B2"], w=[dname], after=after)

        def emit_z(chunks):
            for n in chunks:
                c0 = n * 512
                w_ = min(512, 2816 - c0)
                bank = ZB[n % 2]; bn = f"B{n % 2}"
                for k in range(8):
                    T(lambda e, k=k, bank=bank, c0=c0, w_=w_: e.matmul(bank[:, 0:w_], lhsT=hT[:, k, :], rhs=win[:, k, c0:c0 + w_], start=(k == 0), stop=(k == 7)),
                      r=["hT", f"win_{n}"], w=[bn])
                A(lambda e, bank=bank, c0=c0, w_=w_: e.activation(out=z[:, c0:c0 + w_], in_=bank[:, 0:w_], func=AF.Copy), r=[bn], w=[f"z{n}"], after=FFN_NAMES)

        def sigmoid_parts(src, rn):
            A(lambda e: e.activation(out=t256, in_=src, func=AF.Exp, scale=-1.0), r=rn, w=["_sa"])
            V(lambda e: e.tensor_scalar_add(out=t256, in0=t256, scalar1=1.0), r=["_sa"], w=["_sa"])
            V(lambda e: e.reciprocal(out=t256b[:], in_=t256), r=["_sa"], w=["_sb"])

        def emit_local_ret(ti):
            kd_c = cst["kdecP"] if ti < 16 else cst["kdecS"]
            kdn = "c_kdecP" if ti < 16 else "c_kdecS"
            A(lambda e: e.activation(out=rb[:, 0, :], in_=z[:, 0:256], func=AF.Copy), r=["z0"], w=["rb0"])
            A(lambda e: e.activation(out=rb[:, 1, :], in_=z[:, 256:512], func=AF.Copy, scale=0.125), r=["z0"], w=["rb1"])
            V(lambda e: e.tensor_tensor(out=rb[:, 2, :], in0=z[:, 256:512], in1=kd_c[:], op=ALU.mult), r=["z0", kdn], w=["rb2"])
            A(lambda e: e.activation(out=rb[:, 3, :], in_=z[:, 512:768], func=AF.Copy), r=["z1"], w=["rb3"])

        def emit_su():
            for pr in range(2):
                T(lambda e, pr=pr: e.matmul(B6[:, 256 + pr * 128:256 + (pr + 1) * 128], lhsT=rb[:, 2, pr * 128:(pr + 1) * 128], rhs=rb[:, 3, pr * 128:(pr + 1) * 128], start=True, stop=True),
                  r=["rb2", "rb3"], w=["SU"])

        def state_update(cdec, cn):
            for pr in range(2):
                for hf in range(2):
                    rs = slice(hf * 64, hf * 64 + 64)
                    V(lambda e, pr=pr, hf=hf, rs=rs: e.scalar_tensor_tensor(out=Sst[rs, pr, :], in0=Sst[rs, pr, :], scalar=cdec[rs, pr:pr + 1],
                                                                                in1=B6[rs, 256 + pr * 128 + hf * 64:256 + pr * 128 + hf * 64 + 64], op0=ALU.mult, op1=ALU.add),
                      r=[f"Sst{pr}{hf}", "SU", cn], w=[f"Sst{pr}{hf}"])
            A(lambda e: e.activation(out=Sbf[:], in_=Sst[:], func=AF.Copy), r=SSTN, w=["Sbf"])

        def emit_local_rest(ti, cur):
            sigmoid_parts(z[:, 768:1024], ["z1"])
            V(lambda e: e.tensor_tensor(out=sg, in0=z[:, 768:1024], in1=t256b[:], op=ALU.mult), r=["z1", "_sb"], w=["sg"])
            V(lambda e: e.tensor_tensor(out=tmpf[:, 0:384], in0=z[:, 1024:1408], in1=z[:, 1024:1408], op=ALU.mult), r=["z2"], w=["tmpf0"])
            V(lambda e: e.tensor_reduce(out=st[:, 8:14], in_=tmpf[:, 0:384].rearrange("p (a b) -> p a b", a=6), axis=AX.X, op=ALU.add), r=["tmpf0"], w=["st8"])
            rsqrt_small(st[:, 8:14], st[:, 8:14], 1.0 / 64, ["st8"], ["st8"])
            V(lambda e: e.tensor_tensor(out=tmpf[:, 0:256].rearrange("p (a b) -> p a b", a=4), in0=z[:, 1024:1280].rearrange("p (a b) -> p a b", a=4),
                                        in1=st[:, 8:12][:, :, None].broadcast_to([128, 4, 64]), op=ALU.mult), r=["z2", "st8"], w=["tmpf0"])
            for k2 in range(2):
                for g2 in range(2):
                    hh = k2 * 2 + g2
                    V(lambda e, k2=k2, g2=g2, hh=hh: e.tensor_tensor(out=qn[:, g2 * 128 + k2 * 64:g2 * 128 + k2 * 64 + 64], in0=tmpf[:, hh * 64:(hh + 1) * 64], in1=gqk[:, 0:64], op=ALU.mult),
                      r=["tmpf0", "gqk"], w=["qn"])
            V(lambda e: e.tensor_tensor(out=tmpf[:, 256:384].rearrange("p (a b) -> p a b", a=2), in0=z[:, 1280:1408].rearrange("p (a b) -> p a b", a=2),
                                        in1=st[:, 12:14][:, :, None].broadcast_to([128, 2, 64]), op=ALU.mult), r=["z2", "st8"], w=["tmpf0"])
            for k2 in range(2):
                V(lambda e, k2=k2: e.tensor_tensor(out=knf[:, k2 * 64:(k2 + 1) * 64], in0=tmpf[:, 256 + k2 * 64:256 + (k2 + 1) * 64], in1=gqk[:, 64:128], op=ALU.mult), r=["tmpf0", "gqk"], w=["knf"])
            A(lambda e: e.activation(out=knb[:], in_=knf[:], func=AF.Copy), r=["knf"], w=["knb"])
            A(lambda e: e.activation(out=vaug[cur][:, :, 0:64], in_=z[:, 1408:1536].rearrange("p (a b) -> p a b", a=2), func=AF.Copy), r=["z2"], w=[f"vaug{cur}"])
            sigmoid_parts(z[:, 1792:2048], ["z3"])
            V(lambda e: e.tensor_tensor(out=u, in0=z[:, 1536:1792], in1=t256b[:], op=ALU.mult), r=["z3", "_sb"], w=["u"])
            V(lambda e: e.tensor_tensor(out=vsf, in0=z[:, 2304:2560], in1=z[:, 2560:2816], op=ALU.mult), r=["z4", "z5"], w=["vsf"])

        def emit_transposes(ti, cur):
            P_ = ti < 16
            hu = 30 if P_ else 480
            hv = 2 if P_ else 32
            srcs = [(rb[:, 0, 0:128], "rb0"), (rb[:, 0, 128:256], "rb0"), (rb[:, 1, 0:128], "rb1"), (rb[:, 1, 128:256], "rb1"),
                    (qn[:, 0:128], "qn"), (qn[:, 128:256], "qn"), (knb[:], "knb")]
            for i, (ap, nm) in enumerate(srcs):
                T(lambda e, i=i, ap=ap: e.transpose(B2[:, i * 128:(i + 1) * 128], ap, identb[:]), r=[nm, "identb"], w=["B2"])
            A(lambda e: e.activation(out=tr[:].rearrange("p a b -> p (a b)"), in_=B2[:, 0:768], func=AF.Copy), r=["B2"], w=["tr"])
            A(lambda e: e.activation(out=kTa[cur][:], in_=B2[:, 768:896], func=AF.Copy), r=["B2"], w=[f"kTa{cur}"])
            if P_:
                V(lambda e: e.tensor_tensor(out=qdT[:].rearrange("p a b -> p (a b)"), in0=B2[:, 0:256], in1=cst["qdecP"][:], op=ALU.mult), r=["B2", "c_qdecP"], w=["qdT"])
            for c in range(2):
                T(lambda e, c=c: e.transpose(B3[:, c * 128:(c + 1) * 128], u[:, c * 128:(c + 1) * 128], identf[:]), r=["u", "identf"], w=B3N)
            for c in range(2):
                T(lambda e, c=c: e.transpose(B3[:, 256 + c * 128:256 + (c + 1) * 128], vsf[:, c * 128:(c + 1) * 128], identf[:]), r=["vsf", "identf"], w=B3N)
            A(lambda e: e.activation(out=extu[:, :, hu:hu + 128], in_=B3[:, 0:256].rearrange("p (a b) -> p a b", a=2), func=AF.Copy), r=B3N, w=["extu_new"])
            A(lambda e: e.activation(out=extv[:, :, hv:hv + 128], in_=B3[:, 256:512].rearrange("p (a b) -> p a b", a=2), func=AF.Copy), r=B3N, w=["extv_new"])

        def emit_conv(ti):
            P_ = ti < 16
            stp = 1 if P_ else 16
            for jj in range(31):
                for c in range(2):
                    dst = acc if jj % 2 == 0 else accv
                    dn = f"acc{c}" if jj % 2 == 0 else f"accv{c}"
                    if jj < 2:
                        V(lambda e, c=c, jj=jj, dst=dst: e.tensor_scalar(out=dst[:, c, :], in0=extu[:, c, jj * stp:jj * stp + 128], scalar1=cw[:, c, jj:jj + 1], scalar2=None, op0=ALU.mult),
                          r=["extu_new", "extu_halo", "cw"], w=[dn])
                    else:
                        V(lambda e, c=c, jj=jj, dst=dst: e.scalar_tensor_tensor(out=dst[:, c, :], in0=extu[:, c, jj * stp:jj * stp + 128], scalar=cw[:, c, jj:jj + 1], in1=dst[:, c, :], op0=ALU.mult, op1=ALU.add),
                          r=["extu_new", "extu_halo", "cw", dn], w=[dn])
            for c in range(2):
                V(lambda e, c=c: e.tensor_tensor(out=acc[:, c, :], in0=acc[:, c, :], in1=accv[:, c, :], op=ALU.add), r=[f"acc{c}", f"accv{c}"], w=[f"acc{c}"])
            for jj in range(3):
                for c in range(2):
                    if jj == 0:
                        V(lambda e, c=c: e.tensor_scalar(out=accv[:, c, :], in0=extv[:, c, 0:128], scalar1=sw[:, c, 0:1], scalar2=None, op0=ALU.mult),
                          r=["extv_new", "extv_halo", "sw"], w=[f"accv{c}"])
                    else:
                        V(lambda e, c=c, jj=jj: e.scalar_tensor_tensor(out=accv[:, c, :], in0=extv[:, c, jj * stp:jj * stp + 128], scalar=sw[:, c, jj:jj + 1], in1=accv[:, c, :], op0=ALU.mult, op1=ALU.add),
                          r=["extv_new", "extv_halo", "sw", f"accv{c}"], w=[f"accv{c}"])
            if ti == 0: ck(4.41)
            if P_:
                G(lambda e: e.tensor_copy(out=extu[:, :, 0:30], in_=extu[:, :, 128:158]), r=["extu_new", "acc0", "acc1"], w=["extu_halo"])
                G(lambda e: e.tensor_copy(out=extv[:, :, 0:2], in_=extv[:, :, 128:130]), r=["extv_new", "accv0", "accv1"], w=["extv_halo"])
            if ti == 0: ck(4.42)
            for c in range(2):
                T(lambda e, c=c: e.transpose(B3[:, c * 128:(c + 1) * 128], acc[:, c, :], identf[:]), r=[f"acc{c}", "identf"], w=B3N)
            for c in range(2):
                T(lambda e, c=c: e.transpose(B3[:, 256 + c * 128:256 + (c + 1) * 128], accv[:, c, :], identf[:]), r=[f"accv{c}", "identf"], w=B3N)
            V(lambda e: e.tensor_tensor(out=hb[:, 768:1024], in0=z[:, 2048:2304], in1=B3[:, 256:512], op=ALU.mult), r=["z4"] + B3N, w=["cat3"], after=["hb"])
            if ti == 0: ck(4.43)
            V(lambda e: e.tensor_copy(out=t256, in_=B3[:, 0:256]), r=B3N, w=["_sa"])
            if ti == 0: ck(4.431)
            V(lambda e: e.tensor_reduce(out=st[:, 56:58], in_=t256.rearrange("p (a b) -> p a b", a=2), axis=AX.X, op=ALU.add), r=["_sa"], w=["st56"])
            if ti == 0: ck(4.432)
            A(lambda e: e.activation(out=t256b[:], in_=t256, func=AF.Square), r=["_sa"], w=["_sb"])
            V(lambda e: e.tensor_reduce(out=st[:, 58:60], in_=t256b[:].rearrange("p (a b) -> p a b", a=2), axis=AX.X, op=ALU.add), r=["_sb"], w=["st58"])
            if ti == 0: ck(4.435)
            V(lambda e: e.tensor_tensor(out=st[:, 16:17], in0=st[:, 56:57], in1=st[:, 57:58], op=ALU.add), r=["st56"], w=["st16"])
            V(lambda e: e.tensor_tensor(out=st[:, 17:18], in0=st[:, 58:59], in1=st[:, 59:60], op=ALU.add), r=["st58"], w=["st17"])
            V(lambda e: e.tensor_scalar(out=st[:, 18:19], in0=st[:, 16:17], scalar1=1.0 / 256, scalar2=None, op0=ALU.mult), r=["st16"], w=["st18"])
            V(lambda e: e.tensor_tensor(out=st[:, 19:20], in0=st[:, 18:19], in1=st[:, 18:19], op=ALU.mult), r=["st18"], w=["st19"])
            V(lambda e: e.scalar_tensor_tensor(out=st[:, 20:21], in0=st[:, 17:18], scalar=1.0 / 256, in1=st[:, 19:20], op0=ALU.mult, op1=ALU.subtract), r=["st17", "st19"], w=["st20"])
            if ti == 0: ck(4.437)
            rsqrt_small(st[:, 21:22], st[:, 20:21], 1.0, ["st20"], ["st21"])
            if ti == 0: ck(4.44)
            V(lambda e: e.tensor_scalar(out=t256, in0=t256, scalar1=st[:, 18:19], scalar2=st[:, 21:22], op0=ALU.subtract, op1=ALU.mult), r=["_sa", "st18", "st21"], w=["_sa"])
            V(lambda e: e.tensor_tensor(out=t256, in0=t256, in1=lngb[:, 0:256], op=ALU.mult), r=["_sa", "lngb"], w=["_sa"])
            V(lambda e: e.tensor_tensor(out=t256, in0=t256, in1=lngb[:, 256:512], op=ALU.add), r=["_sa", "lngb"], w=["_sa"])
            A(lambda e: e.activation(out=t256b[:], in_=t256, func=AF.Exp, scale=-1.0), r=["_sa"], w=["_sb"])
            V(lambda e: e.tensor_scalar_add(out=t256b[:], in0=t256b[:], scalar1=1.0), r=["_sb"], w=["_sb"])
            V(lambda e: e.reciprocal(out=t256b[:], in_=t256b[:]), r=["_sb"], w=["_sb"])
            V(lambda e: e.tensor_tensor(out=hb[:, 512:768], in0=t256, in1=t256b[:], op=ALU.mult), r=["_sa", "_sb"], w=["cat2"], after=["hb"])

        def emit_groupnorm_out(o_ap, rnames):
            o3 = o_ap.rearrange("p (a b) -> p a b", a=4)
            tq = tmpf[:, 512:768]
            tq3 = tq.rearrange("p (a b) -> p a b", a=4)
            V(lambda e: e.tensor_reduce(out=st[:, 24:28], in_=o3, axis=AX.X, op=ALU.add), r=rnames, w=["st24"])
            A(lambda e: e.activation(out=tq, in_=o_ap, func=AF.Square), r=rnames, w=["tmpf1"])
            V(lambda e: e.tensor_reduce(out=st[:, 28:32], in_=tq3, axis=AX.X, op=ALU.add), r=["tmpf1"], w=["st28"])
            V(lambda e: e.tensor_scalar(out=st[:, 32:36], in0=st[:, 24:28], scalar1=1.0 / 64, scalar2=None, op0=ALU.mult), r=["st24"], w=["st32"])
            V(lambda e: e.tensor_tensor(out=st[:, 36:40], in0=st[:, 32:36], in1=st[:, 32:36], op=ALU.mult), r=["st32"], w=["st36"])
            V(lambda e: e.scalar_tensor_tensor(out=st[:, 40:44], in0=st[:, 28:32], scalar=1.0 / 64, in1=st[:, 36:40], op0=ALU.mult, op1=ALU.subtract), r=["st28", "st36"], w=["st40"])
            rsqrt_small(st[:, 44:48], st[:, 40:44], 1.0, ["st40"], ["st44"])
            V(lambda e: e.tensor_tensor(out=tq3, in0=o3, in1=st[:, 32:36][:, :, None].broadcast_to([128, 4, 64]), op=ALU.subtract), r=rnames + ["st32"], w=["tmpf1"])
            V(lambda e: e.tensor_tensor(out=tq3, in0=tq3, in1=st[:, 44:48][:, :, None].broadcast_to([128, 4, 64]), op=ALU.mult), r=["tmpf1", "st44"], w=["tmpf1"])
            V(lambda e: e.tensor_tensor(out=hb[:, 0:256], in0=tq, in1=sg, op=ALU.mult), r=["tmpf1", "sg"], w=["cat0"], after=["hb"])

        def emit_attn_finish(oa_ap, rnames):
            V(lambda e: e.tensor_tensor(out=st[:, 48:52], in0=oa_ap[:, :, 64], in1=esink[:], op=ALU.add), r=rnames + ["esink"], w=["st48"])
            V(lambda e: e.reciprocal(out=st[:, 52:56], in_=st[:, 48:52]), r=["st48"], w=["st52"])
            V(lambda e: e.tensor_tensor(out=hb[:, 256:512].rearrange("p (a b) -> p a b", a=4), in0=oa_ap[:, :, 0:64], in1=st[:, 52:56][:, :, None].broadcast_to([128, 4, 64]), op=ALU.mult),
              r=rnames + ["st52"], w=["cat1"], after=["hb"])

        def emit_out(ti):
            for k in range(8):
                T(lambda e, k=k: e.transpose(B2[:, k * 128:(k + 1) * 128], hb[:, k * 128:(k + 1) * 128], identb[:]), r=["cat0", "cat1", "cat2", "cat3", "identb"], w=["B2"])
            A(lambda e: e.activation(out=hT[:].rearrange("p a b -> p (a b)"), in_=B2[:], func=AF.Copy), r=["B2"], w=["hT"])
            gate = gateP if ti < 16 else gateS
            gn = ["gateP"] if ti < 16 else MIXF
            for hf in range(2):
                bank = [B4, B5][hf]; bn = f"B{4 + hf}"
                for k in range(8):
                    T(lambda e, k=k, bank=bank, hf=hf: e.matmul(bank[:], lhsT=hT[:, k, :], rhs=wout[:, k, hf * 512:(hf + 1) * 512], start=(k == 0), stop=(k == 7)), r=["hT", f"wout_{hf}"], w=[bn])
                V(lambda e, bank=bank, hf=hf, gate=gate: e.tensor_tensor(out=tmpf[:, hf * 512:(hf + 1) * 512], in0=bank[:], in1=gate[:, hf * 512:(hf + 1) * 512], op=ALU.mult), r=[bn] + gn, w=[f"tmpf{hf}"])
                G(lambda e, hf=hf: e.tensor_tensor(out=x[:, ti, hf * 512:(hf + 1) * 512], in0=x[:, ti, hf * 512:(hf + 1) * 512], in1=tmpf[:, hf * 512:(hf + 1) * 512], op=ALU.add),
                  r=[f"tmpf{hf}", f"x{ti}"], w=[f"x{ti}"])
            if os.environ.get("KDUMPMIX") and ti < 16:
                D(lambda e: e.dma_start(out=yp[ti * 128:(ti + 1) * 128, :], in_=tmpf[:]), r=TF, w=["yp"], sem="st_x")

        import os
        STOP = float(os.environ.get("KSTOP", "99"))

        class _Stop(Exception):
            pass

        def ck(n):
            if STOP <= n:
                raise _Stop()
        try:
          ck(1)
          def emit_layer(l):
            mod_phase(l, 0)
            if os.environ.get("KDUMPGATE") and l == 0:
                D(lambda e: e.dma_start(out=yp[0:128, :], in_=gateP[:]), r=["gateP"], w=["yp"], sem="st_x")
                V(lambda e: e.tensor_copy(out=tmpf[:, 0:16], in_=modPT[:]), r=["modPT"], w=["tmpf0"])
                D(lambda e: e.dma_start(out=yp[128:256, 0:16], in_=tmpf[:, 0:16]), r=["tmpf0"], w=["yp"], sem="st_x")
            ck(2)
            for n in range(6):
                c0 = n * 512; w_ = min(512, 2816 - c0)
                DG(lambda e, c0=c0, w_=w_: e.dma_start(out=win[:, :, c0:c0 + w_], in_=w_in[l][:, c0:c0 + w_].rearrange("(k p) n -> p k n", p=128)),
                   w=[f"win_{n}"], sem="ldw", after=FFN_NAMES)
            lastw = None
            for hf in range(2):
                lastw = DG(lambda e, hf=hf: e.dma_start(out=wout[:, :, hf * 512:(hf + 1) * 512], in_=w_out[l][:, hf * 512:(hf + 1) * 512].rearrange("(k p) n -> p k n", p=128)),
                           w=[f"wout_{hf}"], sem="ldw", after=FFN_NAMES)
            for nm_ in [f"win_{n}" for n in range(6)] + ["wout_0", "wout_1"]:
                S.res[nm_]['w'] = lastw
            D(lambda e: e.dma_start(out=gqk[:, 0:64], in_=qng[l].partition_broadcast(128)), w=["gqk"], sem="ld_gqk")
            D(lambda e: e.dma_start(out=gqk[:, 64:128], in_=kng[l].partition_broadcast(128)), w=["gqk"], sem="ld_gqk")
            D(lambda e: e.dma_start(out=esink[:], in_=sinks[l].partition_broadcast(128)), w=["esink"], sem="ld_esink")
            A(lambda e: e.activation(out=esink[:], in_=esink[:], func=AF.Exp), r=["esink"], w=["esink"])
            D(lambda e: e.dma_start(out=lngb[:, 0:256], in_=lng[l].partition_broadcast(128)), w=["lngb"], sem="ld_lngb")
            D(lambda e: e.dma_start(out=lngb[:, 256:512], in_=lnb[l].partition_broadcast(128)), w=["lngb"], sem="ld_lngb")
            D(lambda e: e.dma_start(out=cw[:], in_=cdwT[l]), w=["cw"], sem="ld_cw")
            D(lambda e: e.dma_start(out=sw[:], in_=sdwT[l]), w=["sw"], sem="ld_sw")

            V(lambda e: e.memset(Sst[:], 0.0), w=SSTN)
            for ti in range(16):
                emit_h(ti, l, 0)
                evac_hT(ti, hT[:], "hT")
                emit_z([0, 1] if ti < 15 else [0, 1, 2, 3, 4, 5])
                emit_local_ret(ti)
                emit_su()
                state_update(cst["cdecP"], "c_cdecP")
                if ti == 15:
                    emit_local_rest(ti, 0)
                    D(lambda e: e.dma_start(out=gin[l][G_S:G_S + 128, 0:128], in_=Sst[:].rearrange("p a b -> p (a b)")), r=SSTN, w=["gin"], sem="stg")
                    V(lambda e: e.memset(t256b[:, 0:128], 0.0), w=["_sb"])
                    D(lambda e: e.dma_start(out=gin[l][G_S:G_S + 128, 128:256], in_=t256b[:, 0:128]), r=["_sb"], w=["gin"], sem="stg")
                    D(lambda e: e.dma_start(out=gin[l][G_KV:G_KV + 128, 0:128], in_=knf[:]), r=["knf"], w=["gin"], sem="stg")
                    D(lambda e: e.dma_start(out=gin[l][G_KV:G_KV + 128, 128:256], in_=z[:, 1408:1536]), r=["z2"], w=["gin"], sem="stg")
                    D(lambda e: e.dma_start(out=gin[l][G_CF:G_CF + 30, :], in_=u[98:128, :]), r=["u"], w=["gin"], sem="stg")
                    D(lambda e: e.dma_start(out=gin[l][G_SC:G_SC + 2, :], in_=vsf[126:128, :]), r=["vsf"], w=["gin"], sem="stg")
                    D(lambda e: e.dma_start(out=o_kp[l], in_=knf[:]), r=["knf"], w=["o_kp"], sem="st_knf")
                    D(lambda e: e.dma_start(out=o_vp[l], in_=z[:, 1408:1536]), r=["z2"], w=["o_vp"], sem="st_z2")
                    D(lambda e: e.dma_start(out=o_confp[l], in_=u[98:128, :]), r=["u"], w=["o_confp"], sem="st_u")
                    D(lambda e: e.dma_start(out=o_scp[l], in_=vsf[126:128, :]), r=["vsf"], w=["o_scp"], sem="st_vsf")
            ck(3)
            S.op("pool", lambda e: e.collective_compute("AllGather", ALU.bypass, replica_groups=[[0, 1, 2, 3], [4, 5, 6, 7]], ins=[gin[l]], outs=[gout[l]]),
                 reads=["gin"], writes=["gout"], dma_sem="cc", inc=1)
            gv = gout[l].rearrange("(r p) n -> p r n", p=G_ROWS)
            D(lambda e: e.dma_start(out=gsel[:], in_=gv[G_S:G_S + 128, :, :]), r=["gout"], w=TF, sem="ld_tmpf")
            for pr in range(2):
                for r4 in range(4):
                    src = gsel[:, r4, pr * 64:(pr + 1) * 64]
                    if r4 == 0:
                        V(lambda e, pr=pr, src=src: e.tensor_scalar(out=Sst[:, pr, :], in0=src, scalar1=cst["coefS"][:, pr:pr + 1], scalar2=None, op0=ALU.mult),
                          r=TF + ["c_coefS"], w=[f"Sst{pr}0", f"Sst{pr}1"])
                    else:
                        V(lambda e, pr=pr, src=src, r4=r4: e.scalar_tensor_tensor(out=Sst[:, pr, :], in0=src, scalar=cst["coefS"][:, r4 * 2 + pr:r4 * 2 + pr + 1], in1=Sst[:, pr, :], op0=ALU.mult, op1=ALU.add),
                          r=TF + ["c_coefS", f"Sst{pr}0", f"Sst{pr}1"], w=[f"Sst{pr}0", f"Sst{pr}1"])
            A(lambda e: e.activation(out=Sbf[:], in_=Sst[:], func=AF.Copy), r=SSTN, w=["Sbf"])

            def select_rows(rows0, nrows, dst, dname):
                D(lambda e: e.dma_start(out=gsel[0:nrows, :, :], in_=gv[rows0:rows0 + nrows, :, :]), r=["gout"], w=TF, sem="ld_tmpf")
                for r4 in range(4):
                    if r4 == 0:
                        V(lambda e: e.tensor_scalar(out=dst, in0=gsel[0:nrows, 0, :], scalar1=cst["sel"][0:nrows, 0:1], scalar2=None, op0=ALU.mult), r=TF + ["c_sel"], w=[dname])
                    else:
                        V(lambda e, r4=r4: e.scalar_tensor_tensor(out=dst, in0=gsel[0:nrows, r4, :], scalar=cst["sel"][0:nrows, r4:r4 + 1], in1=dst, op0=ALU.mult, op1=ALU.add),
                          r=TF + ["c_sel", dname], w=[dname])
            hsel = ef32[:, 0:256]
            select_rows(G_KV, 128, hsel, "ee0")
            A(lambda e: e.activation(out=knb[:], in_=hsel[:, 0:128], func=AF.Copy), r=["ee0"], w=["knb"])
            A(lambda e: e.activation(out=vaug[1][:, :, 0:64], in_=hsel[:, 128:256].rearrange("p (a b) -> p a b", a=2), func=AF.Copy), r=["ee0"], w=["vaug1"])
            T(lambda e: e.transpose(B2[:, 0:128], knb[:], identb[:]), r=["knb", "identb"], w=["B2"])
            A(lambda e: e.activation(out=kTa[1][:], in_=B2[:, 0:128], func=AF.Copy), r=["B2"], w=["kTa1"])
            select_rows(G_CF, 30, hsel[0:30, :], "ee0")
            for c in range(2):
                T(lambda e, c=c: e.transpose(B3[:, c * 32:c * 32 + 30], hsel[0:30, c * 128:(c + 1) * 128], identf[0:30, 0:30]), r=["ee0", "identf"], w=B3N)
            A(lambda e: e.activation(out=extu[:, :, 0:30], in_=B3[:, 0:64].rearrange("p (a b) -> p a b", a=2)[:, :, 0:30], func=AF.Copy), r=B3N, w=["extu_halo"], after=["extu_new"])
            select_rows(G_SC, 2, hsel[0:2, :], "ee0")
            for c in range(2):
                T(lambda e, c=c: e.transpose(B3[:, 64 + c * 32:64 + c * 32 + 2], hsel[0:2, c * 128:(c + 1) * 128], identf[0:2, 0:2]), r=["ee0", "identf"], w=B3N)
            A(lambda e: e.activation(out=extv[:, :, 0:2], in_=B3[:, 64:128].rearrange("p (a b) -> p a b", a=2)[:, :, 0:2], func=AF.Copy), r=B3N, w=["extv_halo"], after=["extv_new"])

            ck(4)
            for ti in range(16):
                if ti == 1:
                    ck(5)
                cur = ti % 2
                prv = 1 - cur
                emit_h(ti, l, 0)
                evac_hT(ti, hT[:], "hT")
                emit_z([0, 1, 2, 3, 4, 5])
                emit_local_ret(ti)
                emit_local_rest(ti, cur)
                if ti == 0: ck(4.1)
                emit_transposes(ti, cur)
                if ti == 0: ck(4.2)
                for h in range(4):
                    rs = slice((h % 2) * 64, (h % 2) * 64 + 64)
                    bank, bn = (B7, "B7a") if h % 2 == 0 else (B3, "B3a")
                    T(lambda e, h=h, rs=rs, bank=bank: e.matmul(bank[:, (h // 2) * 128:(h // 2 + 1) * 128], lhsT=tr[rs, 2 + h // 2, :], rhs=tr[rs, h // 2, :], start=True, stop=True), r=["tr"], w=[bn])
                attm4 = attm[:].rearrange("p (pr hf i) -> p pr hf i", pr=2, hf=2)
                mask4 = cst["maskP"][:].rearrange("p (pr hf i) -> p pr hf i", pr=2, hf=2)
                for hf, (bank, bn) in enumerate([(B7, "B7a"), (B3, "B3a")]):
                    V(lambda e, hf=hf, bank=bank, attm4=attm4, mask4=mask4: e.tensor_tensor(out=attm4[:, :, hf, :], in0=bank[:, 0:256].rearrange("p (pr i) -> p pr i", pr=2), in1=mask4[:, :, hf, :], op=ALU.mult),
                      r=[bn, "c_maskP"], w=["attm"])
                if ti == 0: ck(4.25)
                for h in range(4):
                    T(lambda e, h=h: e.matmul(B6[:, h * 64:(h + 1) * 64], lhsT=attm[:, h * 128:(h + 1) * 128], rhs=rb[:, 3, h * 64:(h + 1) * 64], start=True, stop=True), r=["attm", "rb3"], w=["o"])
                for h in (0, 2, 1, 3):
                    rs = slice((h % 2) * 64, (h % 2) * 64 + 64)
                    bank, bn = (B7, "B7b") if h % 2 == 0 else (B3, "B3b")
                    T(lambda e, h=h, rs=rs, bank=bank: e.matmul(bank[:, 256 + (h // 2) * 64:256 + (h // 2 + 1) * 64], lhsT=qdT[rs, h // 2, :], rhs=Sbf[rs, h // 2, :], start=True, stop=True), r=["qdT", "Sbf"], w=[bn])
                emit_su()
                state_update(cst["cdecP"], "c_cdecP")
                osum_p = tmpf[:, 0:256]
                os4 = osum_p.rearrange("p (pr hf e) -> p pr hf e", pr=2, hf=2)
                for hf, (bank, bn) in enumerate([(B7, "B7b"), (B3, "B3b")]):
                    A(lambda e, hf=hf, bank=bank: e.activation(out=os4[:, :, hf, :], in_=bank[:, 256:384].rearrange("p (pr e) -> p pr e", pr=2), func=AF.Copy), r=[bn], w=["tmpf0"])
                V(lambda e: e.tensor_tensor(out=osum_p, in0=osum_p, in1=B6[:, 0:256], op=ALU.add), r=["tmpf0", "o"], w=["tmpf0"])
                emit_groupnorm_out(osum_p, ["tmpf0"])
                if ti == 0: ck(4.3)
                for kv, (bank, bn) in enumerate([(B4, "B4"), (B5, "B5")]):
                    rs = slice(kv * 64, kv * 64 + 64)
                    for kb, (kt, ktn) in enumerate([(kTa[prv], f"kTa{prv}"), (kTa[cur], f"kTa{cur}")]):
                        T(lambda e, bank=bank, kb=kb, rs=rs, kt=kt: e.matmul(bank[:, kb * 256:(kb + 1) * 256], lhsT=kt[rs, :], rhs=trf[rs, 512:768], start=True, stop=True),
                          r=[ktn, "tr"], w=[bn])
                    A(lambda e, bank=bank, kv=kv: e.activation(out=ee[:, kv * 512:(kv + 1) * 512], in_=bank[:], func=AF.Exp, scale=0.125), r=[bn], w=[f"ee{kv}"])
                    if ti == 0:
                        V(lambda e, kv=kv: e.scalar_tensor_tensor(out=pp[:, kv * 512:kv * 512 + 256], in0=ee[:, kv * 512:kv * 512 + 256], scalar=cst["hasprev"][:, 0:1], in1=cst["Ep"][:, kv * 512:kv * 512 + 256], op0=ALU.mult, op1=ALU.mult),
                          r=[f"ee{kv}", "c_Ep", "c_hasprev"], w=[f"pp{kv}"])
                        V(lambda e, kv=kv: e.tensor_tensor(out=pp[:, kv * 512 + 256:(kv + 1) * 512], in0=ee[:, kv * 512 + 256:(kv + 1) * 512], in1=cst["Ep"][:, kv * 512 + 256:(kv + 1) * 512], op=ALU.mult),
                          r=[f"ee{kv}", "c_Ep"], w=[f"pp{kv}"])
                    else:
                        V(lambda e, kv=kv: e.tensor_tensor(out=pp[:, kv * 512:(kv + 1) * 512], in0=ee[:, kv * 512:(kv + 1) * 512], in1=cst["Ep"][:, kv * 512:(kv + 1) * 512], op=ALU.mult),
                          r=[f"ee{kv}", "c_Ep"], w=[f"pp{kv}"])
                oa = B7[:, 0:320].rearrange("p (a b) -> p a b", a=4)
                for kv in range(2):
                    for g in range(2):
                        hh = kv * 2 + g
                        for kb, va, van in [(0, vaug[prv], f"vaug{prv}"), (1, vaug[cur], f"vaug{cur}")]:
                            c0 = kv * 512 + kb * 256 + g * 128
                            T(lambda e, kb=kb, kv=kv, hh=hh, va=va, c0=c0: e.matmul(B7[:, hh * 80:hh * 80 + 66], lhsT=pp[:, c0:c0 + 128], rhs=va[:, kv, 0:66],
                                                                                   start=(kb == 0), stop=(kb == 1)), r=[f"pp{kv}", van], w=B7N)
                emit_attn_finish(oa, B7N)
                if ti == 0: ck(4.4)
                emit_conv(ti)
                if ti == 0: ck(4.5)
                if os.environ.get("KDUMPCAT"):
                    A(lambda e: e.activation(out=tmpf[:], in_=hb[:], func=AF.Copy), r=["cat0", "cat1", "cat2", "cat3"], w=TF)
                    D(lambda e, ti=ti: e.dma_start(out=yp[ti * 128:(ti + 1) * 128, :], in_=tmpf[:]), r=TF, w=["yp"], sem="st_x")
                emit_out(ti)
                if ti == 15:
                    D(lambda e: e.dma_start(out=o_retp[l], in_=Sst[:]), r=SSTN, w=["o_retp"], sem="st_Sst")

            ck(6)
            ti = 16
            D(lambda e: e.dma_start(out=S0, in_=retS[l]), w=["S0"], sem="ld_S0", after=["kcTb", "vcb", "vcb1"])
            D(lambda e: e.dma_start(out=extu[:, :, 0:480], in_=confT[l]), w=["extu_halo"], sem="ld_extu", after=["extu_new"])
            D(lambda e: e.dma_start(out=extv[:, :, 0:32], in_=scT[l]), w=["extv_halo"], sem="ld_extv", after=["extv_new"])
            emit_h(ti, l, 0)
            evac_hT(ti, hT[:], "hT")
            emit_z([0, 1, 2, 3, 4, 5])
            A(lambda e: e.activation(out=S0b, in_=S0, func=AF.Copy), r=["S0"], w=EEPP)
            emit_local_ret(ti)
            emit_local_rest(ti, 0)
            emit_transposes(ti, 0)
            ck(6.1)
            for a_ in (range(2) if "trs" not in os.environ.get("KSKIP", "") else []):
                V(lambda e, a_=a_: e.tensor_tensor(out=trs[:, a_, :].rearrange("p (s t) -> p s t", s=16), in0=tr[:, a_, :].rearrange("p (t s) -> p s t", t=8),
                                                   in1=cst["qdecS"][:, a_ * 128:(a_ + 1) * 128].rearrange("p (s t) -> p s t", s=16), op=ALU.mult), r=["tr", "c_qdecS"], w=["trs"])
                V(lambda e, a_=a_: e.tensor_copy(out=trs[:, 2 + a_, :].rearrange("p (s t) -> p s t", s=16), in_=tr[:, 4 + a_, :].rearrange("p (t s) -> p s t", t=8)), r=["tr"], w=["trs"])
            for h in range(4):
                rs = slice((h % 2) * 64, (h % 2) * 64 + 64)
                bank, bn = (B7, "B7a") if h % 2 == 0 else (B3, "B3a")
                T(lambda e, h=h, rs=rs, bank=bank: e.matmul(bank[:, (h // 2) * 128:(h // 2 + 1) * 128], lhsT=tr[rs, 2 + h // 2, :], rhs=tr[rs, h // 2, :], start=True, stop=True), r=["tr"], w=[bn])
            attm4 = attm[:].rearrange("p (pr hf i) -> p pr hf i", pr=2, hf=2)
            mask4 = cst["maskS"][:].rearrange("p (pr hf i) -> p pr hf i", pr=2, hf=2)
            for hf, (bank, bn) in (enumerate([(B7, "B7a"), (B3, "B3a")]) if "attm" not in os.environ.get("KSKIP", "") else []):
                V(lambda e, hf=hf, bank=bank, attm4=attm4, mask4=mask4: e.tensor_tensor(out=attm4[:, :, hf, :], in0=bank[:, 0:256].rearrange("p (pr i) -> p pr i", pr=2), in1=mask4[:, :, hf, :], op=ALU.mult),
                  r=[bn, "c_maskS"], w=["attm"])
            for h in range(4):
                T(lambda e, h=h: e.matmul(B6[:, h * 64:(h + 1) * 64], lhsT=attm[:, h * 128:(h + 1) * 128], rhs=rb[:, 3, h * 64:(h + 1) * 64], start=True, stop=True), r=["attm", "rb3"], w=["o"])
            ck(6.11)
            for h in range(4):
                rs = slice((h % 2) * 64, (h % 2) * 64 + 64)
                bank, bn = (B4, "B4") if h % 2 == 0 else (B5, "B5")
                for s_ in range(16):
                    c0 = (h // 2) * 128 + s_ * 8
                    T(lambda e, h=h, s_=s_, rs=rs, bank=bank, c0=c0: e.matmul(bank[0:64, c0:c0 + 8], lhsT=S0b[rs, h // 2, s_, :], rhs=trs[rs, h // 2, s_ * 8:(s_ + 1) * 8], start=True, stop=True),
                      r=EEPP + ["trs"], w=[bn])
            ck(6.12)
            for h in range(4):
                bank, bn = (B4, "B4") if h % 2 == 0 else (B5, "B5")
                V(lambda e, h=h, bank=bank: e.tensor_copy(out=ocs[0:64, h, :].rearrange("p (t s) -> p t s", t=8), in_=bank[0:64, (h // 2) * 128:(h // 2 + 1) * 128].rearrange("p (s t) -> p t s", s=16)), r=[bn], w=["hT"])
            for h in range(4):
                T(lambda e, h=h: e.transpose(B3[:, 256 + h * 64:256 + (h + 1) * 64], ocs[0:64, h, :], identf[0:64, 0:64]), r=["hT", "identf"], w=["B3b"])
            osum = tmpf[:, 0:256]
            A(lambda e: e.activation(out=osum, in_=B3[:, 256:512], func=AF.Copy), r=["B3b"], w=["tmpf0"])
            V(lambda e: e.tensor_tensor(out=osum, in0=osum, in1=B6[:, 0:256], op=ALU.add), r=["tmpf0", "o"], w=["tmpf0"])
            ck(6.13)
            for h in range(4):
                rs = slice((h % 2) * 64, (h % 2) * 64 + 64)
                pr = h // 2
                V(lambda e, h=h: e.tensor_tensor(out=vbd, in0=rb[:, 3, h * 64:(h + 1) * 64][:, None, :].broadcast_to([128, 16, 64]),
                                                 in1=cst["onehot"][:][:, :, None].broadcast_to([128, 16, 64]), op=ALU.mult), r=["rb3", "c_onehot"], w=["acc0", "acc1", "accv0", "accv1"])
                for q2 in range(2):
                    bank = [B4, B5][q2]; bn = f"B{4 + q2}"
                    T(lambda e, q2=q2, bank=bank, pr=pr: e.matmul(bank[:], lhsT=rb[:, 2, pr * 128:(pr + 1) * 128], rhs=vbdf[:, q2 * 512:(q2 + 1) * 512], start=True, stop=True),
                      r=["rb2", "acc0", "acc1", "accv0", "accv1"], w=[bn])
                    V(lambda e, q2=q2, bank=bank, rs=rs, pr=pr: e.scalar_tensor_tensor(out=S0[rs, pr, q2 * 8:(q2 + 1) * 8, :].rearrange("p a b -> p (a b)"), in0=S0[rs, pr, q2 * 8:(q2 + 1) * 8, :].rearrange("p a b -> p (a b)"),
                                                                                       scalar=cst["cdecS"][rs, pr:pr + 1], in1=bank[rs, :], op0=ALU.mult, op1=ALU.add),
                      r=["S0", bn, "c_cdecS"], w=["S0"])
            ck(6.14)
            D(lambda e: e.dma_start(out=o_rets[l], in_=S0), r=["S0"], w=["o_rets"], sem="st_S0")
            ck(6.15)
            emit_groupnorm_out(osum, ["tmpf0"])
            ck(6.2)
            DG(lambda e: e.dma_start(out=kcTb, in_=kcT[l]), w=["kcTb"], sem="ldk", after=["S0"])
            G(lambda e: e.memset(samp[:, 2048:4160], 1.0), w=["vcb", "vcb1"], after=["S0"])
            lastk = DG(lambda e: e.dma_start(out=vcb[:, :, :, 0:64], in_=vc[l].rearrange("p s (k d) -> p s k d", k=2)), w=["vcb"], sem="ldk", after=["S0"])
            S.res["kcTb"]['w'] = lastk
            for kv, (bank, bn) in enumerate([(B4, "B4"), (B5, "B5")]):
                rs = slice(kv * 64, kv * 64 + 64)
                T(lambda e, bank=bank, rs=rs: e.matmul(bank[:, 0:256], lhsT=kTa[0][rs, :], rhs=trf[rs, 512:768], start=True, stop=True), r=["kTa0", "tr"], w=[bn])
                for s_ in range(16):
                    for g in range(2):
                        c0 = 256 + s_ * 16 + g * 8
                        T(lambda e, s_=s_, g=g, rs=rs, c0=c0, bank=bank: e.matmul(bank[:, c0:c0 + 8], lhsT=kcTb[rs, s_, :], rhs=trs[rs, 2 + g, s_ * 8:(s_ + 1) * 8], start=True, stop=True), r=["kcTb", "trs"], w=[bn])
                A(lambda e, bank=bank, kv=kv: e.activation(out=ee[:, kv * 512:(kv + 1) * 512], in_=bank[:], func=AF.Exp, scale=0.125), r=[bn], w=[f"ee{kv}"])
                V(lambda e, kv=kv: e.tensor_tensor(out=pp[:, kv * 512:(kv + 1) * 512], in0=ee[:, kv * 512:(kv + 1) * 512], in1=cst["Es"][:, kv * 512:(kv + 1) * 512], op=ALU.mult), r=[f"ee{kv}", "c_Es"], w=[f"pp{kv}"])
            for kv in range(2):
                for g in range(2):
                    hh = kv * 2 + g
                    c0 = kv * 512 + g * 128
                    T(lambda e, kv=kv, hh=hh, c0=c0: e.matmul(B7[:, hh * 80:hh * 80 + 66], lhsT=pp[:, c0:c0 + 128], rhs=vaug[0][:, kv, 0:66], start=True, stop=True),
                      r=[f"pp{kv}", "vaug0"], w=B7N)
            for s_ in range(16):
                for kv in range(2):
                    c0 = kv * 256 + s_ * 16
                    T(lambda e, s_=s_, kv=kv, c0=c0: e.matmul(B6[0:65, c0:c0 + 16], lhsT=vcb[:, s_, kv, 0:65], rhs=pp[:, kv * 512 + 256 + s_ * 16:kv * 512 + 256 + (s_ + 1) * 16], start=True, stop=True),
                      r=["vcb", "vcb1", f"pp{kv}"], w=["o", "SU"])
            for k_ in range(2):
                for g_ in range(2):
                    V(lambda e, k_=k_, g_=g_: e.tensor_copy(out=ocs[0:65, k_ * 2 + g_, :].rearrange("p (t s) -> p t s", t=8),
                                                             in_=B6[0:65, k_ * 256:(k_ + 1) * 256].rearrange("p (s g t) -> p g t s", s=16, g=2)[:, g_, :, :]), r=["o", "SU"], w=["hT"])
            for hh in range(4):
                T(lambda e, hh=hh: e.transpose(B3[:, 256 + hh * 64:256 + (hh + 1) * 64], ocs[0:64, hh, :], identf[0:64, 0:64]), r=["hT", "identf"], w=["B3b"])
            for hh in range(4):
                T(lambda e, hh=hh: e.transpose(B4[:, hh * 2:hh * 2 + 1], ocs[64:65, hh, :], identf[64:65, 64:65]), r=["hT", "identf"], w=["B4"])
            oas = tmpf[:, 512:832].rearrange("p (a b) -> p a b", a=4)
            A(lambda e: e.activation(out=oas, in_=B7[:, 0:320].rearrange("p (a b) -> p a b", a=4), func=AF.Copy), r=B7N, w=["tmpf1"])
            V(lambda e: e.tensor_tensor(out=oas[:, :, 0:64], in0=oas[:, :, 0:64], in1=B3[:, 256:512].rearrange("p (a b) -> p a b", a=4), op=ALU.add), r=["tmpf1", "B3b"], w=["tmpf1"])
            V(lambda e: e.tensor_tensor(out=oas[:, :, 64], in0=oas[:, :, 64], in1=B4[:, 0:8].rearrange("p (a b) -> p a b", a=4)[:, :, 0], op=ALU.add), r=["tmpf1", "B4"], w=["tmpf1"])
            emit_attn_finish(oas, ["tmpf1"])
            ck(6.3)
            emit_conv(ti)
            ck(6.4)
            D(lambda e: e.dma_start(out=o_ks[l][:, 0:120, :], in_=kc_o[l][:, 8:128, :]), w=["o_ks"], sem="st_d2d_k")
            D(lambda e: e.dma_start(out=o_vs[l][:, 0:120, :], in_=vc_o[l][:, 8:128, :]), w=["o_vs"], sem="st_d2d_v")
            D(lambda e: e.dma_start(out=o_confs[l][:, 0:22, :], in_=conf_o[l][:, 8:30, :]), w=["o_confs"], sem="st_d2d_c")
            for t in range(8):
                D(lambda e, t=t: e.dma_start(out=o_ks[l][:, 120 + t, :], in_=knf[t * 16:(t + 1) * 16, :]), r=["knf"], w=["o_ks"], sem="st_knf")
                D(lambda e, t=t: e.dma_start(out=o_vs[l][:, 120 + t, :], in_=z[t * 16:(t + 1) * 16, 1408:1536]), r=["z2"], w=["o_vs"], sem="st_z2")
                D(lambda e, t=t: e.dma_start(out=o_confs[l][:, 22 + t, :], in_=u[t * 16:(t + 1) * 16, :]), r=["u"], w=["o_confs"], sem="st_u")
                if t >= 6:
                    D(lambda e, t=t: e.dma_start(out=o_scs[l][:, t - 6, :], in_=vsf[t * 16:(t + 1) * 16, :]), r=["vsf"], w=["o_scs"], sem="st_vsf")
            D(lambda e: e.dma_start(out=gateS[:], in_=modS_d[l][0][:, 2048:3072]), r=["modS_d"], w=MIXF, sem="ld_mixf")
            emit_out(ti)

            ck(7)
            mod_phase(l, 1)
            D(lambda e: e.dma_start(out=gateS[:], in_=modS_d[l][1][:, 2048:3072]), r=["modS_d"], w=MIXF, sem="ld_mixf")
            blocks = [list(range(0, 8)), list(range(8, 17))]
            ubc = [0]
            for bi, tiles in enumerate(blocks):
                for i, ti in enumerate(tiles):
                    emit_h(ti, l, 1)
                    evac_hT(ti, h2T[:, :, i * 128:(i + 1) * 128], "h2T", after=MIX_NAMES)
                subs = [tiles[0:4], tiles[4:8]] + ([tiles[8:9]] if len(tiles) > 8 else [])
                for g8 in range(8):
                    b = g8 % 2
                    DG(lambda e, g8=g8, b=b: e.dma_start(out=W1g[b][:], in_=w_ff1[l][:, g8 * 512:(g8 + 1) * 512].rearrange("(k p) n -> p k n", p=128)),
                       w=[f"W1g{b}"], sem=f"ldf{b}", after=MIX_NAMES)
                    DG(lambda e, g8=g8, b=b: e.dma_start(out=W2g[b][:], in_=w_ff2[l][g8 * 512:(g8 + 1) * 512, :].rearrange("(k p) n -> p k n", p=128)),
                       w=[f"W2g{b}"], sem=f"ldg{b}", after=MIX_NAMES)
                    for si, sub in enumerate(subs):
                        n = len(sub) * 128
                        o0 = (sub[0] - tiles[0]) * 128
                        ub = ubc[0] % 2
                        ubc[0] += 1
                        for fc in range(4):
                            bank = [B0, B1, B3, B6][fc]
                            bn = [["B0"], ["B1"], B3N, ["o", "SU"]][fc]
                            tf = tmpf[:, (fc % 2) * 512:(fc % 2) * 512 + n]
                            tfn = f"tmpf{fc % 2}"
                            for k in range(8):
                                T(lambda e, k=k, fc=fc, bank=bank, b=b, o0=o0, n=n: e.matmul(bank[:, 0:n], lhsT=W1g[b][:, k, fc * 128:(fc + 1) * 128], rhs=h2T[:, k, o0:o0 + n], start=(k == 0), stop=(k == 7)),
                                  r=[f"W1g{b}", "h2T"], w=bn)
                            A(lambda e, bank=bank, n=n, tf=tf: e.activation(out=tf, in_=bank[:, 0:n], func=AF.Relu), r=bn, w=[tfn])
                            G(lambda e, fc=fc, ub=ub, n=n, tf=tf: e.tensor_tensor(out=uT[ub][:, fc, 0:n], in0=tf, in1=tf, op=ALU.mult), r=[tfn], w=[f"uT{ub}"], after=MIX_NAMES)
                        for i, ti in enumerate(sub):
                            gate = gateP if ti < 16 else gateS
                            gn = ["gateP"] if ti < 16 else MIXF
                            for hf in range(2):
                                bank = [B4, B5][hf]; bn = f"B{4 + hf}"
                                tn = ["ee0", "ee1"] if hf == 0 else ["pp0", "pp1"]
                                for fc in range(4):
                                    T(lambda e, fc=fc, bank=bank, hf=hf, i=i, ub=ub, b=b: e.matmul(bank[:], lhsT=uT[ub][:, fc, i * 128:(i + 1) * 128], rhs=W2g[b][:, fc, hf * 512:(hf + 1) * 512], start=(fc == 0), stop=(fc == 3)),
                                      r=[f"uT{ub}", f"W2g{b}"], w=[bn])
                                V(lambda e, bank=bank, hf=hf, gate=gate: e.tensor_tensor(out=t512[hf], in0=bank[:], in1=gate[:, hf * 512:(hf + 1) * 512], op=ALU.mult), r=[bn] + gn, w=tn)
                                G(lambda e, hf=hf, ti=ti: e.tensor_tensor(out=x[:, ti, hf * 512:(hf + 1) * 512], in0=x[:, ti, hf * 512:(hf + 1) * 512], in1=t512[hf], op=ALU.add),
                                  r=tn + [f"x{ti}"], w=[f"x{ti}"])
            ck(8)
            if l == DEPTH - 1:
                for ti in range(16):
                    D(lambda e, ti=ti: e.dma_start(out=yp[ti * 128:(ti + 1) * 128, :], in_=x[:, ti, :]), r=[f"x{ti}"], w=["yp"], sem="st_x")
                D(lambda e: e.dma_start(out=ys, in_=x[:, 16, :]), r=["x16"], w=["ys"], sem="st_x")
          for _l in range(DEPTH):
            emit_layer(_l)
        except _Stop:
            if os.environ.get("KDUMPX"):
                for ti in range(16):
                    D(lambda e, ti=ti: e.dma_start(out=yp[ti * 128:(ti + 1) * 128, :], in_=x[:, ti, :]), r=[f"x{ti}"], w=["yp"], sem="st_x")
            for _i in range(int(os.environ.get("KDUMMY", "0"))):
                A(lambda e: e.activation(out=st[:, 61:62], in_=st[:, 60:61], func=AF.Copy), r=["st60"], w=["st61"])
        S.wait_all("sp")
        S.emit(es)
    return nc


_NC = None


def _host_inputs(inp):
    f = lambda a: np.ascontiguousarray(np.asarray(a, dtype=np.float32))
    maps = []
    shared = {
        "ada_w": f(inp["ada_w"]), "ada_b": f(inp["ada_b"]).reshape(2, 1, 6144),
        "n1g": f(inp["norm1_g"]).reshape(2, 1, 1024), "n2g": f(inp["norm2_g"]).reshape(2, 1, 1024),
        "w_in": f(inp["w_in"]), "w_out": f(inp["w_out"]), "w_ff1": f(inp["w_ff1"]), "w_ff2": f(inp["w_ff2"]),
        "qng": f(inp["q_norm_g"]).reshape(2, 1, 64), "kng": f(inp["k_norm_g"]).reshape(2, 1, 64), "sinks": f(inp["attn_sinks"]).reshape(2, 1, 4),
        "cdwT": f(np.asarray(inp["conf_dw"]).reshape(2, 31, 2, 128).transpose(0, 3, 2, 1)),
        "sdwT": f(np.asarray(inp["sconv_dw"]).reshape(2, 3, 2, 128).transpose(0, 3, 2, 1)),
        "lng": f(inp["conf_ln_g"]).reshape(2, 1, 256), "lnb": f(inp["conf_ln_b"]).reshape(2, 1, 256),
    }
    xp = np.asarray(inp["x_prompt"]); xs = np.asarray(inp["x_sample"])
    for c in range(8):
        b, j = c // 4, c % 4
        sl = slice(16 * c, 16 * c + 16)
        m = dict(shared)
        m["xp"] = f(xp[b, j * 2048:(j + 1) * 2048])
        m["xs"] = f(xs[sl].transpose(1, 0, 2).reshape(128, 1024))
        m["cP"] = f(np.broadcast_to(np.asarray(inp["c_prompt"])[b][None, :], (128, 1024)))
        m["cS"] = f(np.broadcast_to(np.asarray(inp["c_sample"])[sl][None, :, :], (8, 16, 1024)).reshape(128, 1024))
        sr = np.asarray(inp["state_ret"])[:, sl]
        sr = sr.reshape(2, 16, 2, 2, 64, 64)
        m["retS"] = f(sr.transpose(0, 3, 4, 2, 1, 5).reshape(2, 128, 2, 16, 64))
        kc = np.asarray(inp["cache_swa_k"])[:, sl].reshape(2, 16, 128, 128)
        vcc = np.asarray(inp["cache_swa_v"])[:, sl].reshape(2, 16, 128, 128)
        m["kcT"] = f(kc.transpose(0, 3, 1, 2)); m["vc"] = f(vcc.transpose(0, 2, 1, 3))
        m["kc_o"] = f(kc); m["vc_o"] = f(vcc)
        cf = np.asarray(inp["state_conf"])[:, sl]
        m["confT"] = f(cf.reshape(2, 16, 30, 2, 128).transpose(0, 4, 3, 2, 1).reshape(2, 128, 2, 480))
        m["conf_o"] = f(cf)
        sc = np.asarray(inp["state_sconv"])[:, sl]
        m["scT"] = f(sc.reshape(2, 16, 2, 2, 128).transpose(0, 4, 3, 2, 1).reshape(2, 128, 2, 32))
        m["consts"] = _const_tables(c)
        maps.append(m)
    return maps


def kernel(**inp):
    global _NC
    if _NC is None:
        _NC = build()
    maps = _host_inputs(inp)
    res = run_bass_kernel_spmd(_NC, maps, core_ids=list(range(8)))
    R = res.results
    y_prompt = np.stack([np.concatenate([R[b * 4 + j]["yp"] for j in range(4)], 0) for b in range(2)])
    y_sample = np.concatenate([R[c]["ys"].reshape(8, 16, 1024).transpose(1, 0, 2) for c in range(8)], 0)

    def last(name):
        return [R[3][name], R[7][name]]
    ret_p = np.stack([r.reshape(2, 2, 64, 2, 64).transpose(0, 3, 1, 2, 4).reshape(2, 4, 64, 64) for r in last("o_retp")], 1)
    k_p = np.stack([r.reshape(2, 128, 2, 64) for r in last("o_kp")], 1)
    v_p = np.stack([r.reshape(2, 128, 2, 64) for r in last("o_vp")], 1)
    conf_p = np.stack(last("o_confp"), 1)
    sconv_p = np.stack(last("o_scp"), 1)
    ret_s = np.concatenate([R[c]["o_rets"].reshape(2, 2, 64, 2, 16, 64).transpose(0, 4, 3, 1, 2, 5).reshape(2, 16, 4, 64, 64) for c in range(8)], 1)
    k_s = np.concatenate([R[c]["o_ks"].reshape(2, 16, 128, 2, 64) for c in range(8)], 1)
    v_s = np.concatenate([R[c]["o_vs"].reshape(2, 16, 128, 2, 64) for c in range(8)], 1)
    conf_s = np.concatenate([R[c]["o_confs"] for c in range(8)], 1)
    sconv_s = np.concatenate([R[c]["o_scs"] for c in range(8)], 1)
    outs = (y_prompt, y_sample, ret_p, k_p, v_p, conf_p, sconv_p, ret_s, k_s, v_s, conf_s, sconv_s)
    return tuple(np.ascontiguousarray(o, dtype=np.float32) for o in outs)
```
